# Optimizing a Trainium2 kernel written in Bass

```python
import math
import jax, jax.numpy as jnp
from jax import lax
import numpy as np

D_MODEL = 1024
BATCH = 32
SEQ = 256
DEPTH = 1
DEC_BATCH = 4
DEC_SEQ = 1024
PAST_LEN = 256

GRID_W = 64
D_HYENA = 1024
N_HEADS = 16
HEAD_DIM = 64
D_ATTN = N_HEADS * HEAD_DIM
WIN_ROWS_MAX = 8
WIN_COLS = 16
FILTER_EMB = 33
FILTER_BANDS = (FILTER_EMB - 1) // 2
FILTER_HIDDEN = 64
DECAY_TARGET = 1e-2
FAST_DECAY_PCT = 0.3
SLOW_DECAY_PCT = 1.5
MIN_DECAY = math.log(DECAY_TARGET) / SLOW_DECAY_PCT
MAX_DECAY = math.log(DECAY_TARGET) / FAST_DECAY_PCT
DECAY_SHIFT = 0.05
LN_EPS = 1e-5
ALPHA = (2.0 * DEPTH) ** 0.25
BETA = (8.0 * DEPTH) ** -0.25
D_IN = 4 * D_HYENA + 4 * D_ATTN + 2 * D_MODEL
NEG_INF = -1e30

kernel_name = "hyena_natten_gated_deepnorm_step"


def layer_norm(x, g, b):
    xf = x.astype(jnp.float32)
    mu = xf.mean(-1, keepdims=True)
    var = jnp.square(xf - mu).mean(-1, keepdims=True)
    return ((xf - mu) * lax.rsqrt(var + LN_EPS)).astype(x.dtype) * g + b


def modulation(cond, w_ada, b_ada):
    mod = jax.nn.silu(cond) @ w_ada + b_ada
    return jnp.split(mod, 3, axis=-1)


def short_conv(u, w, b):
    L = u.shape[1]
    up = jnp.pad(u, ((0, 0), (1, 1), (0, 0)))
    return up[:, :L] * w[0] + up[:, 1:L + 1] * w[1] + up[:, 2:L + 2] * w[2] + b


def split_projection(h, w_in, conv_w, conv_b):
    B, L, _ = h.shape
    z = h @ w_in
    hy, g_h, qkv, g_a, m = jnp.split(
        z, [3 * D_HYENA, 4 * D_HYENA, 4 * D_HYENA + 3 * D_ATTN, 4 * D_HYENA + 4 * D_ATTN], axis=-1)
    hy = short_conv(hy, conv_w, conv_b)
    v_h, x1, x0 = jnp.split(hy, 3, axis=-1)
    q, k, v = [t.reshape(B, L, N_HEADS, HEAD_DIM) for t in jnp.split(qkv, 3, axis=-1)]
    return (v_h, x1, x0, g_h), (q, k, v, g_a), m


def hyena_filters(L, f_w1, f_b1, f_w2, f_b2, f_w3, f_freq):
    t = jnp.linspace(0.0, 1.0, L, dtype=jnp.float32)[:, None]
    bands = jnp.linspace(1e-4, FILTER_BANDS - 1, FILTER_BANDS, dtype=jnp.float32)[None]
    w = 2.0 * math.pi * jnp.arange(L, dtype=jnp.float32)[:, None] / L
    z = jnp.concatenate([t, jnp.cos(bands * w), -jnp.sin(bands * w)], axis=-1)
    hf = jnp.sin(f_freq[0] * (z @ f_w1 + f_b1))
    hf = jnp.sin(f_freq[1] * (hf @ f_w2 + f_b2))
    hf = (hf @ f_w3).astype(jnp.float32).reshape(L, 2, D_HYENA)
    deltas = jnp.linspace(MIN_DECAY, MAX_DECAY, D_HYENA, dtype=jnp.float32)
    decay = jnp.exp(-t * jnp.abs(deltas)) + DECAY_SHIFT
    hf = hf * decay[:, None, :]
    return hf[:, 0], hf[:, 1]


def hyena_long_conv(v_h, x1, x0, h_fwd, h_bwd, d_bias):
    L = v_h.shape[1]
    u = v_h * x1
    filt_full = jnp.concatenate([h_fwd, jnp.zeros_like(h_fwd[:1]), h_bwd[:0:-1]], axis=0)
    n = 2 * L
    uf = jnp.fft.rfft(u.astype(jnp.float32), n=n, axis=1)
    ff = jnp.fft.rfft(filt_full, n=n, axis=0)
    y = jnp.fft.irfft(uf * ff[None], n=n, axis=1)[:, :L].astype(u.dtype)
    return (y + u * d_bias) * x0


def context_attention(q, k, v):
    B, L, H, Dh = q.shape
    s = jnp.einsum("bqhd,bkhd->bhqk", q, k).astype(jnp.float32) * (HEAD_DIM ** -0.5)
    p = jax.nn.softmax(s, axis=-1).astype(v.dtype)
    return jnp.einsum("bhqk,bkhd->bqhd", p, v).reshape(B, L, H * Dh)


def neighbourhood_attention(q, k, v, ctx_k, ctx_v, rpb):
    B, L, H, Dh = q.shape
    rows = L // GRID_W
    kh = min(WIN_ROWS_MAX, rows)
    kw = WIN_COLS
    r = jnp.arange(rows)
    col = jnp.arange(GRID_W)
    r0 = jnp.clip(r - kh // 2, 0, rows - kh)
    c0 = jnp.clip(col - kw // 2, 0, GRID_W - kw)
    slab_rows = r0[:, None] + jnp.arange(kh)[None]
    n_slab = kh * GRID_W
    kg = k.reshape(B, rows, GRID_W, H, Dh)[:, slab_rows].reshape(B, rows, n_slab, H, Dh)
    vg = v.reshape(B, rows, GRID_W, H, Dh)[:, slab_rows].reshape(B, rows, n_slab, H, Dh)
    qg = q.reshape(B, rows, GRID_W, H, Dh)
    scale = HEAD_DIM ** -0.5
    s_win = jnp.einsum("brqhd,brkhd->bhrqk", qg, kg).astype(jnp.float32) * scale
    key_row = jnp.broadcast_to(slab_rows[:, :, None], (rows, kh, GRID_W)).reshape(rows, n_slab)
    key_col = jnp.tile(col, kh)
    dr = key_row[:, None, :] - r[:, None, None]
    dc = key_col[None, None, :] - col[None, :, None]
    in_win = (key_col[None, :] >= c0[:, None]) & (key_col[None, :] < c0[:, None] + kw)
    bias = rpb[:, dr + WIN_ROWS_MAX - 1, jnp.clip(dc + WIN_COLS - 1, 0, 2 * WIN_COLS - 2)]
    s_win = jnp.where(in_win[None, None, None], s_win + bias[None].astype(jnp.float32), NEG_INF)
    s_ctx = jnp.einsum("brqhd,bkhd->bhrqk", qg, ctx_k).astype(jnp.float32) * scale
    p = jax.nn.softmax(jnp.concatenate([s_win, s_ctx], axis=-1), axis=-1).astype(v.dtype)
    o = (jnp.einsum("bhrqk,brkhd->brqhd", p[..., :n_slab], vg)
         + jnp.einsum("bhrqk,bkhd->brqhd", p[..., n_slab:], ctx_v))
    return o.reshape(B, L, H * Dh)


def merge_branches(y_h, g_h, y_a, g_a, m, w_bh, w_ba, w_out):
    p_h = (y_h * jax.nn.silu(g_h)) @ w_bh
    p_a = (y_a * jax.nn.silu(g_a)) @ w_ba
    m_h, m_a = jnp.split(jax.nn.sigmoid(m), 2, axis=-1)
    return (m_h * p_h + m_a * p_a) @ w_out


def layer_forward(x, cond, ctx_kv, p):
    shift, scale, gate = modulation(cond, p["w_ada"], p["b_ada"])
    h = x * (1 + scale) + shift
    (v_h, x1, x0, g_h), (q, k, v, g_a), m = split_projection(h, p["w_in"], p["conv_w"], p["conv_b"])
    h_fwd, h_bwd = hyena_filters(x.shape[1], p["filt_w1"], p["filt_b1"], p["filt_w2"],
                                 p["filt_b2"], p["filt_w3"], p["filt_freq"])
    y_h = hyena_long_conv(v_h, x1, x0, h_fwd, h_bwd, p["hyena_d"])
    if ctx_kv is None:
        y_a = context_attention(q, k, v)
        kv_out = (k, v)
    else:
        y_a = neighbourhood_attention(q, k, v, ctx_kv[0], ctx_kv[1], p["rpb"])
        kv_out = None
    out = merge_branches(y_h, g_h, y_a.reshape(g_a.shape), g_a, m, p["w_bh"], p["w_ba"], p["w_out"])
    x_new = layer_norm(ALPHA * x + gate * out, p["ln_g"], p["ln_b"])
    return x_new, kv_out


def setup_inputs(seed: int = 0) -> dict:
    key = jax.random.key(seed)
    ks = jax.random.split(key, 24)

    def nrm(k, shape, s):
        return jax.random.normal(k, shape, jnp.float32) * s

    return {
        "x_prompt": nrm(ks[0], (BATCH, SEQ, D_MODEL), 1.0),
        "x_sample": nrm(ks[1], (DEC_BATCH, DEC_SEQ, D_MODEL), 1.0),
        "c": nrm(ks[2], (DEC_BATCH, D_MODEL), 1.0),
        "cache_k": nrm(ks[3], (DEC_BATCH, DEPTH, PAST_LEN, N_HEADS, HEAD_DIM), 1.0),
        "cache_v": nrm(ks[4], (DEC_BATCH, DEPTH, PAST_LEN, N_HEADS, HEAD_DIM), 1.0),
        "c_ctx": nrm(ks[5], (D_MODEL,), 1.0),
        "w_ada": nrm(ks[6], (DEPTH, D_MODEL, 3 * D_MODEL), 0.5 * D_MODEL ** -0.5),
        "b_ada": nrm(ks[7], (DEPTH, 3 * D_MODEL), 0.02),
        "w_in": nrm(ks[8], (DEPTH, D_MODEL, D_IN), D_MODEL ** -0.5),
        "conv_w": nrm(ks[9], (DEPTH, 3, 3 * D_HYENA), 0.5),
        "conv_b": nrm(ks[10], (DEPTH, 3 * D_HYENA), 0.02),
        "filt_w1": nrm(ks[11], (DEPTH, FILTER_EMB, FILTER_HIDDEN), FILTER_EMB ** -0.5),
        "filt_b1": nrm(ks[12], (DEPTH, FILTER_HIDDEN), 0.1),
        "filt_w2": nrm(ks[13], (DEPTH, FILTER_HIDDEN, FILTER_HIDDEN), FILTER_HIDDEN ** -0.5),
        "filt_b2": nrm(ks[14], (DEPTH, FILTER_HIDDEN), 0.1),
        "filt_w3": nrm(ks[15], (DEPTH, FILTER_HIDDEN, 2 * D_HYENA), 0.1 * FILTER_HIDDEN ** -0.5),
        "filt_freq": 1.0 + nrm(ks[16], (DEPTH, 2, FILTER_HIDDEN), 0.1),
        "hyena_d": nrm(ks[17], (DEPTH, D_HYENA), 0.5),
        "rpb": nrm(ks[18], (DEPTH, N_HEADS, 2 * WIN_ROWS_MAX - 1, 2 * WIN_COLS - 1), 0.1),
        "w_bh": nrm(ks[19], (DEPTH, D_HYENA, D_MODEL), BETA * D_HYENA ** -0.5),
        "w_ba": nrm(ks[20], (DEPTH, D_ATTN, D_MODEL), BETA * D_ATTN ** -0.5),
        "w_out": nrm(ks[21], (DEPTH, D_MODEL, D_MODEL), BETA * D_MODEL ** -0.5),
        "ln_g": 1.0 + nrm(ks[22], (DEPTH, D_MODEL), 0.01),
        "ln_b": nrm(ks[23], (DEPTH, D_MODEL), 0.01),
    }


def reference(x_prompt, x_sample, c, cache_k, cache_v, c_ctx, w_ada, b_ada, w_in, conv_w, conv_b,
              filt_w1, filt_b1, filt_w2, filt_b2, filt_w3, filt_freq, hyena_d, rpb,
              w_bh, w_ba, w_out, ln_g, ln_b):
    xp = x_prompt
    xs = x_sample
    cond_lat = c[:, None, :]
    new_k = []
    new_v = []
    for l in range(DEPTH):
        p = {
            "w_ada": w_ada[l], "b_ada": b_ada[l], "w_in": w_in[l],
            "conv_w": conv_w[l], "conv_b": conv_b[l],
            "filt_w1": filt_w1[l], "filt_b1": filt_b1[l], "filt_w2": filt_w2[l],
            "filt_b2": filt_b2[l], "filt_w3": filt_w3[l], "filt_freq": filt_freq[l],
            "hyena_d": hyena_d[l], "rpb": rpb[l],
            "w_bh": w_bh[l], "w_ba": w_ba[l], "w_out": w_out[l],
            "ln_g": ln_g[l], "ln_b": ln_b[l],
        }
        xp, (k_ctx, v_ctx) = layer_forward(xp, c_ctx, None, p)
        new_k.append(k_ctx)
        new_v.append(v_ctx)
        xs, _ = layer_forward(xs, cond_lat, (cache_k[:, l], cache_v[:, l]), p)
    return (xp, xs, jnp.stack(new_k, axis=1), jnp.stack(new_v, axis=1))
```

```python
import math
from contextlib import ExitStack

import numpy as np
import concourse.bass as bass
import concourse.mybir as mybir
from concourse.bass_utils import run_bass_kernel_spmd

F32 = mybir.dt.float32
BF16 = mybir.dt.bfloat16
AF = mybir.ActivationFunctionType
ALU = mybir.AluOpType

D = 1024
NCORES = 8
PR0, OWN0, OTH0, HALO0, NTOK = 0, 1024, 1536, 2048, 2304
NOWN = 1536
ALPHA = 2.0 ** 0.25
LN_EPS = 1e-5
TWO_PI = 2.0 * math.pi

P_COND, P_BADA, P_CW, P_CB, P_HD, P_FLAG, P_FPAR, P_NP = 0, 16, 40, 112, 136, 144, 146, 150

class StopBuild(Exception):
    pass


HY_STOP = 0
AT_STOP = 0
SAME_ENGINE_NOSYNC = ("pe",)
AT_NHP = 8
AT_VAR = 0
DEBUG = {}
STOP_AFTER = None


class Tok:
    __slots__ = ("w", "r", "x")

    def __init__(self):
        self.w = {}
        self.r = {}
        self.x = {}


class Bld:
    def __init__(self, nc, es, n_dma=32):
        self.nc = nc
        self.es = es
        self.engs = {"pe": nc.tensor, "act": nc.scalar, "dve": nc.vector, "pool": nc.gpsimd, "sp": nc.sync}
        self.sems = {}
        self.cnt = {}
        for e in ("pe", "act", "dve", "pool"):
            self.sems[e] = es.enter_context(nc.semaphore("s_" + e))
            self.cnt[e] = 0
        self.dcnt = []
        for i in range(n_dma):
            self.sems[("d", i)] = es.enter_context(nc.semaphore("s_d%d" % i))
            self.dcnt.append(0)
        self.drr = 0
        self.drr_sw = 0
        self.waited = {e: {} for e in self.engs}
        self.out_events = {}
        self.ps = []
        for i in range(8):
            t = es.enter_context(nc.psum_tensor("psb%d" % i, [128, 512], F32))
            self.ps.append((t, Tok()))
        self.psi = 0
        self.dbg = {}

    def psum(self):
        t = self.ps[self.psi]
        self.psi = (self.psi + 1) % 8
        return t

    def _deps(self, reads, writes, pw=()):
        deps = {}

        def add(d):
            for sk, v in d.items():
                if deps.get(sk, 0) < v:
                    deps[sk] = v
        for t in reads:
            add(t.w)
        for t in writes:
            add(t.w)
            add(t.r)
        for t in pw:
            add(t.r)
            add(t.x)
        return deps

    def _wait(self, eng, deps):
        for sk, v in deps.items():
            if sk == eng and eng in SAME_ENGINE_NOSYNC:
                continue
            if self.waited[eng].get(sk, 0) >= v:
                continue
            self.engs[eng].wait_ge(self.sems[sk], v)
            self.waited[eng][sk] = v

    def _record(self, sk, v, reads, writes, pw=()):
        for t in writes:
            t.w = {sk: v}
            t.x = {sk: v}
            t.r = {}
        for t in pw:
            if t.w.get(sk, 0) < v:
                t.w[sk] = v
        for t in reads:
            if t.r.get(sk, 0) < v:
                t.r[sk] = v

    def op(self, eng, fn, reads=(), writes=(), pw=()):
        self._wait(eng, self._deps(reads, writes, pw))
        ins = fn(self.engs[eng])
        self.cnt[eng] += 1
        ins.then_inc(self.sems[eng], 1)
        self._record(eng, self.cnt[eng], reads, writes, pw)

    def dma(self, q, out, in_, reads=(), writes=(), pw=(), is_output=False, **kw):
        self._wait(q, self._deps(reads, writes, pw))
        half = len(self.dcnt) // 2
        if q == "pool":
            i = half + self.drr_sw
            self.drr_sw = (self.drr_sw + 1) % half
        else:
            i = self.drr
            self.drr = (self.drr + 1) % half
        if self.dcnt[i] > 0:
            self._wait(q, {("d", i): self.dcnt[i]})
        self.dcnt[i] += 16
        self.engs[q].dma_start(out=out, in_=in_, **kw).then_inc(self.sems[("d", i)], 16)
        self._record(("d", i), self.dcnt[i], reads, writes, pw)
        if is_output:
            self.out_events[("d", i)] = self.dcnt[i]

    def barrier(self, dma=True):
        deps = {e: c for e, c in self.cnt.items() if c > 0}
        if dma:
            for i, c in enumerate(self.dcnt):
                if c > 0:
                    deps[("d", i)] = c
        for e in self.engs:
            self._wait(e, deps)

    def finish(self):
        self._wait("sp", dict(self.out_events))
        self.barrier()

    def init_wrings(self):
        self.wsm = [self.sb("wsm%d" % i, [128, 8, 128], BF16) for i in range(8)]
        self.wsi = 0
        self.wbi = 0

    def alloc_big(self, n=2):
        self.wbi += 1000
        self.wbg = [self.sb("wbg%d_%d" % (self.wbi, i), [128, 8, 512], BF16) for i in range(n)]

    def load_w(self, dram_w, col0, ncols):
        if ncols <= 128:
            t, tk = self.wsm[self.wsi % len(self.wsm)]
            self.wsi += 1
        else:
            t, tk = self.wbg[self.wbi % len(self.wbg)]
            self.wbi += 1
        self.dma("pool", t[:, :, 0:ncols], dram_w[:, col0:col0 + ncols].rearrange("(c p) n -> p c n", p=128), writes=[tk])
        return t, tk

    def sb(self, name, shape, dt, side=None):
        if side is None:
            t = self.es.enter_context(self.nc.sbuf_tensor("sb_" + name, shape, dt))
        else:
            t = self.es.enter_context(self.nc.sbuf_tensor("sb_" + name, shape, dt, side=side))
        return t, Tok()

    def dump(self, name, ap, tok, shape, dt=F32):
        if name not in DEBUG:
            return
        d = self.nc.dram_tensor("dbg_" + name, list(shape), dt, kind="ExternalOutput").ap()
        self.dma("sp", d, ap, reads=[tok], is_output=True)
        self.dbg[name] = d


def _fwd_tab(L, n):
    s = np.arange(L, dtype=np.float64)[:, None]
    f = (np.arange(n // 2, dtype=np.float64) + 0.5)[None, :]
    ang = 2.0 * np.pi * s * f / n
    return np.concatenate([np.cos(ang), -np.sin(ang)], 1).astype(np.float32)


def _bwd_tab(L, n):
    s = np.arange(L, dtype=np.float64)[:, None]
    f = (np.arange(n // 2, dtype=np.float64) + 0.5)[None, :]
    ang = 2.0 * np.pi * s * f / n
    t = np.concatenate([np.cos(ang), np.sin(ang)], 1)
    t[0] = 0.0
    return t.astype(np.float32)


def _inv_tab(n, Lout):
    t = np.arange(Lout, dtype=np.float64)[None, :]
    f = (np.arange(n // 2, dtype=np.float64) + 0.5)[:, None]
    ang = 2.0 * np.pi * f * t / n
    return ((2.0 / n) * np.concatenate([np.cos(ang), -np.sin(ang)], 0)).astype(np.float32)


def _x_tab(half):
    d = (np.arange(1024, dtype=np.float64) - 512.0)[:, None]
    f = (np.arange(512, dtype=np.float64) + 0.5)[None, :]
    ang = 2.0 * np.pi * d * f / 1024.0
    sign = -1.0 if half == 1 else 1.0
    t = np.concatenate([np.cos(ang), sign * np.sin(ang)], 1)
    t[0] = 0.0
    return t.astype(np.float32)


def _z_tab(L):
    t = np.linspace(0.0, 1.0, L, dtype=np.float32)[:, None]
    bands = np.linspace(1e-4, 15.0, 16, dtype=np.float32)[None]
    w = (2.0 * np.float32(math.pi) * np.arange(L, dtype=np.float32)[:, None] / np.float32(L)).astype(np.float32)
    z = np.concatenate([t, np.cos(bands * w), -np.sin(bands * w)], axis=-1).astype(np.float32)
    return np.ascontiguousarray(z.T)


def _decay_consts():
    mn = math.log(1e-2) / 1.5
    mx = math.log(1e-2) / 0.3
    deltas = np.linspace(mn, mx, 1024, dtype=np.float32)
    absd = np.abs(deltas).astype(np.float32)[None, :]
    negt = np.zeros((128, 10), np.float32)
    t256 = np.linspace(0.0, 1.0, 256, dtype=np.float32)
    t1k = np.linspace(0.0, 1.0, 1024, dtype=np.float32)
    for tc in range(2):
        negt[:, tc] = -t256[tc * 128:(tc + 1) * 128]
    for tc in range(8):
        negt[:, 2 + tc] = -t1k[tc * 128:(tc + 1) * 128]
    return absd, negt


_CONST_CACHE = {}


def _consts(half):
    if half in _CONST_CACHE:
        return _CONST_CACHE[half]
    c = {
        "fwd256": _fwd_tab(256, 512), "bwd256": _bwd_tab(256, 512), "inv256": _inv_tab(512, 256),
        "fwd512": _fwd_tab(512, 1024), "bwd512": _bwd_tab(512, 1024), "inv512": _inv_tab(1024, 512),
        "x1024": _x_tab(half),
        "zt256": _z_tab(256), "zt1024": _z_tab(1024),
        "absd": _decay_consts()[0], "negt": _decay_consts()[1],
        "ident": np.eye(128, dtype=np.float32),
    }
    import ml_dtypes
    for k in ("fwd256", "bwd256", "inv256", "fwd512", "bwd512", "inv512", "x1024"):
        c[k] = c[k].astype(ml_dtypes.bfloat16)
    _CONST_CACHE[half] = c
    return c


def build_program():
    nc = bass.Bass("TRN2", target_bir_lowering=False)

    BF_TABS = ("fwd256", "bwd256", "inv256", "fwd512", "bwd512", "inv512", "x1024")

    def din(name, shape):
        return nc.dram_tensor(name, list(shape), BF16 if name in BF_TABS else F32, kind="ExternalInput").ap()

    def dout(name, shape):
        return nc.dram_tensor(name, list(shape), F32, kind="ExternalOutput").ap()

    I = {}
    for name, shape in [
        ("xall", (NTOK, D)), ("params", (128, P_NP)), ("ck", (256, D)), ("cv", (256, D)),
        ("w_ada", (D, 3 * D)), ("w_in", (D, 10 * D)), ("w_bh", (D, D)), ("w_ba", (D, D)), ("w_out", (D, D)),
        ("fw1", (33, 64)), ("fw2", (64, 64)), ("w3c", (64, 3 * D)),
        ("rows", (3, D)),
        ("fwd256", (256, 512)), ("bwd256", (256, 512)), ("inv256", (512, 256)),
        ("fwd512", (512, 1024)), ("bwd512", (512, 1024)), ("inv512", (1024, 512)), ("x1024", (1024, 1024)),
        ("zt256", (33, 256)), ("zt1024", (33, 1024)), ("absd", (1, D)), ("negt", (128, 10)),
        ("ident", (128, 128)), ("btab", (16, 128, 24 * 64)), ("cmask", (128, 64)), ("rmask", (128, 48)),
    ]:
        I[name] = din(name, shape)
    O = {"y": dout("y", (NOWN, D)), "nk": dout("nk", (1024, D)), "nv": dout("nv", (1024, D))}

    with ExitStack() as es:
        b = Bld(nc, es)
        par, par_t = b.sb("par", [128, P_NP], F32)
        ident, ident_t = b.sb("ident", [128, 128], F32)
        identb, identb_t = b.sb("identb", [128, 128], BF16)
        b.dma("sp", par[:], I["params"][:, :], writes=[par_t])
        b.dma("sp", ident[:], I["ident"][:, :], writes=[ident_t])
        b.op("dve", lambda e: e.tensor_copy(out=identb[:], in_=ident[:]), reads=[ident_t], writes=[identb_t])

        esH = ExitStack()
        b.es = esH
        h256, h256_t = b.sb("h256", [128, 4, D], BF16, side="right")
        hoo, hoo_t = b.sb("hoo", [128, 8, D], BF16, side="right")
        hox, hox_t = b.sb("hox", [128, 8, D], BF16, side="right")
        dft = {}
        for name, rows, cols in (("fwd256", 256, 512), ("fwd512", 512, 1024), ("inv256", 512, 256), ("inv512", 1024, 512)):
            t, tk = b.sb("t_" + name, [128, rows // 128, cols], BF16, side="right")
            dft[name] = (t, tk)
        b.es = es

        b.init_wrings()
        esA = ExitStack()
        b.es = esA
        b.wbg = [b.sb("wbgA%d" % i, [128, 8, 512], BF16, side="right") for i in range(4)]
        b.es = es
        ada_pre = []
        phase_filters(b, I, par, par_t, h256, h256_t, hoo, hoo_t, hox, hox_t, dft, ada_pre)
        b.dump("h256", h256[:], h256_t, [128, 4, D], BF16)
        b.dump("hoo", hoo[:], hoo_t, [128, 8, D], BF16)
        b.dump("hox", hox[:], hox_t, [128, 8, D], BF16)
        if STOP_AFTER == "filters":
            b.finish()
            esH.close()
            return nc, b

        modsb, modsb_t = b.sb("modsb", [128, 24, 2], F32)
        hT, _ = b.sb("hT", [128, 8, NTOK], BF16)
        hT_t = Tok()
        yg, yg_t = b.sb("yg", [128, 8, NOWN], BF16)
        scb, scb_t = b.sb("scb", [128, 16], BF16)
        esX = ExitStack()
        b.es = esX
        xts = [b.sb("xt%d" % i, [128, D], F32) for i in range(8)]
        b.es = es
        for i in range(8):
            b.dma("sp", xts[i][0][:], I["xall"][i * 128:(i + 1) * 128, :], writes=[xts[i][1]])
        phase_mod(b, I, par, par_t, modsb, modsb_t, ada_pre, scb, scb_t)
        b.dump("mod", modsb[:], modsb_t, [128, 24, 2])
        pre_w = {(cc, sec): b.load_w(I["w_in"], sec * D + cc * 128, 128) for cc in range(2) for sec in range(2)}
        shared = {}
        phase_ht(b, I, ident, ident_t, modsb, modsb_t, hT, hT_t, xts, 8)
        esX.close()
        esA.close()
        b.dump("hT", hT[:], hT_t, [128, 8, NTOK], BF16)
        if STOP_AFTER == "ht":
            b.finish()
            esH.close()
            return nc, b
        try:
            phase_hyena(b, I, par, par_t, identb, identb_t, hT, hT_t, yg, yg_t, h256, h256_t, hoo, hoo_t, hox, hox_t, dft, pre_w, shared)
        except StopBuild:
            b.finish()
            esH.close()
            return nc, b
        b.dump("yg", yg[:], yg_t, [128, 8, NOWN], BF16)
        if STOP_AFTER == "hyena":
            b.finish()
            esH.close()
            return nc, b
        b.barrier()
        esH.close()
        ya, ya_t = b.sb("ya", [128, 8, NOWN], BF16)
        phase_attn(b, I, O, par, par_t, ident, ident_t, hT, hT_t, ya, ya_t, shared)
        b.dump("ya", ya[:], ya_t, [128, 8, NOWN], BF16)
        if STOP_AFTER == "attn":
            b.finish()
            return nc, b
        mg, mg_t = b.sb("mg", [128, 8, NOWN], BF16)
        b.wbi += 1000
        b.wbg = [b.sb("wbgF%d" % i, [128, 8, 512], BF16) for i in range(4)]
        pre = {"wa": [b.load_w(I["w_ada"], 2 * D + cb * 512, 512) for cb in range(2)],
               "wo": [b.load_w(I["w_out"], cb * 512, 512) for cb in range(2)]}
        pre["rows"] = b.sb("rows", [128, 2, D], F32)
        pre["brow"] = b.sb("brow", [128, D], F32)
        for r in range(2):
            b.dma("sp", pre["rows"][0][:, r, :], I["rows"][r + 1:r + 2, :].to_broadcast([128, D]), pw=[pre["rows"][1]])
        b.dma("sp", pre["brow"][0][:], I["rows"][0:1, :].to_broadcast([128, D]), writes=[pre["brow"][1]])
        phase_merge(b, I, hT, hT_t, yg, yg_t, ya, ya_t, mg, mg_t, shared)
        b.dump("mg", mg[:], mg_t, [128, 8, NOWN], BF16)
        phase_final(b, I, O, par, par_t, mg, mg_t, pre)

        b.finish()
    return nc, b


def phase_mod(b, I, par, par_t, modsb, modsb_t, ada_pre, scb, scb_t):
    with ExitStack() as es:
        old_es = b.es
        b.es = es
        b.op("act", lambda e: e.activation(out=scb[:], in_=par[:, P_COND:P_COND + 16], func=AF.Silu),
             reads=[par_t], writes=[scb_t])
        ps, ps_t = b.psum()
        for jb in range(4):
            w, w_t = ada_pre[jb]
            for c4 in range(4):
                ch = jb * 4 + c4
                for kc in range(8):
                    b.op("pe", lambda e: e.matmul(ps[:, 2 * ch:2 * ch + 2], lhsT=w[:, kc, c4 * 128:(c4 + 1) * 128],
                                                   rhs=scb[:, 2 * kc:2 * kc + 2], start=(kc == 0), stop=(kc == 7)),
                         reads=[w_t, scb_t], writes=[ps_t])
        b.op("dve", lambda e: e.tensor_tensor(
            out=modsb[:, 0:16, :], in0=ps[:, 0:32].rearrange("p (c w) -> p c w", w=2),
            in1=par[:, P_BADA:P_BADA + 16].unsqueeze(2).to_broadcast([128, 16, 2]), op=ALU.add),
            reads=[ps_t, par_t], writes=[modsb_t])
        b.op("dve", lambda e: e.tensor_scalar(out=modsb[:, 8:16, :], in0=modsb[:, 8:16, :], scalar1=1.0, scalar2=None,
                                               op0=ALU.add), reads=[modsb_t], writes=[modsb_t])
        b.es = old_es


def phase_ht(b, I, ident, ident_t, modsb, modsb_t, hT, hT_t, xts, npre):
    with ExitStack() as es:
        old_es = b.es
        b.es = es
        xi = 0
        k = 0
        groups = [(0, 4, 0), (512, 4, 0), (1024, 4, 1), (1536, 4, 1), (2048, 2, 1)]
        for (col0, ntile, w) in groups:
            tiles = []
            for t in range(ntile):
                xt, xt_t = xts[xi % 8]
                xi += 1
                r0 = col0 + t * 128
                if xi > npre:
                    b.dma("sp", xt[:], I["xall"][r0:r0 + 128, :], writes=[xt_t])
                tiles.append((xt, xt_t))
            for fc in range(8):
                ps, ps_t = b.psum()
                for t, (xt, xt_t) in enumerate(tiles):
                    b.op("pe", lambda e: e.transpose(ps[:, t * 128:(t + 1) * 128], xt[:, fc * 128:(fc + 1) * 128], ident[:]),
                         reads=[xt_t, ident_t], writes=[ps_t])
                n = ntile * 128
                k += 1
                if k % 2 == 0:
                    b.op("act", lambda e: e.activation(out=hT[:, fc, col0:col0 + n], in_=ps[:, 0:n], func=AF.Identity,
                                                        bias=modsb[:, fc, w:w + 1], scale=modsb[:, 8 + fc, w:w + 1]),
                         reads=[ps_t, modsb_t], pw=[hT_t])
                else:
                    b.op("dve", lambda e: e.tensor_scalar(out=hT[:, fc, col0:col0 + n], in0=ps[:, 0:n],
                                                           scalar1=modsb[:, 8 + fc, w:w + 1], scalar2=modsb[:, fc, w:w + 1],
                                                           op0=ALU.mult, op1=ALU.add),
                         reads=[ps_t, modsb_t], pw=[hT_t])
        b.barrier()
        b.es = old_es


def phase_hyena(b, I, par, par_t, identb, identb_t, hT, hT_t, yg, yg_t, h256, h256_t, hoo, hoo_t, hox, hox_t, dft, pre_w, shared):
    with ExitStack() as es:
        old_es = b.es
        b.es = es

        fwd256, fwd256_t = dft["fwd256"]
        inv256, inv256_t = dft["inv256"]
        fwd512, fwd512_t = dft["fwd512"]
        inv512, inv512_t = dft["inv512"]

        cwf, cwf_t = b.sb("cwf", [128, 4, 24], F32)
        for i, (j, fl) in enumerate(((0, 0), (2, 0), (0, 1), (2, 1))):
            b.op("dve", lambda e: e.tensor_scalar(out=cwf[:, i, :], in0=par[:, P_CW + j * 24:P_CW + (j + 1) * 24],
                                                   scalar1=par[:, P_FLAG + fl:P_FLAG + fl + 1], scalar2=None, op0=ALU.mult),
                 reads=[par_t], pw=[cwf_t])

        def cw(j, ci):
            return par[:, P_CW + j * 24 + ci:P_CW + j * 24 + ci + 1]

        cA, _ = b.sb("cA", [128, 2048], F32)
        cB, _ = b.sb("cB", [128, 2048], F32)
        cA_t = [Tok() for _ in range(4)]
        cB_t = [Tok() for _ in range(4)]
        edgs = [b.sb("edges%d" % i, [128, 4], F32) for i in range(2)]
        sgs = [b.sb("sg%d" % i, [128, 512], F32) for i in range(2)]
        tmps = [b.sb("sp%d" % i, [128, 256], F32) for i in range(10)]
        tmpi = [0]

        def tmp(eng="dve"):
            tmpi[0] += 1
            ring = tmps if eng == "dve" else ptmps
            return ring[tmpi[0] % len(ring)]
        utm, utm_t = b.sb("utm", [128, 16, 256], BF16)
        ufm, ufm_t = b.sb("ufm", [128, 2, 2048], BF16)
        xg, xg_t = b.sb("xg", [128, 2, NOWN], BF16)
        Y4s = [b.sb("Y4_%d" % i, [128, 4, 256], BF16) for i in range(2)]
        Y8, Y8_t = b.sb("Y8", [128, 8, 256], BF16)
        eps = [b.sb("ep%d" % i, [128, 512], F32) for i in range(4)]
        epi = [0]
        ptmps = []
        pend_is = [None]
        pend_T = []

        def conv_block(ps, ps_t, acc, acc_t, col0, ci, nseg):
            seglen = 512 // nseg
            b.op("act", lambda e: e.activation(out=acc[:, col0:col0 + 512], in_=ps[:, :], func=AF.Identity,
                                                bias=par[:, P_CB + ci:P_CB + ci + 1], scale=cw(1, ci)),
                 reads=[ps_t, par_t], writes=[acc_t])
            av = acc[:, col0:col0 + 512].rearrange("p (s l) -> p s l", s=nseg)
            pv = ps[:, :].rearrange("p (s l) -> p s l", s=nseg)
            b.op("dve", lambda e: e.scalar_tensor_tensor(out=av[:, :, 1:seglen], in0=pv[:, :, 0:seglen - 1], scalar=cw(0, ci),
                                                          in1=av[:, :, 1:seglen], op0=ALU.mult, op1=ALU.add),
                 reads=[ps_t, par_t, acc_t], writes=[acc_t])
            b.op("dve", lambda e: e.scalar_tensor_tensor(out=av[:, :, 0:seglen - 1], in0=pv[:, :, 1:seglen], scalar=cw(2, ci),
                                                          in1=av[:, :, 0:seglen - 1], op0=ALU.mult, op1=ALU.add),
                 reads=[ps_t, par_t, acc_t], writes=[acc_t])

        def fix(acc, acc_t, col, wi, ci, src, src_t):
            b.op("dve", lambda e: e.scalar_tensor_tensor(out=acc[:, col:col + 1], in0=src, scalar=cwf[:, wi, ci:ci + 1],
                                                          in1=acc[:, col:col + 1], op0=ALU.mult, op1=ALU.add),
                 reads=[src_t, cwf_t, acc_t], writes=[acc_t])

        def proj(w, w_t, col0, n=512):
            ps, ps_t = b.psum()
            for kc in range(8):
                b.op("pe", lambda e: e.matmul(ps[:, 0:n], lhsT=w[:, kc, 0:128], rhs=hT[:, kc, col0:col0 + n],
                                               start=(kc == 0), stop=(kc == 7)),
                     reads=[w_t, hT_t], writes=[ps_t])
            return ps, ps_t

        for g in range(4):
            ch0 = g * 256
            for cc in range(2):
                chunk = g * 2 + cc
                ws = []
                for sec in range(2):
                    if g == 0:
                        ws.append(pre_w[(cc, sec)])
                    elif cc == 0:
                        ws.append(shared["ws_next"][sec])
                    else:
                        ws.append(b.load_w(I["w_in"], sec * D + chunk * 128, 128))

                def P(sec, tb, cc=cc, chunk=chunk, ws=ws):
                    w, w_t = ws[sec]
                    acc, acc_t = (cA, cA_t) if sec == 0 else (cB, cB_t)
                    ed, ed_t = edgs[sec]
                    ci = sec * 8 + chunk
                    ps, ps_t = proj(w, w_t, tb * 512)
                    conv_block(ps, ps_t, acc, acc_t[tb], tb * 512, ci, 2 if tb < 2 else 1)
                    if tb >= 2:
                        b.op("dve", lambda e: e.tensor_copy(out=ed[:, 2 * (tb - 2):2 * (tb - 2) + 2], in_=ps[:, 0:512:511]),
                             reads=[ps_t], writes=[ed_t] if tb == 2 else [], pw=[] if tb == 2 else [ed_t])

                def U(tb, cc=cc):
                    first = (cc == 0 and tb == 0)
                    sl = slice(tb * 512, (tb + 1) * 512)
                    b.op("dve", lambda e: e.tensor_tensor(out=ufm[:, cc, sl], in0=cA[:, sl], in1=cB[:, sl], op=ALU.mult),
                         reads=[cA_t[tb], cB_t[tb]], writes=[ufm_t] if first else [], pw=[] if first else [ufm_t])

                def T(tb, cc=cc):
                    first = (cc == 0 and tb == 0)
                    ps, ps_t = b.psum()
                    psb = ps[:, :].bitcast(BF16)
                    for j in range(4):
                        tt = tb * 4 + j
                        b.op("pe", lambda e: e.transpose(psb[:, j * 128:(j + 1) * 128], ufm[:, cc, tt * 128:(tt + 1) * 128],
                                                          identb[:]),
                             reads=[ufm_t, identb_t], writes=[ps_t])
                    b.op("act", lambda e: e.copy(out=utm[:, tb * 4:tb * 4 + 4, cc * 128:(cc + 1) * 128],
                                                  in_=psb[:, 0:512].rearrange("p (j c) -> p j c", j=4)),
                         reads=[ps_t], writes=[utm_t] if first else [], pw=[] if first else [utm_t])

                P(0, 0); P(1, 0); U(0)
                P(0, 1); P(1, 1); U(1)
                for tfn in pend_T:
                    tfn()
                pend_T.clear()
                P(0, 2); P(1, 2); T(0)
                P(0, 3); P(1, 3)
                for sec in range(2):
                    acc, acc_t = (cA, cA_t) if sec == 0 else (cB, cB_t)
                    ed, ed_t = edgs[sec]
                    ci = sec * 8 + chunk
                    fix(acc, acc_t[2], 1024, 0, ci, ed[:, 3:4], ed_t)
                    fix(acc, acc_t[2], 1535, 3, ci, ed[:, 2:3], ed_t)
                    fix(acc, acc_t[3], 1536, 2, ci, ed[:, 1:2], ed_t)
                    fix(acc, acc_t[3], 2047, 1, ci, ed[:, 0:1], ed_t)
                U(2); U(3)
                pend_T.extend([lambda T=T: T(1), lambda T=T: T(2), lambda T=T: T(3)])
            if HY_STOP == 4:
                b.barrier(); b.es = old_es; return
            def X(tb, g=g, ccs=(0, 1)):
                for cc in ccs:
                    chunk = g * 2 + cc
                    w, w_t = xw[(cc, 2)]
                    ci = 16 + chunk
                    acc, acc_t = (cA, cA_t) if cc == 0 else (cB, cB_t)
                    ps, ps_t = proj(w, w_t, tb * 512)
                    conv_block(ps, ps_t, acc, acc_t[tb], tb * 512, ci, 2 if tb < 2 else 1)
                    if tb == 2:
                        pse, pse_t = b.psum()
                        for kc in range(8):
                            b.op("pe", lambda e: e.matmul(pse[:, 0:2], lhsT=w[:, kc, 0:128], rhs=hT[:, kc, OTH0:OTH0 + 512:511],
                                                           start=(kc == 0), stop=(kc == 7)),
                                 reads=[w_t, hT_t], writes=[pse_t])
                        fix(acc, acc_t[2], 1024, 0, ci, pse[:, 1:2], pse_t)
                        fix(acc, acc_t[2], 1535, 3, ci, pse[:, 0:1], pse_t)
                    w, w_t = xw[(cc, 3)]
                    ps, ps_t = proj(w, w_t, tb * 512)
                    sg, sg_t = sgs[cc]
                    b.op("act", lambda e: e.activation(out=sg[:], in_=ps[:, :], func=AF.Silu), reads=[ps_t], writes=[sg_t])
                    b.op("pool", lambda e: e.tensor_tensor(out=xg[:, cc, tb * 512:(tb + 1) * 512], in0=acc[:, tb * 512:(tb + 1) * 512],
                                                            in1=sg[:], op=ALU.mult),
                         reads=[acc_t[tb], sg_t], writes=[xg_t] if (cc == 0 and tb == 0) else [],
                         pw=[] if (cc == 0 and tb == 0) else [xg_t])

            xw = {}
            for cc in range(2):
                for sec in (2, 3):
                    xw[(cc, sec)] = b.load_w(I["w_in"], sec * D + (g * 2 + cc) * 128, 128)
            if g < 3:
                shared["ws_next"] = [b.load_w(I["w_in"], sec * D + ((g + 1) * 2) * 128, 128) for sec in range(2)]
            if g == 3 and HY_STOP == 0:
                shared["W0"] = [b.load_w(I["w_in"], (4 + i) * D, 128) for i in range(4)]
            def spectral(Ups, Hre, Him, H_t, Yre, Yim, Y_t, first, accum, eng="dve", ceng="pool"):
                terms_re = []
                terms_im = []
                for (ps, ps_t, hre, him, h_t) in accum:
                    ure, uim = ps[:, 0:256], ps[:, 256:512]
                    terms_re += [(ure, hre, ps_t, h_t, 1.0), (uim, him, ps_t, h_t, -1.0)]
                    terms_im += [(ure, him, ps_t, h_t, 1.0), (uim, hre, ps_t, h_t, 1.0)]
                for terms, Yo in ((terms_re, Yre), (terms_im, Yim)):
                    acc = None
                    for i, (u, h, u_t, h_t, sgn) in enumerate(terms):
                        t, t_t = tmp(eng)
                        b.op(eng, lambda e: e.tensor_tensor(out=t[:], in0=u, in1=h, op=ALU.mult),
                             reads=[u_t, h_t], writes=[t_t])
                        if acc is None:
                            acc = (t, t_t)
                            continue
                        last = (i == len(terms) - 1)
                        op = ALU.add if sgn > 0 else ALU.subtract
                        if last:
                            b.op(ceng, lambda e: e.tensor_tensor(out=Yo, in0=acc[0][:], in1=t[:], op=op),
                                 reads=[acc[1], t_t], pw=[Y_t])
                        else:
                            n, n_t = tmp(eng)
                            b.op(ceng, lambda e: e.tensor_tensor(out=n[:], in0=acc[0][:], in1=t[:], op=op),
                                 reads=[acc[1], t_t], writes=[n_t])
                            acc = (n, n_t)

            def fwd_dft(tab, tab_t, nk, tc0, ci_re, ci_im):
                ps, ps_t = b.psum()
                for part, ci in ((0, ci_re), (1, ci_im)):
                    for kc in range(nk):
                        b.op("pe", lambda e: e.matmul(ps[:, part * 256:(part + 1) * 256], lhsT=tab[:, kc, ci * 128:(ci + 1) * 128],
                                                       rhs=utm[:, tc0 + kc, :], start=(kc == 0), stop=(kc == nk - 1)),
                             reads=[tab_t, utm_t], writes=[ps_t])
                return ps, ps_t

            def epilogue(ps, ps_t, off, cc, tok0, n, g=g):
                chunk = g * 2 + cc
                epi[0] += 1
                ep, ep_t = eps[epi[0] % 4]
                b.op("dve", lambda e: e.scalar_tensor_tensor(out=ep[:, 0:n], in0=ufm[:, cc, tok0:tok0 + n],
                                                              scalar=par[:, P_HD + chunk:P_HD + chunk + 1],
                                                              in1=ps[:, off:off + n], op0=ALU.mult, op1=ALU.add),
                     reads=[ufm_t, par_t, ps_t], writes=[ep_t])
                b.op("pool", lambda e: e.tensor_tensor(out=yg[:, chunk, tok0:tok0 + n], in0=ep[:, 0:n],
                                                        in1=xg[:, cc, tok0:tok0 + n], op=ALU.mult),
                     reads=[ep_t, xg_t], pw=[yg_t])

            def F(bb):
                Y4, Y4_t = Y4s[bb % 2]
                for ip in range(2):
                    ps, ps_t = fwd_dft(fwd256, fwd256_t, 2, 2 * bb, ip, ip + 2)
                    spectral(None, None, None, None, Y4[:, ip, :], Y4[:, ip + 2, :], Y4_t, True,
                             [(ps, ps_t, h256[:, ip, ch0:ch0 + 256], h256[:, ip + 2, ch0:ch0 + 256], h256_t)])

            def Iv(bb):
                Y4, Y4_t = Y4s[bb % 2]
                ps, ps_t = b.psum()
                for cc in range(2):
                    for ci in range(4):
                        b.op("pe", lambda e: e.matmul(ps[:, cc * 256:(cc + 1) * 256], lhsT=Y4[:, ci, cc * 128:(cc + 1) * 128],
                                                       rhs=inv256[:, ci, :], start=(ci == 0), stop=(ci == 3)),
                             reads=[Y4_t, inv256_t], writes=[ps_t])
                for cc in range(2):
                    epilogue(ps, ps_t, cc * 256, cc, bb * 256, 256)

            def FS(ip):
                ps1, ps1_t = fwd_dft(fwd512, fwd512_t, 4, 8, ip, ip + 4)
                ps2, ps2_t = fwd_dft(fwd512, fwd512_t, 4, 12, ip, ip + 4)
                spectral(None, None, None, None, Y8[:, ip, :], Y8[:, ip + 4, :], Y8_t, True,
                         [(ps1, ps1_t, hoo[:, ip, ch0:ch0 + 256], hoo[:, ip + 4, ch0:ch0 + 256], hoo_t),
                          (ps2, ps2_t, hox[:, ip, ch0:ch0 + 256], hox[:, ip + 4, ch0:ch0 + 256], hox_t)])

            def IS(epilogue=epilogue):
                for cc in range(2):
                    ps, ps_t = b.psum()
                    for ci in range(8):
                        b.op("pe", lambda e: e.matmul(ps[:, :], lhsT=Y8[:, ci, cc * 128:(cc + 1) * 128], rhs=inv512[:, ci, :],
                                                       start=(ci == 0), stop=(ci == 7)),
                             reads=[Y8_t, inv512_t], writes=[ps_t])
                    epilogue(ps, ps_t, 0, cc, 1024, 512)

            X(0)
            for tfn in pend_T:
                tfn()
            pend_T.clear()
            F(0); F(1); Iv(0); F(2); Iv(1); X(1); F(3); Iv(2); FS(0); Iv(3); FS(1); X(2, ccs=(0,)); FS(2); X(2, ccs=(1,)); FS(3); IS()
            if HY_STOP == 8:
                b.barrier(); b.es = old_es; return
        b.barrier()
        b.es = old_es


def interleave_gen(*gens):
    gens = list(gens)
    while gens:
        for g in list(gens):
            try:
                next(g)
            except StopIteration:
                gens.remove(g)
        yield


def interleave(*gens):
    gens = list(gens)
    while gens:
        for g in list(gens):
            try:
                next(g)
            except StopIteration:
                gens.remove(g)


def rr(lst, state=[0]):
    state[0] += 1
    return lst[state[0] % len(lst)]


def phase_filters(b, I, par, par_t, h256, h256_t, hoo, hoo_t, hox, hox_t, dft, ada_pre):
    nc = b.nc
    with ExitStack() as es:
        old_es = b.es
        b.es = es
        fw1, fw1_t = b.sb("fw1", [33, 64], F32)
        fw2, fw2_t = b.sb("fw2", [64, 64], F32)
        w3, w3_t = b.sb("w3", [64, 3 * D], BF16)
        fb, fb_t = b.sb("fb", [64, 4], F32)
        b.dma("sp", fw1[:], I["fw1"][:, :], writes=[fw1_t])
        b.dma("sp", fw2[:], I["fw2"][:, :], writes=[fw2_t])
        b.dma("pool", w3[:], I["w3c"][:, :], writes=[w3_t])
        zts = {}
        for L in (256, 1024):
            zts[L] = b.sb("zt%d" % L, [33, L], F32)
            b.dma("sp", zts[L][0][:], I["zt%d" % L][:, :], writes=[zts[L][1]])
        absd, absd_t = b.sb("absd", [128, D], F32)
        b.dma("sp", absd[:], I["absd"][0:1, :].to_broadcast([128, D]), writes=[absd_t])
        negt, negt_t = b.sb("negt", [128, 10], F32)
        b.dma("sp", negt[:], I["negt"][:, :], writes=[negt_t])
        for name in ("fwd256",):
            b.dma("sp", dft[name][0][:], I[name].rearrange("(c p) n -> p c n", p=128), writes=[dft[name][1]])
        fpar = par[0:64, P_FPAR:P_FPAR + 4]
        for l in range(2):
            b.op("dve", lambda e, l=l: e.tensor_scalar(out=fb[:, l:l + 1], in0=fpar[:, l:l + 1],
                                                        scalar1=fpar[:, 2 + l:3 + l], scalar2=None,
                                                        op0=ALU.mult),
                 reads=[par_t], writes=[fb_t])

        def load_tab(name, rows, cols):
            t, tk = b.sb(name, [128, rows // 128, cols], BF16)
            b.dma("sp", t[:], I[name].rearrange("(c p) n -> p c n", p=128), writes=[tk])
            return t, tk
        fwd256, fwd256_t = dft["fwd256"]
        fwd512, fwd512_t = dft["fwd512"]
        bwd256, bwd256_t = load_tab("bwd256", 256, 512)
        b.dma("sp", dft["fwd512"][0][:], I["fwd512"].rearrange("(c p) n -> p c n", p=128), writes=[dft["fwd512"][1]])
        bwd512, bwd512_t = load_tab("bwd512", 512, 1024)
        x1024, x1024_t = load_tab("x1024", 1024, 1024)
        for jb in range(4):
            ada_pre.append(b.load_w(I["w_ada"], jb * 512, 512))
        for name in ("inv256", "inv512"):
            b.dma("sp", dft[name][0][:], I[name].rearrange("(c p) n -> p c n", p=128), writes=[dft[name][1]])

        for L, ztn, decn in ((256, "zt256", "dec256"), (1024, "zt1024", "dec1024")):
            with ExitStack() as es2:
                b.es = es2
                zt, zt_t = zts[L]
                hid = []
                for l in range(2):
                    hid.append(b.sb("hid%d_%d" % (l, L), [64, L], F32))
                hidb, hidb_t = b.sb("hidb%d" % L, [64, L], BF16)
                tmp, tmp_t = b.sb("ftmp%d" % L, [64, 512], F32)
                tmpi, tmpi_t = b.sb("ftmpi%d" % L, [64, 512], mybir.dt.int32)
                tmpf, tmpf_t = b.sb("ftmpf%d" % L, [64, 512], F32)
                nblk = max(1, L // 512)
                bw = min(L, 512)
                for l in range(2):
                    src, src_t = (zt, zt_t) if l == 0 else hid[0]
                    wl, wl_t = (fw1, fw1_t) if l == 0 else (fw2, fw2_t)
                    K = 33 if l == 0 else 64
                    dst, dst_t = hid[l]
                    for blk in range(nblk):
                        ps, ps_t = b.psum()
                        sl = slice(blk * bw, (blk + 1) * bw)
                        b.op("pe", lambda e: e.matmul(ps[0:64, 0:bw], lhsT=wl[0:K, :], rhs=src[0:K, sl],
                                                       start=True, stop=True),
                             reads=[wl_t, src_t], writes=[ps_t])
                        b.op("dve", lambda e: e.tensor_scalar(out=tmp[:, 0:bw], in0=ps[0:64, 0:bw],
                                                               scalar1=fpar[:, 2 + l:3 + l], scalar2=fb[:, l:l + 1],
                                                               op0=ALU.mult, op1=ALU.add),
                             reads=[ps_t, par_t, fb_t], writes=[tmp_t])
                        b.op("dve", lambda e: e.tensor_scalar(out=tmpi[:, 0:bw], in0=tmp[:, 0:bw],
                                                               scalar1=1.0 / TWO_PI, scalar2=None, op0=ALU.mult),
                             reads=[tmp_t], writes=[tmpi_t])
                        b.op("dve", lambda e: e.tensor_copy(out=tmpf[:, 0:bw], in_=tmpi[:, 0:bw]),
                             reads=[tmpi_t], writes=[tmpf_t])
                        b.op("dve", lambda e: e.scalar_tensor_tensor(out=tmp[:, 0:bw], in0=tmpf[:, 0:bw], scalar=-TWO_PI,
                                                                      in1=tmp[:, 0:bw], op0=ALU.mult, op1=ALU.add),
                             reads=[tmpf_t, tmp_t], writes=[tmp_t])
                        b.op("act", lambda e: e.activation(out=dst[:, sl], in_=tmp[:, 0:bw], func=AF.Sin),
                             reads=[tmp_t], writes=[dst_t])
                b.op("dve", lambda e: e.tensor_copy(out=hidb[:], in_=hid[1][0][:]), reads=[hid[1][1]], writes=[hidb_t])
                b.dump("hid%d" % L, hid[1][0][:], hid[1][1], [64, L])

                ntc = L // 128
                ntc_ab = ntc if L == 256 else 4
                taps, taps_t = b.sb("taps%d" % L, [128, ntc_ab, 2 * D], BF16)
                if L == 1024:
                    tapsx, tapsx_t = b.sb("tapsx%d" % L, [128, ntc, D], BF16)
                decs = [b.sb("dec%d_%d" % (L, i), [128, D], F32) for i in range(2)]
                for tc in range(ntc):
                    dec, dec_t = decs[tc % 2]
                    tcol = tc if L == 256 else 2 + tc
                    b.op("act", lambda e: e.activation(out=dec[:], in_=absd[:], func=AF.Exp, scale=negt[:, tcol:tcol + 1]),
                         reads=[absd_t, negt_t], writes=[dec_t])
                    if L == 256:
                        colblks = [0, 1, 2, 3]
                    else:
                        colblks = ([0, 1, 2, 3] if tc < 4 else []) + [4, 5]
                    for cb in colblks:
                        ps, ps_t = b.psum()
                        b.op("pe", lambda e: e.matmul(ps[:, :], lhsT=hidb[:, tc * 128:(tc + 1) * 128],
                                                       rhs=w3[:, cb * 512:(cb + 1) * 512], start=True, stop=True),
                             reads=[hidb_t, w3_t], writes=[ps_t])
                        dc = (cb % 2) * 512
                        if cb < 4:
                            dst, dst_t = taps[:, tc, cb * 512:(cb + 1) * 512], taps_t
                        else:
                            dst, dst_t = tapsx[:, tc, (cb - 4) * 512:(cb - 3) * 512], tapsx_t
                        b.op("dve", lambda e: e.scalar_tensor_tensor(out=dst, in0=dec[:, dc:dc + 512],
                                                                      scalar=0.05, in1=ps[:, :], op0=ALU.add, op1=ALU.mult),
                             reads=[ps_t, dec_t], writes=[dst_t])

                evac = ["act", "dve"]
                if L == 256:
                    jobs = [(h256, h256_t, 4, [(fwd256, fwd256_t, 0, 2), (bwd256, bwd256_t, 1, 2)])]
                else:
                    jobs = [(hoo, hoo_t, 8, [(fwd512, fwd512_t, 0, 4), (bwd512, bwd512_t, 1, 4)]),
                            (hox, hox_t, 8, [(x1024, x1024_t, 2, 8)])]
                k = 0
                for (H, H_t, ncoef, srcs) in jobs:
                    for ci in range(ncoef):
                        for cb in range(2):
                            ps, ps_t = b.psum()
                            mm = []
                            for (tab, tab_t, sec, ntcs) in srcs:
                                for tc in range(ntcs):
                                    mm.append((tab, tab_t, sec, tc))
                            for i, (tab, tab_t, sec, tc) in enumerate(mm):
                                b.op("pe", lambda e: e.matmul(
                                    ps[:, :], lhsT=tab[:, tc, ci * 128:(ci + 1) * 128],
                                    rhs=(taps[:, tc, sec * D + cb * 512: sec * D + (cb + 1) * 512] if sec < 2
                                         else tapsx[:, tc, cb * 512:(cb + 1) * 512]),
                                    start=(i == 0), stop=(i == len(mm) - 1)),
                                    reads=[tab_t, taps_t if sec < 2 else tapsx_t], writes=[ps_t])
                            k += 1
                            if k % 2 == 0:
                                b.op("act", lambda e: e.copy(out=H[:, ci, cb * 512:(cb + 1) * 512], in_=ps[:, :]),
                                     reads=[ps_t], writes=[H_t])
                            else:
                                b.op("dve", lambda e: e.tensor_copy(out=H[:, ci, cb * 512:(cb + 1) * 512], in_=ps[:, :]),
                                     reads=[ps_t], writes=[H_t])
                b.barrier(dma=False)
            b.es = es
        b.barrier(dma=False)
        b.es = old_es


def phase_attn(b, I, O, par, par_t, ident, ident_t, hT, hT_t, ya, ya_t, shared):
    with ExitStack() as es:
        old_es = b.es
        b.es = es
        cmask, cmask_t = b.sb("cmask", [128, 64], F32)
        rmask, rmask_t = b.sb("rmask", [128, 6, 8], BF16)
        b.dma("sp", cmask[:], I["cmask"][:, :], writes=[cmask_t])
        b.dma("pool", rmask[:], I["rmask"].rearrange("p (j r) -> p j r", j=6), writes=[rmask_t])
        vexts = [b.sb("vext%d" % i, [128, 16, 2, 128], BF16) for i in range(2)]
        for (v, v_t) in vexts:
            b.op("pool", lambda e: e.memset(v[:], 1.0), writes=[v_t])
        qTs = [b.sb("qT%d" % i, [128, NOWN], BF16) for i in range(2)]
        kTs = [b.sb("kT%d" % i, [128, 2048], BF16) for i in range(2)]
        kf, kf_t = b.sb("kf", [128, 1024], F32)
        ckst, ckst_t = b.sb("ckst", [128, 2, 128], F32)
        nkst, nkst_t = b.sb("nkst", [128, 8, 128], F32)
        nvst, nvst_t = b.sb("nvst", [128, 8, 128], F32)
        sgas = [b.sb("sga%d" % i, [128, NOWN], BF16) for i in range(2)]
        vsts = [b.sb("vst%d" % i, [128, 4, 128], F32) for i in range(2)]
        bts = [b.sb("bt%d" % i, [128, 24, 64], F32) for i in range(2)]
        ptss = [[b.sb("pt%d_%d" % (s2, i), [128, 512], BF16) for i in range(9)] for s2 in range(2)]
        Ts = [b.sb("T%d" % i, [128, 512], F32) for i in range(2)]
        rcs = [b.sb("rc%d" % i, [128, 512], F32) for i in range(4)]
        pti = [0, 0]

        def pt(s2):
            pti[s2] += 1
            return ptss[s2][pti[s2] % 9]

        def proj(w, w_t, col0, n=512):
            ps, ps_t = b.psum()
            for kc in range(8):
                b.op("pe", lambda e: e.matmul(ps[:, 0:n], lhsT=w[:, kc, 0:128], rhs=hT[:, kc, col0:col0 + n],
                                               start=(kc == 0), stop=(kc == 7)),
                     reads=[w_t, hT_t], writes=[ps_t])
            return ps, ps_t

        fin = [0]

        def finalize(pso, pso_t, e, hp, tok0, n, sga, sga_t):
            ob = 64 * e
            db = 64 * (1 - e)
            rc, rc_t = rcs[fin[0] % len(rcs)]
            fin[0] += 1
            b.op("dve", lambda en: en.tensor_copy(out=rc[ob:ob + 64, 0:n], in_=pso[db:db + 64, 0:n]),
                 reads=[pso_t], writes=[rc_t])
            b.op("act", lambda en: en.activation(out=rc[ob:ob + 64, 0:n], in_=rc[ob:ob + 64, 0:n], func=AF.Ln),
                 reads=[rc_t], writes=[rc_t])
            b.op("act", lambda en: en.activation(out=rc[ob:ob + 64, 0:n], in_=rc[ob:ob + 64, 0:n], func=AF.Exp, scale=-1.0),
                 reads=[rc_t], writes=[rc_t])
            b.op("dve", lambda en: en.tensor_tensor(out=rc[ob:ob + 64, 0:n], in0=pso[ob:ob + 64, 0:n],
                                                     in1=rc[ob:ob + 64, 0:n], op=ALU.mult),
                 reads=[pso_t, rc_t], writes=[rc_t])
            b.op("pool", lambda en: en.tensor_tensor(out=ya[ob:ob + 64, hp, tok0:tok0 + n], in0=rc[ob:ob + 64, 0:n],
                                                      in1=sga[ob:ob + 64, tok0:tok0 + n], op=ALU.mult),
                 reads=[rc_t, sga_t], pw=[ya_t])

        def load_pair(hp):
            return [b.load_w(I["w_in"], (4 + i) * D + hp * 128, 128) for i in range(4)]

        def bufs(hp):
            return qTs[hp % 2] + kTs[hp % 2] + sgas[hp % 2] + vexts[hp % 2]

        def proj_gen(hp, W):
            qT, qT_t, kT, kT_t, sga, sga_t, vext, vext_t = bufs(hp)
            (wq, wq_t), (wk, wk_t), (wv, wv_t), (wg, wg_t) = W
            for tb in range(3):
                ps, ps_t = proj(wq, wq_t, tb * 512)
                b.op("act", lambda e: e.activation(out=qT[:, tb * 512:(tb + 1) * 512], in_=ps[:, :], func=AF.Copy, scale=0.125),
                     reads=[ps_t], writes=[qT_t] if tb == 0 else [], pw=[] if tb == 0 else [qT_t])
                yield
            for i, (src0, n, dst0) in enumerate(((0, 512, 0), (512, 512, 512), (1024, 512, 1024), (2048, 256, 1536))):
                ps, ps_t = proj(wk, wk_t, src0, n)
                if i < 2:
                    b.op("act", lambda e: e.copy(out=kf[:, dst0:dst0 + n], in_=ps[:, 0:n]),
                         reads=[ps_t], writes=[kf_t] if i == 0 else [], pw=[] if i == 0 else [kf_t])
                    b.op("dve", lambda e: e.tensor_copy(out=kT[:, dst0:dst0 + n], in_=kf[:, dst0:dst0 + n]),
                         reads=[kf_t], writes=[kT_t] if i == 0 else [], pw=[] if i == 0 else [kT_t])
                else:
                    b.op("dve", lambda e: e.tensor_copy(out=kT[:, dst0:dst0 + n], in_=ps[:, 0:n]),
                         reads=[ps_t], pw=[kT_t])
                yield
            b.dma("sp", ckst[:], I["ck"][:, hp * 128:(hp + 1) * 128].rearrange("(c p) n -> p c n", p=128), writes=[ckst_t])
            ps, ps_t = b.psum()
            for c in range(2):
                b.op("pe", lambda e: e.transpose(ps[:, c * 128:(c + 1) * 128], ckst[:, c, :], ident[:]),
                     reads=[ckst_t, ident_t], writes=[ps_t])
            b.op("dve", lambda e: e.tensor_copy(out=kT[:, 1792:2048], in_=ps[:, 0:256]), reads=[ps_t], pw=[kT_t])
            for half2 in range(2):
                ps, ps_t = b.psum()
                for j in range(4):
                    tt = half2 * 4 + j
                    b.op("pe", lambda e: e.transpose(ps[:, j * 128:(j + 1) * 128], kf[:, tt * 128:(tt + 1) * 128], ident[:]),
                         reads=[kf_t, ident_t], writes=[ps_t])
                b.op("act", lambda e: e.copy(out=nkst[:, half2 * 4:half2 * 4 + 4, :],
                                              in_=ps[:, :].rearrange("p (j c) -> p j c", j=4)),
                     reads=[ps_t], writes=[nkst_t] if half2 == 0 else [], pw=[] if half2 == 0 else [nkst_t])
            b.dma("sp", O["nk"][:, hp * 128:(hp + 1) * 128].rearrange("(t p) c -> p t c", p=128), nkst[:],
                  reads=[nkst_t], is_output=True)
            cols = [t * 128 for t in range(12)] + [2048, 2176]
            for q4 in range(4):
                tl = cols[q4 * 4:q4 * 4 + 4]
                ps, ps_t = b.psum()
                for j, col in enumerate(tl):
                    for kc in range(8):
                        b.op("pe", lambda e: e.matmul(ps[:, j * 128:(j + 1) * 128], lhsT=hT[:, kc, col:col + 128],
                                                       rhs=wv[:, kc, 0:128], start=(kc == 0), stop=(kc == 7)),
                             reads=[hT_t, wv_t], writes=[ps_t])
                nj = len(tl)
                pv = ps[:, 0:nj * 128].rearrange("p (j c) -> p j c", j=nj)
                first = (q4 == 0)
                if q4 < 2:
                    st, st_t = nvst[:, q4 * 4:q4 * 4 + 4, :], nvst_t
                    b.op("act", lambda e: e.copy(out=st, in_=pv), reads=[ps_t],
                         writes=[nvst_t] if first else [], pw=[] if first else [nvst_t])
                else:
                    vs, st_t = vsts[q4 % 2]
                    st = vs[:, 0:nj, :]
                    b.op("act", lambda e: e.copy(out=st, in_=pv), reads=[ps_t], writes=[st_t])
                b.op("dve", lambda e: e.tensor_copy(out=vext[:, q4 * 4:q4 * 4 + nj, 0, 0:64], in_=st[:, :, 0:64]),
                     reads=[st_t], writes=[vext_t] if first else [], pw=[] if first else [vext_t])
                b.op("pool", lambda e: e.tensor_copy(out=vext[:, q4 * 4:q4 * 4 + nj, 1, 64:128], in_=st[:, :, 64:128]),
                     reads=[st_t], pw=[vext_t])
                yield
            b.dma("sp", O["nv"][:, hp * 128:(hp + 1) * 128].rearrange("(t p) c -> p t c", p=128), nvst[:],
                  reads=[nvst_t], is_output=True)
            cvv = I["cv"][:, hp * 128:(hp + 1) * 128].rearrange("(c p) n -> p c n", p=128)
            b.dma("pool", vext[:, 14:16, 0, 0:64], cvv[:, :, 0:64], pw=[vext_t])
            b.dma("pool", vext[:, 14:16, 1, 64:128], cvv[:, :, 64:128], pw=[vext_t])
            for tb in range(3):
                ps, ps_t = proj(wg, wg_t, tb * 512)
                b.op("act", lambda e: e.activation(out=sga[:, tb * 512:(tb + 1) * 512], in_=ps[:, :], func=AF.Silu),
                     reads=[ps_t], writes=[sga_t] if tb == 0 else [], pw=[] if tb == 0 else [sga_t])
                yield

        def attn_gen(hp):
            qT, qT_t, kT, kT_t, sga, sga_t, vext, vext_t = bufs(hp)
            if AT_STOP == 1:
                return
            def head_gen(e2, hp=hp, vext=vext, vext_t=vext_t):
                h = 2 * hp + e2
                pb = 64 * e2
                bt, bt_t = bts[h % 2]
                b.dma("sp", bt[:], I["btab"][h].rearrange("p (i c) -> p i c", i=24), writes=[bt_t])
                b.op("pool", lambda e: e.tensor_tensor(out=bt[:], in0=bt[:],
                                                        in1=cmask[:].unsqueeze(1).to_broadcast([128, 24, 64]), op=ALU.add),
                     reads=[bt_t, cmask_t], writes=[bt_t])
                def s_stage(bb):
                    ptl = []
                    for kc2 in range(2):
                        ch = 2 * bb + kc2
                        ps, ps_t = b.psum()
                        b.op("pe", lambda e: e.matmul(ps[:, 0:256], lhsT=kT[pb:pb + 64, ch * 128:(ch + 1) * 128],
                                                       rhs=qT[pb:pb + 64, bb * 256:(bb + 1) * 256], start=True, stop=True),
                             reads=[kT_t, qT_t], writes=[ps_t])
                        p, p_t = pt(e2)
                        b.op("act", lambda e: e.activation(out=p[:, 0:256], in_=ps[:, 0:256], func=AF.Exp),
                             reads=[ps_t], writes=[p_t])
                        ptl.append((p, p_t, ch))
                    return ptl

                def pv_stage(bb, ptl):
                    pso, pso_t = b.psum()
                    for i, (p, p_t, ch) in enumerate(ptl):
                        b.op("pe", lambda e: e.matmul(pso[:, 0:256], lhsT=vext[:, ch, e2, :], rhs=p[:, 0:256],
                                                       start=(i == 0), stop=(i == 1)),
                             reads=[vext_t, p_t], writes=[pso_t])
                    finalize(pso, pso_t, e2, hp, bb * 256, 256, sga, sga_t)

                prev = s_stage(0)
                yield
                for bb in range(1, 4):
                    cur = s_stage(bb)
                    pv_stage(bb - 1, prev)
                    prev = cur
                    yield
                pend = (3, prev)
                if AT_STOP == 2:
                    pv_stage(*pend)
                    return
                ptl = []
                for j in range(6):
                    kcol = 1024 + j * 128 if j < 4 else 1536 + (j - 4) * 128
                    i0 = (6 - 2 * j) if j < 4 else (14 + 2 - 2 * (j - 4))
                    ps, ps_t = b.psum()
                    b.op("pe", lambda e: e.matmul(ps[:, :], lhsT=kT[pb:pb + 64, kcol:kcol + 128],
                                                   rhs=qT[pb:pb + 64, 1024:1536], start=True, stop=True),
                         reads=[kT_t, qT_t], writes=[ps_t])
                    T, T_t = Ts[j % 2]
                    b.op("dve", lambda e: e.tensor_tensor(out=T[:].rearrange("p (r c) -> p r c", r=8),
                                                           in0=ps[:, :].rearrange("p (r c) -> p r c", r=8),
                                                           in1=bt[:, i0:i0 + 8, :], op=ALU.add),
                         reads=[ps_t, bt_t], writes=[T_t])
                    p, p_t = pt(e2)
                    b.op("act", lambda e: e.activation(out=p[:], in_=T[:], func=AF.Exp), reads=[T_t], writes=[p_t])
                    b.op("pool", lambda e: e.tensor_tensor(out=p[:].rearrange("p (r c) -> p r c", r=8),
                                                            in0=p[:].rearrange("p (r c) -> p r c", r=8),
                                                            in1=rmask[:, j, :].unsqueeze(2).to_broadcast([128, 8, 64]), op=ALU.mult),
                         reads=[p_t, rmask_t], writes=[p_t])
                    ptl.append((p, p_t, 8 + j))
                    if pend is not None:
                        pv_stage(*pend)
                        pend = None
                    yield
                for c in range(2):
                    ps, ps_t = b.psum()
                    b.op("pe", lambda e: e.matmul(ps[:, :], lhsT=kT[pb:pb + 64, 1792 + c * 128:1792 + (c + 1) * 128],
                                                   rhs=qT[pb:pb + 64, 1024:1536], start=True, stop=True),
                         reads=[kT_t, qT_t], writes=[ps_t])
                    p, p_t = pt(e2)
                    b.op("act", lambda e: e.activation(out=p[:], in_=ps[:, :], func=AF.Exp), reads=[ps_t], writes=[p_t])
                    ptl.append((p, p_t, 14 + c))
                    yield
                pso, pso_t = b.psum()
                for i, (p, p_t, ch) in enumerate(ptl):
                    b.op("pe", lambda e: e.matmul(pso[:, :], lhsT=vext[:, ch, e2, :], rhs=p[:], start=(i == 0), stop=(i == 7)),
                         reads=[vext_t, p_t], writes=[pso_t])
                finalize(pso, pso_t, e2, hp, 1024, 512, sga, sga_t)
            yield from interleave_gen(head_gen(0), head_gen(1))

        W = {0: shared.get("W0") or load_pair(0)}
        if AT_NHP > 1:
            W[1] = load_pair(1)
        interleave(proj_gen(0, W[0]))
        for hp in range(AT_NHP):
            if hp + 2 < AT_NHP:
                W[hp + 2] = load_pair(hp + 2)
            if hp == AT_NHP - 1 and AT_NHP == 8:
                shared["M0"] = [b.load_w(I["w_bh"], 0, 128), b.load_w(I["w_in"], 8 * D, 128),
                                b.load_w(I["w_ba"], 0, 128), b.load_w(I["w_in"], 9 * D, 128)]
                shared["M1"] = [b.load_w(I["w_bh"], 128, 128), b.load_w(I["w_in"], 8 * D + 128, 128),
                                b.load_w(I["w_ba"], 128, 128), b.load_w(I["w_in"], 9 * D + 128, 128)]
            gens = [attn_gen(hp)]
            if hp + 1 < AT_NHP:
                gens.append(proj_gen(hp + 1, W[hp + 1]))
            interleave(*gens)
        b.barrier()
        b.es = old_es


def phase_merge(b, I, hT, hT_t, yg, yg_t, ya, ya_t, mg, mg_t, shared):
    with ExitStack() as es:
        old_es = b.es
        b.es = es
        sgs = [b.sb("msg%d" % i, [128, 512], F32) for i in range(4)]
        t1s = [b.sb("mt%d" % i, [128, 512], F32) for i in range(4)]
        k = 0
        def load_oc(oc):
            return [b.load_w(I["w_bh"], oc * 128, 128), b.load_w(I["w_in"], 8 * D + oc * 128, 128),
                    b.load_w(I["w_ba"], oc * 128, 128), b.load_w(I["w_in"], 9 * D + oc * 128, 128)]
        nxt = shared.get("M0") or load_oc(0)
        for oc in range(8):
            (wbh, wbh_t), (wmh, wmh_t), (wba, wba_t), (wma, wma_t) = nxt
            if oc + 1 < 8:
                nxt = shared["M1"] if (oc == 0 and "M1" in shared) else load_oc(oc + 1)
            for tb in range(3):
                prods = []
                for (wp, wp_t, src, src_t, wm, wm_t) in ((wbh, wbh_t, yg, yg_t, wmh, wmh_t), (wba, wba_t, ya, ya_t, wma, wma_t)):
                    psm, psm_t = b.psum()
                    for kc in range(8):
                        b.op("pe", lambda e: e.matmul(psm[:, :], lhsT=wm[:, kc, 0:128], rhs=hT[:, kc, tb * 512:(tb + 1) * 512],
                                                       start=(kc == 0), stop=(kc == 7)),
                             reads=[wm_t, hT_t], writes=[psm_t])
                    sg, sg_t = sgs[k % 4]
                    b.op("act", lambda e: e.activation(out=sg[:], in_=psm[:, :], func=AF.Sigmoid), reads=[psm_t], writes=[sg_t])
                    psp, psp_t = b.psum()
                    for kc in range(8):
                        b.op("pe", lambda e: e.matmul(psp[:, :], lhsT=wp[:, kc, 0:128], rhs=src[:, kc, tb * 512:(tb + 1) * 512],
                                                       start=(kc == 0), stop=(kc == 7)),
                             reads=[wp_t, src_t], writes=[psp_t])
                    t1, t1_t = t1s[k % 4]
                    k += 1
                    b.op("dve", lambda e: e.tensor_tensor(out=t1[:], in0=psp[:, :], in1=sg[:], op=ALU.mult),
                         reads=[psp_t, sg_t], writes=[t1_t])
                    prods.append((t1, t1_t))
                b.op("pool", lambda e: e.tensor_tensor(out=mg[:, oc, tb * 512:(tb + 1) * 512], in0=prods[0][0][:],
                                                        in1=prods[1][0][:], op=ALU.add),
                     reads=[prods[0][1], prods[1][1]], pw=[mg_t])
        b.barrier()
        b.es = old_es


def phase_final(b, I, O, par, par_t, mg, mg_t, pre):
    nc = b.nc
    with ExitStack() as es:
        old_es = b.es
        b.es = es
        rows, rows_t = pre["rows"]
        grow, grow_t = b.sb("grow", [128, 2, D], F32)
        with ExitStack() as es2:
            b.es = es2
            brow, brow_t = pre["brow"]
            sc, sc_t = b.sb("sc", [128, 16], F32)
            screp, screp_t = b.sb("screp", [128, 2, 8, 128], BF16)
            b.op("act", lambda e: e.activation(out=sc[:], in_=par[:, P_COND:P_COND + 16], func=AF.Silu),
                 reads=[par_t], writes=[sc_t])
            for w in range(2):
                b.op("dve", lambda e: e.tensor_copy(out=screp[:, w, :, :],
                                                     in_=sc[:, w:16:2].unsqueeze(2).to_broadcast([128, 8, 128])),
                     reads=[sc_t], pw=[screp_t])
            for cb in range(2):
                wa, wa_t = pre["wa"][cb]
                for w in range(2):
                    ps, ps_t = b.psum()
                    for kc in range(8):
                        b.op("pe", lambda e: e.matmul(ps[:, :], lhsT=screp[:, w, kc, :], rhs=wa[:, kc, :],
                                                       start=(kc == 0), stop=(kc == 7)),
                             reads=[screp_t, wa_t], writes=[ps_t])
                    b.op("dve", lambda e: e.tensor_tensor(out=grow[:, w, cb * 512:(cb + 1) * 512], in0=ps[:, :],
                                                           in1=brow[:, cb * 512:(cb + 1) * 512], op=ALU.add),
                         reads=[ps_t, brow_t], pw=[grow_t])
            b.barrier()
        b.es = es
        wo = pre["wo"]
        xts = [b.sb("fx%d" % i, [128, D], F32) for i in range(3)]
        rts = [b.sb("fr%d" % i, [128, D], F32) for i in range(4)]
        FMAX = int(nc.vector.BN_STATS_FMAX)
        SD = int(nc.vector.BN_STATS_DIM)
        AD = int(nc.vector.BN_AGGR_DIM)
        nst = (D + FMAX - 1) // FMAX
        statss = [b.sb("stats%d" % i, [128, nst, SD], F32) for i in range(4)]
        mvs = [b.sb("mv%d" % i, [128, AD], F32) for i in range(4)]
        rstds = [b.sb("rstd%d" % i, [128, 1], F32) for i in range(4)]
        for tt in range(12):
            w = 0 if tt < 8 else 1
            xt, xt_t = xts[tt % 3]
            rt, rt_t = rts[tt % 4]
            stats, stats_t = statss[tt % 4]
            mv, mv_t = mvs[tt % 4]
            rstd, rstd_t = rstds[tt % 4]
            if tt == 0:
                for t0 in range(2):
                    b.dma("sp", xts[t0][0][:], I["xall"][t0 * 128:(t0 + 1) * 128, :], writes=[xts[t0][1]])
            if tt + 2 < 12:
                nx, nx_t = xts[(tt + 2) % 3]
                b.dma("sp", nx[:], I["xall"][(tt + 2) * 128:(tt + 3) * 128, :], writes=[nx_t])
            for cb in range(2):
                ps, ps_t = b.psum()
                for kc in range(8):
                    b.op("pe", lambda e: e.matmul(ps[:, :], lhsT=mg[:, kc, tt * 128:(tt + 1) * 128], rhs=wo[cb][0][:, kc, :],
                                                   start=(kc == 0), stop=(kc == 7)),
                         reads=[mg_t, wo[cb][1]], writes=[ps_t])
                b.op("dve", lambda e: e.tensor_tensor(out=rt[:, cb * 512:(cb + 1) * 512], in0=ps[:, :],
                                                       in1=grow[:, w, cb * 512:(cb + 1) * 512], op=ALU.mult),
                     reads=[ps_t, grow_t], writes=[rt_t] if cb == 0 else [], pw=[] if cb == 0 else [rt_t])
            b.op("dve", lambda e: e.scalar_tensor_tensor(out=rt[:], in0=xt[:], scalar=ALPHA, in1=rt[:],
                                                          op0=ALU.mult, op1=ALU.add),
                 reads=[xt_t, rt_t], writes=[rt_t])
            for c in range(nst):
                lo = c * FMAX
                hi = min(D, lo + FMAX)
                b.op("dve", lambda e: e.bn_stats(out=stats[:, c, :], in_=rt[:, lo:hi]),
                     reads=[rt_t], writes=[stats_t] if c == 0 else [], pw=[] if c == 0 else [stats_t])
            b.op("dve", lambda e: e.bn_aggr(out=mv[:], in_=stats[:]), reads=[stats_t], writes=[mv_t])
            b.op("dve", lambda e: e.tensor_scalar(out=rstd[:], in0=mv[:, 1:2], scalar1=LN_EPS, scalar2=None, op0=ALU.add),
                 reads=[mv_t], writes=[rstd_t])
            b.op("act", lambda e: e.sqrt(out=rstd[:], in_=rstd[:]), reads=[rstd_t], writes=[rstd_t])
            b.op("dve", lambda e: e.reciprocal(out=rstd[:], in_=rstd[:]), reads=[rstd_t], writes=[rstd_t])
            b.op("dve", lambda e: e.tensor_scalar(out=rt[:], in0=rt[:], scalar1=mv[:, 0:1], scalar2=rstd[:, 0:1],
                                                   op0=ALU.subtract, op1=ALU.mult),
                 reads=[rt_t, mv_t, rstd_t], writes=[rt_t])
            b.op("pool", lambda e: e.tensor_tensor(out=rt[:], in0=rt[:], in1=rows[:, 0, :], op=ALU.mult),
                 reads=[rt_t, rows_t], writes=[rt_t])
            b.op("pool", lambda e: e.tensor_tensor(out=rt[:], in0=rt[:], in1=rows[:, 1, :], op=ALU.add),
                 reads=[rt_t, rows_t], writes=[rt_t])
            b.dma("pool", O["y"][tt * 128:(tt + 1) * 128, :], rt[:], reads=[rt_t], is_output=True)
        b.barrier()
        b.es = old_es


def _attn_tabs(half, rpb):
    kc = np.arange(64)[:, None]
    col = np.arange(64)[None, :]
    dcidx = np.clip(kc - col + 15, 0, 30)
    bt = np.zeros((16, 128, 24, 64), np.float32)
    for e in range(2):
        for ip in range(14):
            dr = (13 - ip) - 7 + e
            if -7 <= dr <= 7:
                bt[:, e * 64:(e + 1) * 64, ip, :] = rpb[:, dr + 7][:, dcidx]
        D0 = 8 if half == 0 else -4
        for ip in range(10):
            dr = D0 + (9 - ip) - 7 + e
            if -7 <= dr <= 7:
                bt[:, e * 64:(e + 1) * 64, 14 + ip, :] = rpb[:, dr + 7][:, dcidx]
    c0 = np.clip(col - 8, 0, 48)
    inwin = (kc >= c0) & (kc < c0 + 16)
    cm = np.where(inwin, 0.0, -30000.0).astype(np.float32)
    cmask = np.concatenate([cm, cm], 0)
    rmask = np.zeros((128, 6, 8), np.float32)
    for j in range(6):
        for e in range(2):
            if half == 0:
                kr = 2 * j + e if j < 4 else 8 + 2 * (j - 4) + e
            else:
                kr = 8 + 2 * j + e if j < 4 else 4 + 2 * (j - 4) + e
            for rl in range(8):
                r = rl if half == 0 else 8 + rl
                r0 = min(max(r - 4, 0), 8)
                rmask[e * 64:(e + 1) * 64, j, rl] = 1.0 if (r0 <= kr < r0 + 8) else 0.0
    return bt.reshape(16, 128, 24 * 64), cmask, rmask.reshape(128, 48)

def _core_inputs(j, inp):
    bs, half = j // 2, j % 2
    c = _consts(half)
    xs = inp["x_sample"][bs]
    own = xs[half * 512:(half + 1) * 512]
    oth = xs[(1 - half) * 512:(2 - half) * 512]
    halo = xs[512:768] if half == 0 else xs[256:512]
    xall = np.concatenate([inp["x_prompt"][4 * j:4 * j + 4].reshape(1024, D), own, oth, halo], 0)
    par = np.zeros((128, P_NP), np.float32)
    cond = np.stack([inp["c_ctx"], inp["c"][bs]], 0)
    par[:, P_COND:P_COND + 16] = cond.reshape(2, 8, 128).transpose(2, 1, 0).reshape(128, 16)
    par[:, P_BADA:P_BADA + 24] = inp["b_ada"][0].reshape(24, 128).T
    par[:, P_CW:P_CW + 72] = inp["conv_w"][0].reshape(3, 24, 128).transpose(2, 0, 1).reshape(128, 72)
    par[:, P_CB:P_CB + 24] = inp["conv_b"][0].reshape(24, 128).T
    par[:, P_HD:P_HD + 8] = inp["hyena_d"][0].reshape(8, 128).T
    par[:, P_FLAG] = 1.0 if half == 1 else 0.0
    par[:, P_FLAG + 1] = 1.0 if half == 0 else 0.0
    par[0:64, P_FPAR + 0] = inp["filt_b1"][0]
    par[0:64, P_FPAR + 1] = inp["filt_b2"][0]
    par[0:64, P_FPAR + 2] = inp["filt_freq"][0, 0]
    par[0:64, P_FPAR + 3] = inp["filt_freq"][0, 1]
    w3 = inp["filt_w3"][0]
    w3c = np.concatenate([w3, w3[:, :D] if half == 1 else w3[:, D:]], 1)
    m = {
        "xall": xall, "params": par,
        "ck": inp["cache_k"][bs, 0].reshape(256, D), "cv": inp["cache_v"][bs, 0].reshape(256, D),
        "w_ada": inp["w_ada"][0], "w_in": inp["w_in"][0], "w_bh": inp["w_bh"][0], "w_ba": inp["w_ba"][0],
        "w_out": inp["w_out"][0], "fw1": inp["filt_w1"][0], "fw2": inp["filt_w2"][0], "w3c": w3c,
        "rows": np.stack([inp["b_ada"][0, 2 * D:], inp["ln_g"][0], inp["ln_b"][0]], 0),
    }
    m["btab"], m["cmask"], m["rmask"] = _attn_tabs(half, inp["rpb"][0])
    m.update(c)
    out = {}
    for k, v in m.items():
        if k in ("fwd256", "bwd256", "inv256", "fwd512", "bwd512", "inv512", "x1024"):
            out[k] = np.ascontiguousarray(v)
        else:
            out[k] = np.ascontiguousarray(v, dtype=np.float32)
    return out


_PROG = {}


def kernel(**inputs):
    inp = {k: np.asarray(v) for k, v in inputs.items()}
    if "nc" not in _PROG:
        _PROG["nc"], _PROG["b"] = build_program()
    nc = _PROG["nc"]
    in_maps = [_core_inputs(j, inp) for j in range(NCORES)]
    res = run_bass_kernel_spmd(nc, in_maps, core_ids=list(range(NCORES)))
    _PROG["last"] = res
    y_p = np.zeros((32, 256, D), np.float32)
    y_s = np.zeros((4, 1024, D), np.float32)
    nk = np.zeros((32, 1, 256, 16, 64), np.float32)
    nv = np.zeros((32, 1, 256, 16, 64), np.float32)
    for j in range(NCORES):
        r = res.results[j]
        bs, half = j // 2, j % 2
        y_p[4 * j:4 * j + 4] = r["y"][:1024].reshape(4, 256, D)
        y_s[bs, half * 512:(half + 1) * 512] = r["y"][1024:]
        nk[4 * j:4 * j + 4, 0] = r["nk"].reshape(4, 256, 16, 64)
        nv[4 * j:4 * j + 4, 0] = r["nv"].reshape(4, 256, 16, 64)
    return y_p, y_s, nk, nv
```

```python
import math
from contextlib import ExitStack

import numpy as np
import concourse.bass as bass
import concourse.mybir as mybir
from concourse.bass_utils import run_bass_kernel_spmd

F32 = mybir.dt.float32
BF16 = mybir.dt.bfloat16
AF = mybir.ActivationFunctionType
ALU = mybir.AluOpType

D = 1024
NCORES = 8
PR0, OWN0, OTH0, HALO0, NTOK = 0, 1024, 1536, 2048, 2304
NOWN = 1536
ALPHA = 2.0 ** 0.25
LN_EPS = 1e-5
TWO_PI = 2.0 * math.pi

P_COND, P_BADA, P_CW, P_CB, P_HD, P_FLAG, P_FPAR, P_NP = 0, 16, 40, 112, 136, 144, 146, 150

class StopBuild(Exception):
    pass


HY_STOP = 0
AT_STOP = 0
SAME_ENGINE_NOSYNC = ("pe",)
AT_NHP = 8
AT_VAR = 0
DEBUG = {}
STOP_AFTER = None


class Tok:
    __slots__ = ("w", "r", "x")

    def __init__(self):
        self.w = {}
        self.r = {}
        self.x = {}


class Bld:
    def __init__(self, nc, es, n_dma=32):
        self.nc = nc
        self.es = es
        self.engs = {"pe": nc.tensor, "act": nc.scalar, "dve": nc.vector, "pool": nc.gpsimd, "sp": nc.sync}
        self.sems = {}
        self.cnt = {}
        for e in ("pe", "act", "dve", "pool"):
            self.sems[e] = es.enter_context(nc.semaphore("s_" + e))
            self.cnt[e] = 0
        self.dcnt = []
        for i in range(n_dma):
            self.sems[("d", i)] = es.enter_context(nc.semaphore("s_d%d" % i))
            self.dcnt.append(0)
        self.drr = 0
        self.drr_sw = 0
        self.waited = {e: {} for e in self.engs}
        self.out_events = {}
        self.ps = []
        for i in range(8):
            t = es.enter_context(nc.psum_tensor("psb%d" % i, [128, 512], F32))
            self.ps.append((t, Tok()))
        self.psi = 0
        self.dbg = {}

    def psum(self):
        t = self.ps[self.psi]
        self.psi = (self.psi + 1) % 8
        return t

    def _deps(self, reads, writes, pw=()):
        deps = {}

        def add(d):
            for sk, v in d.items():
                if deps.get(sk, 0) < v:
                    deps[sk] = v
        for t in reads:
            add(t.w)
        for t in writes:
            add(t.w)
            add(t.r)
        for t in pw:
            add(t.r)
            add(t.x)
        return deps

    def _wait(self, eng, deps):
        for sk, v in deps.items():
            if sk == eng and eng in SAME_ENGINE_NOSYNC:
                continue
            if self.waited[eng].get(sk, 0) >= v:
                continue
            self.engs[eng].wait_ge(self.sems[sk], v)
            self.waited[eng][sk] = v

    def _record(self, sk, v, reads, writes, pw=()):
        for t in writes:
            t.w = {sk: v}
            t.x = {sk: v}
            t.r = {}
        for t in pw:
            if t.w.get(sk, 0) < v:
                t.w[sk] = v
        for t in reads:
            if t.r.get(sk, 0) < v:
                t.r[sk] = v

    def op(self, eng, fn, reads=(), writes=(), pw=()):
        self._wait(eng, self._deps(reads, writes, pw))
        ins = fn(self.engs[eng])
        self.cnt[eng] += 1
        ins.then_inc(self.sems[eng], 1)
        self._record(eng, self.cnt[eng], reads, writes, pw)

    def dma(self, q, out, in_, reads=(), writes=(), pw=(), is_output=False, **kw):
        self._wait(q, self._deps(reads, writes, pw))
        half = len(self.dcnt) // 2
        if q == "pool":
            i = half + self.drr_sw
            self.drr_sw = (self.drr_sw + 1) % half
        else:
            i = self.drr
            self.drr = (self.drr + 1) % half
        if self.dcnt[i] > 0:
            self._wait(q, {("d", i): self.dcnt[i]})
        self.dcnt[i] += 16
        self.engs[q].dma_start(out=out, in_=in_, **kw).then_inc(self.sems[("d", i)], 16)
        self._record(("d", i), self.dcnt[i], reads, writes, pw)
        if is_output:
            self.out_events[("d", i)] = self.dcnt[i]

    def barrier(self, dma=True):
        deps = {e: c for e, c in self.cnt.items() if c > 0}
        if dma:
            for i, c in enumerate(self.dcnt):
                if c > 0:
                    deps[("d", i)] = c
        for e in self.engs:
            self._wait(e, deps)

    def finish(self):
        self._wait("sp", dict(self.out_events))
        self.barrier()

    def init_wrings(self):
        self.wsm = [self.sb("wsm%d" % i, [128, 8, 128], BF16) for i in range(8)]
        self.wsi = 0
        self.wbi = 0

    def alloc_big(self, n=2):
        self.wbi += 1000
        self.wbg = [self.sb("wbg%d_%d" % (self.wbi, i), [128, 8, 512], BF16) for i in range(n)]

    def load_w(self, dram_w, col0, ncols):
        if ncols <= 128:
            t, tk = self.wsm[self.wsi % len(self.wsm)]
            self.wsi += 1
        else:
            t, tk = self.wbg[self.wbi % len(self.wbg)]
            self.wbi += 1
        self.dma("pool", t[:, :, 0:ncols], dram_w[:, col0:col0 + ncols].rearrange("(c p) n -> p c n", p=128), writes=[tk])
        return t, tk

    def sb(self, name, shape, dt, side=None):
        if side is None:
            t = self.es.enter_context(self.nc.sbuf_tensor("sb_" + name, shape, dt))
        else:
            t = self.es.enter_context(self.nc.sbuf_tensor("sb_" + name, shape, dt, side=side))
        return t, Tok()

    def dump(self, name, ap, tok, shape, dt=F32):
        if name not in DEBUG:
            return
        d = self.nc.dram_tensor("dbg_" + name, list(shape), dt, kind="ExternalOutput").ap()
        self.dma("sp", d, ap, reads=[tok], is_output=True)
        self.dbg[name] = d


def _fwd_tab(L, n):
    s = np.arange(L, dtype=np.float64)[:, None]
    f = (np.arange(n // 2, dtype=np.float64) + 0.5)[None, :]
    ang = 2.0 * np.pi * s * f / n
    return np.concatenate([np.cos(ang), -np.sin(ang)], 1).astype(np.float32)


def _bwd_tab(L, n):
    s = np.arange(L, dtype=np.float64)[:, None]
    f = (np.arange(n // 2, dtype=np.float64) + 0.5)[None, :]
    ang = 2.0 * np.pi * s * f / n
    t = np.concatenate([np.cos(ang), np.sin(ang)], 1)
    t[0] = 0.0
    return t.astype(np.float32)


def _inv_tab(n, Lout):
    t = np.arange(Lout, dtype=np.float64)[None, :]
    f = (np.arange(n // 2, dtype=np.float64) + 0.5)[:, None]
    ang = 2.0 * np.pi * f * t / n
    return ((2.0 / n) * np.concatenate([np.cos(ang), -np.sin(ang)], 0)).astype(np.float32)


def _x_tab(half):
    d = (np.arange(1024, dtype=np.float64) - 512.0)[:, None]
    f = (np.arange(512, dtype=np.float64) + 0.5)[None, :]
    ang = 2.0 * np.pi * d * f / 1024.0
    sign = -1.0 if half == 1 else 1.0
    t = np.concatenate([np.cos(ang), sign * np.sin(ang)], 1)
    t[0] = 0.0
    return t.astype(np.float32)


def _z_tab(L):
    t = np.linspace(0.0, 1.0, L, dtype=np.float32)[:, None]
    bands = np.linspace(1e-4, 15.0, 16, dtype=np.float32)[None]
    w = (2.0 * np.float32(math.pi) * np.arange(L, dtype=np.float32)[:, None] / np.float32(L)).astype(np.float32)
    z = np.concatenate([t, np.cos(bands * w), -np.sin(bands * w)], axis=-1).astype(np.float32)
    return np.ascontiguousarray(z.T)


def _decay_consts():
    mn = math.log(1e-2) / 1.5
    mx = math.log(1e-2) / 0.3
    deltas = np.linspace(mn, mx, 1024, dtype=np.float32)
    absd = np.abs(deltas).astype(np.float32)[None, :]
    negt = np.zeros((128, 10), np.float32)
    t256 = np.linspace(0.0, 1.0, 256, dtype=np.float32)
    t1k = np.linspace(0.0, 1.0, 1024, dtype=np.float32)
    for tc in range(2):
        negt[:, tc] = -t256[tc * 128:(tc + 1) * 128]
    for tc in range(8):
        negt[:, 2 + tc] = -t1k[tc * 128:(tc + 1) * 128]
    return absd, negt


_CONST_CACHE = {}


def _consts(half):
    if half in _CONST_CACHE:
        return _CONST_CACHE[half]
    c = {
        "fwd256": _fwd_tab(256, 512), "bwd256": _bwd_tab(256, 512), "inv256": _inv_tab(512, 256),
        "fwd512": _fwd_tab(512, 1024), "bwd512": _bwd_tab(512, 1024), "inv512": _inv_tab(1024, 512),
        "x1024": _x_tab(half),
        "zt256": _z_tab(256), "zt1024": _z_tab(1024),
        "absd": _decay_consts()[0], "negt": _decay_consts()[1],
        "ident": np.eye(128, dtype=np.float32),
    }
    import ml_dtypes
    for k in ("fwd256", "bwd256", "inv256", "fwd512", "bwd512", "inv512", "x1024"):
        c[k] = c[k].astype(ml_dtypes.bfloat16)
    _CONST_CACHE[half] = c
    return c


def build_program():
    nc = bass.Bass("TRN2", target_bir_lowering=False)

    BF_TABS = ("fwd256", "bwd256", "inv256", "fwd512", "bwd512", "inv512", "x1024")

    def din(name, shape):
        return nc.dram_tensor(name, list(shape), BF16 if name in BF_TABS else F32, kind="ExternalInput").ap()

    def dout(name, shape):
        return nc.dram_tensor(name, list(shape), F32, kind="ExternalOutput").ap()

    I = {}
    for name, shape in [
        ("xall", (NTOK, D)), ("params", (128, P_NP)), ("ck", (256, D)), ("cv", (256, D)),
        ("w_ada", (D, 3 * D)), ("w_in", (D, 10 * D)), ("w_bh", (D, D)), ("w_ba", (D, D)), ("w_out", (D, D)),
        ("fw1", (33, 64)), ("fw2", (64, 64)), ("w3c", (64, 3 * D)),
        ("rows", (3, D)),
        ("fwd256", (256, 512)), ("bwd256", (256, 512)), ("inv256", (512, 256)),
        ("fwd512", (512, 1024)), ("bwd512", (512, 1024)), ("inv512", (1024, 512)), ("x1024", (1024, 1024)),
        ("zt256", (33, 256)), ("zt1024", (33, 1024)), ("absd", (1, D)), ("negt", (128, 10)),
        ("ident", (128, 128)), ("btab", (16, 128, 24 * 64)), ("cmask", (128, 64)), ("rmask", (128, 48)),
    ]:
        I[name] = din(name, shape)
    O = {"y": dout("y", (NOWN, D)), "nk": dout("nk", (1024, D)), "nv": dout("nv", (1024, D))}

    with ExitStack() as es:
        b = Bld(nc, es)
        par, par_t = b.sb("par", [128, P_NP], F32)
        ident, ident_t = b.sb("ident", [128, 128], F32)
        identb, identb_t = b.sb("identb", [128, 128], BF16)
        b.dma("sp", par[:], I["params"][:, :], writes=[par_t])
        b.dma("sp", ident[:], I["ident"][:, :], writes=[ident_t])
        b.op("dve", lambda e: e.tensor_copy(out=identb[:], in_=ident[:]), reads=[ident_t], writes=[identb_t])

        esH = ExitStack()
        b.es = esH
        h256, h256_t = b.sb("h256", [128, 4, D], BF16, side="right")
        hoo, hoo_t = b.sb("hoo", [128, 8, D], BF16, side="right")
        hox, hox_t = b.sb("hox", [128, 8, D], BF16, side="right")
        dft = {}
        for name, rows, cols in (("fwd256", 256, 512), ("fwd512", 512, 1024), ("inv256", 512, 256), ("inv512", 1024, 512)):
            t, tk = b.sb("t_" + name, [128, rows // 128, cols], BF16, side="right")
            dft[name] = (t, tk)
        b.es = es

        b.init_wrings()
        esA = ExitStack()
        b.es = esA
        b.wbg = [b.sb("wbgA%d" % i, [128, 8, 512], BF16, side="right") for i in range(4)]
        b.es = es
        ada_pre = []
        phase_filters(b, I, par, par_t, h256, h256_t, hoo, hoo_t, hox, hox_t, dft, ada_pre)
        b.dump("h256", h256[:], h256_t, [128, 4, D], BF16)
        b.dump("hoo", hoo[:], hoo_t, [128, 8, D], BF16)
        b.dump("hox", hox[:], hox_t, [128, 8, D], BF16)
        if STOP_AFTER == "filters":
            b.finish()
            esH.close()
            return nc, b

        modsb, modsb_t = b.sb("modsb", [128, 24, 2], F32)
        hT, _ = b.sb("hT", [128, 8, NTOK], BF16)
        hT_t = Tok()
        yg, yg_t = b.sb("yg", [128, 8, NOWN], BF16)
        scb, scb_t = b.sb("scb", [128, 16], BF16)
        esX = ExitStack()
        b.es = esX
        xts = [b.sb("xt%d" % i, [128, D], F32) for i in range(8)]
        b.es = es
        for i in range(8):
            b.dma("sp", xts[i][0][:], I["xall"][i * 128:(i + 1) * 128, :], writes=[xts[i][1]])
        phase_mod(b, I, par, par_t, modsb, modsb_t, ada_pre, scb, scb_t)
        b.dump("mod", modsb[:], modsb_t, [128, 24, 2])
        pre_w = {(cc, sec): b.load_w(I["w_in"], sec * D + cc * 128, 128) for cc in range(2) for sec in range(2)}
        shared = {}
        phase_ht(b, I, ident, ident_t, modsb, modsb_t, hT, hT_t, xts, 8)
        esX.close()
        esA.close()
        b.dump("hT", hT[:], hT_t, [128, 8, NTOK], BF16)
        if STOP_AFTER == "ht":
            b.finish()
            esH.close()
            return nc, b
        try:
            phase_hyena(b, I, par, par_t, identb, identb_t, hT, hT_t, yg, yg_t, h256, h256_t, hoo, hoo_t, hox, hox_t, dft, pre_w, shared)
        except StopBuild:
            b.finish()
            esH.close()
            return nc, b
        b.dump("yg", yg[:], yg_t, [128, 8, NOWN], BF16)
        if STOP_AFTER == "hyena":
            b.finish()
            esH.close()
            return nc, b
        b.barrier()
        esH.close()
        ya, ya_t = b.sb("ya", [128, 8, NOWN], BF16)
        phase_attn(b, I, O, par, par_t, ident, ident_t, hT, hT_t, ya, ya_t, shared)
        b.dump("ya", ya[:], ya_t, [128, 8, NOWN], BF16)
        if STOP_AFTER == "attn":
            b.finish()
            return nc, b
        mg, mg_t = b.sb("mg", [128, 8, NOWN], BF16)
        b.wbi += 1000
        b.wbg = [b.sb("wbgF%d" % i, [128, 8, 512], BF16) for i in range(4)]
        pre = {"wa": [b.load_w(I["w_ada"], 2 * D + cb * 512, 512) for cb in range(2)],
               "wo": [b.load_w(I["w_out"], cb * 512, 512) for cb in range(2)]}
        pre["rows"] = b.sb("rows", [128, 2, D], F32)
        pre["brow"] = b.sb("brow", [128, D], F32)
        for r in range(2):
            b.dma("sp", pre["rows"][0][:, r, :], I["rows"][r + 1:r + 2, :].to_broadcast([128, D]), pw=[pre["rows"][1]])
        b.dma("sp", pre["brow"][0][:], I["rows"][0:1, :].to_broadcast([128, D]), writes=[pre["brow"][1]])
        phase_merge(b, I, hT, hT_t, yg, yg_t, ya, ya_t, mg, mg_t, shared)
        b.dump("mg", mg[:], mg_t, [128, 8, NOWN], BF16)
        phase_final(b, I, O, par, par_t, mg, mg_t, pre)

        b.finish()
    return nc, b


def phase_mod(b, I, par, par_t, modsb, modsb_t, ada_pre, scb, scb_t):
    with ExitStack() as es:
        old_es = b.es
        b.es = es
        b.op("act", lambda e: e.activation(out=scb[:], in_=par[:, P_COND:P_COND + 16], func=AF.Silu),
             reads=[par_t], writes=[scb_t])
        ps, ps_t = b.psum()
        for jb in range(4):
            w, w_t = ada_pre[jb]
            for c4 in range(4):
                ch = jb * 4 + c4
                for kc in range(8):
                    b.op("pe", lambda e: e.matmul(ps[:, 2 * ch:2 * ch + 2], lhsT=w[:, kc, c4 * 128:(c4 + 1) * 128],
                                                   rhs=scb[:, 2 * kc:2 * kc + 2], start=(kc == 0), stop=(kc == 7)),
                         reads=[w_t, scb_t], writes=[ps_t])
        b.op("dve", lambda e: e.tensor_tensor(
            out=modsb[:, 0:16, :], in0=ps[:, 0:32].rearrange("p (c w) -> p c w", w=2),
            in1=par[:, P_BADA:P_BADA + 16].unsqueeze(2).to_broadcast([128, 16, 2]), op=ALU.add),
            reads=[ps_t, par_t], writes=[modsb_t])
        b.op("dve", lambda e: e.tensor_scalar(out=modsb[:, 8:16, :], in0=modsb[:, 8:16, :], scalar1=1.0, scalar2=None,
                                               op0=ALU.add), reads=[modsb_t], writes=[modsb_t])
        b.es = old_es


def phase_ht(b, I, ident, ident_t, modsb, modsb_t, hT, hT_t, xts, npre):
    with ExitStack() as es:
        old_es = b.es
        b.es = es
        xi = 0
        k = 0
        groups = [(0, 4, 0), (512, 4, 0), (1024, 4, 1), (1536, 4, 1), (2048, 2, 1)]
        for (col0, ntile, w) in groups:
            tiles = []
            for t in range(ntile):
                xt, xt_t = xts[xi % 8]
                xi += 1
                r0 = col0 + t * 128
                if xi > npre:
                    b.dma("sp", xt[:], I["xall"][r0:r0 + 128, :], writes=[xt_t])
                tiles.append((xt, xt_t))
            for fc in range(8):
                ps, ps_t = b.psum()
                for t, (xt, xt_t) in enumerate(tiles):
                    b.op("pe", lambda e: e.transpose(ps[:, t * 128:(t + 1) * 128], xt[:, fc * 128:(fc + 1) * 128], ident[:]),
                         reads=[xt_t, ident_t], writes=[ps_t])
                n = ntile * 128
                k += 1
                if k % 2 == 0:
                    b.op("act", lambda e: e.activation(out=hT[:, fc, col0:col0 + n], in_=ps[:, 0:n], func=AF.Identity,
                                                        bias=modsb[:, fc, w:w + 1], scale=modsb[:, 8 + fc, w:w + 1]),
                         reads=[ps_t, modsb_t], pw=[hT_t])
                else:
                    b.op("dve", lambda e: e.tensor_scalar(out=hT[:, fc, col0:col0 + n], in0=ps[:, 0:n],
                                                           scalar1=modsb[:, 8 + fc, w:w + 1], scalar2=modsb[:, fc, w:w + 1],
                                                           op0=ALU.mult, op1=ALU.add),
                         reads=[ps_t, modsb_t], pw=[hT_t])
        b.barrier()
        b.es = old_es


def phase_hyena(b, I, par, par_t, identb, identb_t, hT, hT_t, yg, yg_t, h256, h256_t, hoo, hoo_t, hox, hox_t, dft, pre_w, shared):
    with ExitStack() as es:
        old_es = b.es
        b.es = es

        fwd256, fwd256_t = dft["fwd256"]
        inv256, inv256_t = dft["inv256"]
        fwd512, fwd512_t = dft["fwd512"]
        inv512, inv512_t = dft["inv512"]

        cwf, cwf_t = b.sb("cwf", [128, 4, 24], F32)
        for i, (j, fl) in enumerate(((0, 0), (2, 0), (0, 1), (2, 1))):
            b.op("dve", lambda e: e.tensor_scalar(out=cwf[:, i, :], in0=par[:, P_CW + j * 24:P_CW + (j + 1) * 24],
                                                   scalar1=par[:, P_FLAG + fl:P_FLAG + fl + 1], scalar2=None, op0=ALU.mult),
                 reads=[par_t], pw=[cwf_t])

        def cw(j, ci):
            return par[:, P_CW + j * 24 + ci:P_CW + j * 24 + ci + 1]

        cA, _ = b.sb("cA", [128, 2048], F32)
        cB, _ = b.sb("cB", [128, 2048], F32)
        cA_t = [Tok() for _ in range(4)]
        cB_t = [Tok() for _ in range(4)]
        edgs = [b.sb("edges%d" % i, [128, 4], F32) for i in range(2)]
        sgs = [b.sb("sg%d" % i, [128, 512], F32) for i in range(2)]
        tmps = [b.sb("sp%d" % i, [128, 256], F32) for i in range(10)]
        tmpi = [0]

        def tmp(eng="dve"):
            tmpi[0] += 1
            ring = tmps if eng == "dve" else ptmps
            return ring[tmpi[0] % len(ring)]
        utm, utm_t = b.sb("utm", [128, 16, 256], BF16)
        ufm, ufm_t = b.sb("ufm", [128, 2, 2048], BF16)
        xg, xg_t = b.sb("xg", [128, 2, NOWN], BF16)
        Y4s = [b.sb("Y4_%d" % i, [128, 4, 256], BF16) for i in range(2)]
        Y8, Y8_t = b.sb("Y8", [128, 8, 256], BF16)
        eps = [b.sb("ep%d" % i, [128, 512], F32) for i in range(4)]
        epi = [0]
        ptmps = []
        pend_is = [None]
        pend_T = []

        def conv_block(ps, ps_t, acc, acc_t, col0, ci, nseg):
            seglen = 512 // nseg
            b.op("act", lambda e: e.activation(out=acc[:, col0:col0 + 512], in_=ps[:, :], func=AF.Identity,
                                                bias=par[:, P_CB + ci:P_CB + ci + 1], scale=cw(1, ci)),
                 reads=[ps_t, par_t], writes=[acc_t])
            av = acc[:, col0:col0 + 512].rearrange("p (s l) -> p s l", s=nseg)
            pv = ps[:, :].rearrange("p (s l) -> p s l", s=nseg)
            b.op("dve", lambda e: e.scalar_tensor_tensor(out=av[:, :, 1:seglen], in0=pv[:, :, 0:seglen - 1], scalar=cw(0, ci),
                                                          in1=av[:, :, 1:seglen], op0=ALU.mult, op1=ALU.add),
                 reads=[ps_t, par_t, acc_t], writes=[acc_t])
            b.op("dve", lambda e: e.scalar_tensor_tensor(out=av[:, :, 0:seglen - 1], in0=pv[:, :, 1:seglen], scalar=cw(2, ci),
                                                          in1=av[:, :, 0:seglen - 1], op0=ALU.mult, op1=ALU.add),
                 reads=[ps_t, par_t, acc_t], writes=[acc_t])

        def fix(acc, acc_t, col, wi, ci, src, src_t):
            b.op("dve", lambda e: e.scalar_tensor_tensor(out=acc[:, col:col + 1], in0=src, scalar=cwf[:, wi, ci:ci + 1],
                                                          in1=acc[:, col:col + 1], op0=ALU.mult, op1=ALU.add),
                 reads=[src_t, cwf_t, acc_t], writes=[acc_t])

        def proj(w, w_t, col0, n=512):
            ps, ps_t = b.psum()
            for kc in range(8):
                b.op("pe", lambda e: e.matmul(ps[:, 0:n], lhsT=w[:, kc, 0:128], rhs=hT[:, kc, col0:col0 + n],
                                               start=(kc == 0), stop=(kc == 7)),
                     reads=[w_t, hT_t], writes=[ps_t])
            return ps, ps_t

        for g in range(4):
            ch0 = g * 256
            for cc in range(2):
                chunk = g * 2 + cc
                ws = []
                for sec in range(2):
                    if g == 0:
                        ws.append(pre_w[(cc, sec)])
                    elif cc == 0:
                        ws.append(shared["ws_next"][sec])
                    else:
                        ws.append(b.load_w(I["w_in"], sec * D + chunk * 128, 128))

                def P(sec, tb, cc=cc, chunk=chunk, ws=ws):
                    w, w_t = ws[sec]
                    acc, acc_t = (cA, cA_t) if sec == 0 else (cB, cB_t)
                    ed, ed_t = edgs[sec]
                    ci = sec * 8 + chunk
                    ps, ps_t = proj(w, w_t, tb * 512)
                    conv_block(ps, ps_t, acc, acc_t[tb], tb * 512, ci, 2 if tb < 2 else 1)
                    if tb >= 2:
                        b.op("dve", lambda e: e.tensor_copy(out=ed[:, 2 * (tb - 2):2 * (tb - 2) + 2], in_=ps[:, 0:512:511]),
                             reads=[ps_t], writes=[ed_t] if tb == 2 else [], pw=[] if tb == 2 else [ed_t])

                def U(tb, cc=cc):
                    first = (cc == 0 and tb == 0)
                    sl = slice(tb * 512, (tb + 1) * 512)
                    b.op("dve", lambda e: e.tensor_tensor(out=ufm[:, cc, sl], in0=cA[:, sl], in1=cB[:, sl], op=ALU.mult),
                         reads=[cA_t[tb], cB_t[tb]], writes=[ufm_t] if first else [], pw=[] if first else [ufm_t])

                def T(tb, cc=cc):
                    first = (cc == 0 and tb == 0)
                    ps, ps_t = b.psum()
                    psb = ps[:, :].bitcast(BF16)
                    for j in range(4):
                        tt = tb * 4 + j
                        b.op("pe", lambda e: e.transpose(psb[:, j * 128:(j + 1) * 128], ufm[:, cc, tt * 128:(tt + 1) * 128],
                                                          identb[:]),
                             reads=[ufm_t, identb_t], writes=[ps_t])
                    b.op("act", lambda e: e.copy(out=utm[:, tb * 4:tb * 4 + 4, cc * 128:(cc + 1) * 128],
                                                  in_=psb[:, 0:512].rearrange("p (j c) -> p j c", j=4)),
                         reads=[ps_t], writes=[utm_t] if first else [], pw=[] if first else [utm_t])

                P(0, 0); P(1, 0); U(0)
                P(0, 1); P(1, 1); U(1)
                for tfn in pend_T:
                    tfn()
                pend_T.clear()
                P(0, 2); P(1, 2); T(0)
                P(0, 3); P(1, 3); T(1)
                for sec in range(2):
                    acc, acc_t = (cA, cA_t) if sec == 0 else (cB, cB_t)
                    ed, ed_t = edgs[sec]
                    ci = sec * 8 + chunk
                    fix(acc, acc_t[2], 1024, 0, ci, ed[:, 3:4], ed_t)
                    fix(acc, acc_t[2], 1535, 3, ci, ed[:, 2:3], ed_t)
                    fix(acc, acc_t[3], 1536, 2, ci, ed[:, 1:2], ed_t)
                    fix(acc, acc_t[3], 2047, 1, ci, ed[:, 0:1], ed_t)
                U(2); U(3)
                pend_T.extend([lambda T=T: T(2), lambda T=T: T(3)])
            if HY_STOP == 4:
                b.barrier(); b.es = old_es; return
            def X(tb, g=g, ccs=(0, 1)):
                for cc in ccs:
                    chunk = g * 2 + cc
                    w, w_t = xw[(cc, 2)]
                    ci = 16 + chunk
                    acc, acc_t = (cA, cA_t) if cc == 0 else (cB, cB_t)
                    ps, ps_t = proj(w, w_t, tb * 512)
                    conv_block(ps, ps_t, acc, acc_t[tb], tb * 512, ci, 2 if tb < 2 else 1)
                    if tb == 2:
                        pse, pse_t = b.psum()
                        for kc in range(8):
                            b.op("pe", lambda e: e.matmul(pse[:, 0:2], lhsT=w[:, kc, 0:128], rhs=hT[:, kc, OTH0:OTH0 + 512:511],
                                                           start=(kc == 0), stop=(kc == 7)),
                                 reads=[w_t, hT_t], writes=[pse_t])
                        fix(acc, acc_t[2], 1024, 0, ci, pse[:, 1:2], pse_t)
                        fix(acc, acc_t[2], 1535, 3, ci, pse[:, 0:1], pse_t)
                    w, w_t = xw[(cc, 3)]
                    ps, ps_t = proj(w, w_t, tb * 512)
                    sg, sg_t = sgs[cc]
                    b.op("act", lambda e: e.activation(out=sg[:], in_=ps[:, :], func=AF.Silu), reads=[ps_t], writes=[sg_t])
                    b.op("pool", lambda e: e.tensor_tensor(out=xg[:, cc, tb * 512:(tb + 1) * 512], in0=acc[:, tb * 512:(tb + 1) * 512],
                                                            in1=sg[:], op=ALU.mult),
                         reads=[acc_t[tb], sg_t], writes=[xg_t] if (cc == 0 and tb == 0) else [],
                         pw=[] if (cc == 0 and tb == 0) else [xg_t])

            xw = {}
            for cc in range(2):
                for sec in (2, 3):
                    xw[(cc, sec)] = b.load_w(I["w_in"], sec * D + (g * 2 + cc) * 128, 128)
            if g < 3:
                shared["ws_next"] = [b.load_w(I["w_in"], sec * D + ((g + 1) * 2) * 128, 128) for sec in range(2)]
            if g == 3 and HY_STOP == 0:
                shared["W0"] = [b.load_w(I["w_in"], (4 + i) * D, 128) for i in range(4)]
            def spectral(Ups, Hre, Him, H_t, Yre, Yim, Y_t, first, accum, eng="dve", ceng="pool"):
                terms_re = []
                terms_im = []
                for (ps, ps_t, hre, him, h_t) in accum:
                    ure, uim = ps[:, 0:256], ps[:, 256:512]
                    terms_re += [(ure, hre, ps_t, h_t, 1.0), (uim, him, ps_t, h_t, -1.0)]
                    terms_im += [(ure, him, ps_t, h_t, 1.0), (uim, hre, ps_t, h_t, 1.0)]
                for terms, Yo in ((terms_re, Yre), (terms_im, Yim)):
                    acc = None
                    for i, (u, h, u_t, h_t, sgn) in enumerate(terms):
                        t, t_t = tmp(eng)
                        b.op(eng, lambda e: e.tensor_tensor(out=t[:], in0=u, in1=h, op=ALU.mult),
                             reads=[u_t, h_t], writes=[t_t])
                        if acc is None:
                            acc = (t, t_t)
                            continue
                        last = (i == len(terms) - 1)
                        op = ALU.add if sgn > 0 else ALU.subtract
                        if last:
                            b.op(ceng, lambda e: e.tensor_tensor(out=Yo, in0=acc[0][:], in1=t[:], op=op),
                                 reads=[acc[1], t_t], pw=[Y_t])
                        else:
                            n, n_t = tmp(eng)
                            b.op(ceng, lambda e: e.tensor_tensor(out=n[:], in0=acc[0][:], in1=t[:], op=op),
                                 reads=[acc[1], t_t], writes=[n_t])
                            acc = (n, n_t)

            def fwd_dft(tab, tab_t, nk, tc0, ci_re, ci_im):
                ps, ps_t = b.psum()
                for part, ci in ((0, ci_re), (1, ci_im)):
                    for kc in range(nk):
                        b.op("pe", lambda e: e.matmul(ps[:, part * 256:(part + 1) * 256], lhsT=tab[:, kc, ci * 128:(ci + 1) * 128],
                                                       rhs=utm[:, tc0 + kc, :], start=(kc == 0), stop=(kc == nk - 1)),
                             reads=[tab_t, utm_t], writes=[ps_t])
                return ps, ps_t

            def epilogue(ps, ps_t, off, cc, tok0, n, g=g):
                chunk = g * 2 + cc
                epi[0] += 1
                ep, ep_t = eps[epi[0] % 4]
                b.op("dve", lambda e: e.scalar_tensor_tensor(out=ep[:, 0:n], in0=ufm[:, cc, tok0:tok0 + n],
                                                              scalar=par[:, P_HD + chunk:P_HD + chunk + 1],
                                                              in1=ps[:, off:off + n], op0=ALU.mult, op1=ALU.add),
                     reads=[ufm_t, par_t, ps_t], writes=[ep_t])
                b.op("pool", lambda e: e.tensor_tensor(out=yg[:, chunk, tok0:tok0 + n], in0=ep[:, 0:n],
                                                        in1=xg[:, cc, tok0:tok0 + n], op=ALU.mult),
                     reads=[ep_t, xg_t], pw=[yg_t])

            def F(bb):
                Y4, Y4_t = Y4s[bb % 2]
                for ip in range(2):
                    ps, ps_t = fwd_dft(fwd256, fwd256_t, 2, 2 * bb, ip, ip + 2)
                    spectral(None, None, None, None, Y4[:, ip, :], Y4[:, ip + 2, :], Y4_t, True,
                             [(ps, ps_t, h256[:, ip, ch0:ch0 + 256], h256[:, ip + 2, ch0:ch0 + 256], h256_t)])

            def Iv(bb):
                Y4, Y4_t = Y4s[bb % 2]
                ps, ps_t = b.psum()
                for cc in range(2):
                    for ci in range(4):
                        b.op("pe", lambda e: e.matmul(ps[:, cc * 256:(cc + 1) * 256], lhsT=Y4[:, ci, cc * 128:(cc + 1) * 128],
                                                       rhs=inv256[:, ci, :], start=(ci == 0), stop=(ci == 3)),
                             reads=[Y4_t, inv256_t], writes=[ps_t])
                for cc in range(2):
                    epilogue(ps, ps_t, cc * 256, cc, bb * 256, 256)

            def FS(ip):
                ps1, ps1_t = fwd_dft(fwd512, fwd512_t, 4, 8, ip, ip + 4)
                ps2, ps2_t = fwd_dft(fwd512, fwd512_t, 4, 12, ip, ip + 4)
                spectral(None, None, None, None, Y8[:, ip, :], Y8[:, ip + 4, :], Y8_t, True,
                         [(ps1, ps1_t, hoo[:, ip, ch0:ch0 + 256], hoo[:, ip + 4, ch0:ch0 + 256], hoo_t),
                          (ps2, ps2_t, hox[:, ip, ch0:ch0 + 256], hox[:, ip + 4, ch0:ch0 + 256], hox_t)])

            def IS(epilogue=epilogue):
                for cc in range(2):
                    ps, ps_t = b.psum()
                    for ci in range(8):
                        b.op("pe", lambda e: e.matmul(ps[:, :], lhsT=Y8[:, ci, cc * 128:(cc + 1) * 128], rhs=inv512[:, ci, :],
                                                       start=(ci == 0), stop=(ci == 7)),
                             reads=[Y8_t, inv512_t], writes=[ps_t])
                    epilogue(ps, ps_t, 0, cc, 1024, 512)

            X(0)
            for tfn in pend_T:
                tfn()
            pend_T.clear()
            F(0); F(1); Iv(0); F(2); Iv(1); X(1); F(3); Iv(2); FS(0); Iv(3); FS(1); X(2, ccs=(0,)); FS(2); X(2, ccs=(1,)); FS(3); IS()
            if HY_STOP == 8:
                b.barrier(); b.es = old_es; return
        b.barrier()
        b.es = old_es


def interleave_gen(*gens):
    gens = list(gens)
    while gens:
        for g in list(gens):
            try:
                next(g)
            except StopIteration:
                gens.remove(g)
        yield


def interleave(*gens):
    gens = list(gens)
    while gens:
        for g in list(gens):
            try:
                next(g)
            except StopIteration:
                gens.remove(g)


def rr(lst, state=[0]):
    state[0] += 1
    return lst[state[0] % len(lst)]


def phase_filters(b, I, par, par_t, h256, h256_t, hoo, hoo_t, hox, hox_t, dft, ada_pre):
    nc = b.nc
    with ExitStack() as es:
        old_es = b.es
        b.es = es
        fw1, fw1_t = b.sb("fw1", [33, 64], F32)
        fw2, fw2_t = b.sb("fw2", [64, 64], F32)
        w3, w3_t = b.sb("w3", [64, 3 * D], BF16)
        fb, fb_t = b.sb("fb", [64, 4], F32)
        b.dma("sp", fw1[:], I["fw1"][:, :], writes=[fw1_t])
        b.dma("sp", fw2[:], I["fw2"][:, :], writes=[fw2_t])
        b.dma("pool", w3[:], I["w3c"][:, :], writes=[w3_t])
        zts = {}
        for L in (256, 1024):
            zts[L] = b.sb("zt%d" % L, [33, L], F32)
            b.dma("sp", zts[L][0][:], I["zt%d" % L][:, :], writes=[zts[L][1]])
        absd, absd_t = b.sb("absd", [128, D], F32)
        b.dma("sp", absd[:], I["absd"][0:1, :].to_broadcast([128, D]), writes=[absd_t])
        negt, negt_t = b.sb("negt", [128, 10], F32)
        b.dma("sp", negt[:], I["negt"][:, :], writes=[negt_t])
        for name in ("fwd256",):
            b.dma("sp", dft[name][0][:], I[name].rearrange("(c p) n -> p c n", p=128), writes=[dft[name][1]])
        fpar = par[0:64, P_FPAR:P_FPAR + 4]
        for l in range(2):
            b.op("dve", lambda e, l=l: e.tensor_scalar(out=fb[:, l:l + 1], in0=fpar[:, l:l + 1],
                                                        scalar1=fpar[:, 2 + l:3 + l], scalar2=None,
                                                        op0=ALU.mult),
                 reads=[par_t], writes=[fb_t])

        def load_tab(name, rows, cols):
            t, tk = b.sb(name, [128, rows // 128, cols], BF16)
            b.dma("sp", t[:], I[name].rearrange("(c p) n -> p c n", p=128), writes=[tk])
            return t, tk
        fwd256, fwd256_t = dft["fwd256"]
        fwd512, fwd512_t = dft["fwd512"]
        bwd256, bwd256_t = load_tab("bwd256", 256, 512)
        b.dma("sp", dft["fwd512"][0][:], I["fwd512"].rearrange("(c p) n -> p c n", p=128), writes=[dft["fwd512"][1]])
        bwd512, bwd512_t = load_tab("bwd512", 512, 1024)
        x1024, x1024_t = load_tab("x1024", 1024, 1024)
        for jb in range(4):
            ada_pre.append(b.load_w(I["w_ada"], jb * 512, 512))
        for name in ("inv256", "inv512"):
            b.dma("sp", dft[name][0][:], I[name].rearrange("(c p) n -> p c n", p=128), writes=[dft[name][1]])

        for L, ztn, decn in ((256, "zt256", "dec256"), (1024, "zt1024", "dec1024")):
            with ExitStack() as es2:
                b.es = es2
                zt, zt_t = zts[L]
                hid = []
                for l in range(2):
                    hid.append(b.sb("hid%d_%d" % (l, L), [64, L], F32))
                hidb, hidb_t = b.sb("hidb%d" % L, [64, L], BF16)
                tmp, tmp_t = b.sb("ftmp%d" % L, [64, 512], F32)
                tmpi, tmpi_t = b.sb("ftmpi%d" % L, [64, 512], mybir.dt.int32)
                tmpf, tmpf_t = b.sb("ftmpf%d" % L, [64, 512], F32)
                nblk = max(1, L // 512)
                bw = min(L, 512)
                for l in range(2):
                    src, src_t = (zt, zt_t) if l == 0 else hid[0]
                    wl, wl_t = (fw1, fw1_t) if l == 0 else (fw2, fw2_t)
                    K = 33 if l == 0 else 64
                    dst, dst_t = hid[l]
                    for blk in range(nblk):
                        ps, ps_t = b.psum()
                        sl = slice(blk * bw, (blk + 1) * bw)
                        b.op("pe", lambda e: e.matmul(ps[0:64, 0:bw], lhsT=wl[0:K, :], rhs=src[0:K, sl],
                                                       start=True, stop=True),
                             reads=[wl_t, src_t], writes=[ps_t])
                        b.op("dve", lambda e: e.tensor_scalar(out=tmp[:, 0:bw], in0=ps[0:64, 0:bw],
                                                               scalar1=fpar[:, 2 + l:3 + l], scalar2=fb[:, l:l + 1],
                                                               op0=ALU.mult, op1=ALU.add),
                             reads=[ps_t, par_t, fb_t], writes=[tmp_t])
                        b.op("dve", lambda e: e.tensor_scalar(out=tmpi[:, 0:bw], in0=tmp[:, 0:bw],
                                                               scalar1=1.0 / TWO_PI, scalar2=None, op0=ALU.mult),
                             reads=[tmp_t], writes=[tmpi_t])
                        b.op("dve", lambda e: e.tensor_copy(out=tmpf[:, 0:bw], in_=tmpi[:, 0:bw]),
                             reads=[tmpi_t], writes=[tmpf_t])
                        b.op("dve", lambda e: e.scalar_tensor_tensor(out=tmp[:, 0:bw], in0=tmpf[:, 0:bw], scalar=-TWO_PI,
                                                                      in1=tmp[:, 0:bw], op0=ALU.mult, op1=ALU.add),
                             reads=[tmpf_t, tmp_t], writes=[tmp_t])
                        b.op("act", lambda e: e.activation(out=dst[:, sl], in_=tmp[:, 0:bw], func=AF.Sin),
                             reads=[tmp_t], writes=[dst_t])
                b.op("dve", lambda e: e.tensor_copy(out=hidb[:], in_=hid[1][0][:]), reads=[hid[1][1]], writes=[hidb_t])
                b.dump("hid%d" % L, hid[1][0][:], hid[1][1], [64, L])

                ntc = L // 128
                ntc_ab = ntc if L == 256 else 4
                taps, taps_t = b.sb("taps%d" % L, [128, ntc_ab, 2 * D], BF16)
                if L == 1024:
                    tapsx, tapsx_t = b.sb("tapsx%d" % L, [128, ntc, D], BF16)
                decs = [b.sb("dec%d_%d" % (L, i), [128, D], F32) for i in range(2)]
                for tc in range(ntc):
                    dec, dec_t = decs[tc % 2]
                    tcol = tc if L == 256 else 2 + tc
                    b.op("act", lambda e: e.activation(out=dec[:], in_=absd[:], func=AF.Exp, scale=negt[:, tcol:tcol + 1]),
                         reads=[absd_t, negt_t], writes=[dec_t])
                    if L == 256:
                        colblks = [0, 1, 2, 3]
                    else:
                        colblks = ([0, 1, 2, 3] if tc < 4 else []) + [4, 5]
                    for cb in colblks:
                        ps, ps_t = b.psum()
                        b.op("pe", lambda e: e.matmul(ps[:, :], lhsT=hidb[:, tc * 128:(tc + 1) * 128],
                                                       rhs=w3[:, cb * 512:(cb + 1) * 512], start=True, stop=True),
                             reads=[hidb_t, w3_t], writes=[ps_t])
                        dc = (cb % 2) * 512
                        if cb < 4:
                            dst, dst_t = taps[:, tc, cb * 512:(cb + 1) * 512], taps_t
                        else:
                            dst, dst_t = tapsx[:, tc, (cb - 4) * 512:(cb - 3) * 512], tapsx_t
                        b.op("dve", lambda e: e.scalar_tensor_tensor(out=dst, in0=dec[:, dc:dc + 512],
                                                                      scalar=0.05, in1=ps[:, :], op0=ALU.add, op1=ALU.mult),
                             reads=[ps_t, dec_t], writes=[dst_t])

                evac = ["act", "dve"]
                if L == 256:
                    jobs = [(h256, h256_t, 4, [(fwd256, fwd256_t, 0, 2), (bwd256, bwd256_t, 1, 2)])]
                else:
                    jobs = [(hoo, hoo_t, 8, [(fwd512, fwd512_t, 0, 4), (bwd512, bwd512_t, 1, 4)]),
                            (hox, hox_t, 8, [(x1024, x1024_t, 2, 8)])]
                k = 0
                for (H, H_t, ncoef, srcs) in jobs:
                    for ci in range(ncoef):
                        for cb in range(2):
                            ps, ps_t = b.psum()
                            mm = []
                            for (tab, tab_t, sec, ntcs) in srcs:
                                for tc in range(ntcs):
                                    mm.append((tab, tab_t, sec, tc))
                            for i, (tab, tab_t, sec, tc) in enumerate(mm):
                                b.op("pe", lambda e: e.matmul(
                                    ps[:, :], lhsT=tab[:, tc, ci * 128:(ci + 1) * 128],
                                    rhs=(taps[:, tc, sec * D + cb * 512: sec * D + (cb + 1) * 512] if sec < 2
                                         else tapsx[:, tc, cb * 512:(cb + 1) * 512]),
                                    start=(i == 0), stop=(i == len(mm) - 1)),
                                    reads=[tab_t, taps_t if sec < 2 else tapsx_t], writes=[ps_t])
                            k += 1
                            if k % 2 == 0:
                                b.op("act", lambda e: e.copy(out=H[:, ci, cb * 512:(cb + 1) * 512], in_=ps[:, :]),
                                     reads=[ps_t], writes=[H_t])
                            else:
                                b.op("dve", lambda e: e.tensor_copy(out=H[:, ci, cb * 512:(cb + 1) * 512], in_=ps[:, :]),
                                     reads=[ps_t], writes=[H_t])
                b.barrier(dma=False)
            b.es = es
        b.barrier(dma=False)
        b.es = old_es


def phase_attn(b, I, O, par, par_t, ident, ident_t, hT, hT_t, ya, ya_t, shared):
    with ExitStack() as es:
        old_es = b.es
        b.es = es
        cmask, cmask_t = b.sb("cmask", [128, 64], F32)
        rmask, rmask_t = b.sb("rmask", [128, 6, 8], BF16)
        b.dma("sp", cmask[:], I["cmask"][:, :], writes=[cmask_t])
        b.dma("pool", rmask[:], I["rmask"].rearrange("p (j r) -> p j r", j=6), writes=[rmask_t])
        vexts = [b.sb("vext%d" % i, [128, 16, 2, 128], BF16) for i in range(2)]
        for (v, v_t) in vexts:
            b.op("pool", lambda e: e.memset(v[:], 1.0), writes=[v_t])
        qTs = [b.sb("qT%d" % i, [128, NOWN], BF16) for i in range(2)]
        kTs = [b.sb("kT%d" % i, [128, 2048], BF16) for i in range(2)]
        kf, kf_t = b.sb("kf", [128, 1024], F32)
        ckst, ckst_t = b.sb("ckst", [128, 2, 128], F32)
        nkst, nkst_t = b.sb("nkst", [128, 8, 128], F32)
        nvst, nvst_t = b.sb("nvst", [128, 8, 128], F32)
        sgas = [b.sb("sga%d" % i, [128, NOWN], BF16) for i in range(2)]
        vsts = [b.sb("vst%d" % i, [128, 4, 128], F32) for i in range(2)]
        bts = [b.sb("bt%d" % i, [128, 24, 64], F32) for i in range(2)]
        ptss = [[b.sb("pt%d_%d" % (s2, i), [128, 512], BF16) for i in range(9)] for s2 in range(2)]
        Ts = [b.sb("T%d" % i, [128, 512], F32) for i in range(2)]
        rcs = [b.sb("rc%d" % i, [128, 512], F32) for i in range(4)]
        pti = [0, 0]

        def pt(s2):
            pti[s2] += 1
            return ptss[s2][pti[s2] % 9]

        def proj(w, w_t, col0, n=512):
            ps, ps_t = b.psum()
            for kc in range(8):
                b.op("pe", lambda e: e.matmul(ps[:, 0:n], lhsT=w[:, kc, 0:128], rhs=hT[:, kc, col0:col0 + n],
                                               start=(kc == 0), stop=(kc == 7)),
                     reads=[w_t, hT_t], writes=[ps_t])
            return ps, ps_t

        fin = [0]

        def finalize(pso, pso_t, e, hp, tok0, n, sga, sga_t):
            ob = 64 * e
            db = 64 * (1 - e)
            rc, rc_t = rcs[fin[0] % len(rcs)]
            fin[0] += 1
            b.op("dve", lambda en: en.tensor_copy(out=rc[ob:ob + 64, 0:n], in_=pso[db:db + 64, 0:n]),
                 reads=[pso_t], writes=[rc_t])
            b.op("act", lambda en: en.activation(out=rc[ob:ob + 64, 0:n], in_=rc[ob:ob + 64, 0:n], func=AF.Ln),
                 reads=[rc_t], writes=[rc_t])
            b.op("act", lambda en: en.activation(out=rc[ob:ob + 64, 0:n], in_=rc[ob:ob + 64, 0:n], func=AF.Exp, scale=-1.0),
                 reads=[rc_t], writes=[rc_t])
            b.op("dve", lambda en: en.tensor_tensor(out=rc[ob:ob + 64, 0:n], in0=pso[ob:ob + 64, 0:n],
                                                     in1=rc[ob:ob + 64, 0:n], op=ALU.mult),
                 reads=[pso_t, rc_t], writes=[rc_t])
            b.op("pool", lambda en: en.tensor_tensor(out=ya[ob:ob + 64, hp, tok0:tok0 + n], in0=rc[ob:ob + 64, 0:n],
                                                      in1=sga[ob:ob + 64, tok0:tok0 + n], op=ALU.mult),
                 reads=[rc_t, sga_t], pw=[ya_t])

        def load_pair(hp):
            return [b.load_w(I["w_in"], (4 + i) * D + hp * 128, 128) for i in range(4)]

        def bufs(hp):
            return qTs[hp % 2] + kTs[hp % 2] + sgas[hp % 2] + vexts[hp % 2]

        def proj_gen(hp, W):
            qT, qT_t, kT, kT_t, sga, sga_t, vext, vext_t = bufs(hp)
            (wq, wq_t), (wk, wk_t), (wv, wv_t), (wg, wg_t) = W
            for tb in range(3):
                ps, ps_t = proj(wq, wq_t, tb * 512)
                b.op("act", lambda e: e.activation(out=qT[:, tb * 512:(tb + 1) * 512], in_=ps[:, :], func=AF.Copy, scale=0.125),
                     reads=[ps_t], writes=[qT_t] if tb == 0 else [], pw=[] if tb == 0 else [qT_t])
                yield
            for i, (src0, n, dst0) in enumerate(((0, 512, 0), (512, 512, 512), (1024, 512, 1024), (2048, 256, 1536))):
                ps, ps_t = proj(wk, wk_t, src0, n)
                if i < 2:
                    b.op("act", lambda e: e.copy(out=kf[:, dst0:dst0 + n], in_=ps[:, 0:n]),
                         reads=[ps_t], writes=[kf_t] if i == 0 else [], pw=[] if i == 0 else [kf_t])
                    b.op("pool", lambda e: e.tensor_copy(out=kT[:, dst0:dst0 + n], in_=kf[:, dst0:dst0 + n]),
                         reads=[kf_t], writes=[kT_t] if i == 0 else [], pw=[] if i == 0 else [kT_t])
                else:
                    b.op("dve", lambda e: e.tensor_copy(out=kT[:, dst0:dst0 + n], in_=ps[:, 0:n]),
                         reads=[ps_t], pw=[kT_t])
                yield
            b.dma("sp", ckst[:], I["ck"][:, hp * 128:(hp + 1) * 128].rearrange("(c p) n -> p c n", p=128), writes=[ckst_t])
            ps, ps_t = b.psum()
            for c in range(2):
                b.op("pe", lambda e: e.transpose(ps[:, c * 128:(c + 1) * 128], ckst[:, c, :], ident[:]),
                     reads=[ckst_t, ident_t], writes=[ps_t])
            b.op("dve", lambda e: e.tensor_copy(out=kT[:, 1792:2048], in_=ps[:, 0:256]), reads=[ps_t], pw=[kT_t])
            for half2 in range(2):
                ps, ps_t = b.psum()
                for j in range(4):
                    tt = half2 * 4 + j
                    b.op("pe", lambda e: e.transpose(ps[:, j * 128:(j + 1) * 128], kf[:, tt * 128:(tt + 1) * 128], ident[:]),
                         reads=[kf_t, ident_t], writes=[ps_t])
                b.op("act", lambda e: e.copy(out=nkst[:, half2 * 4:half2 * 4 + 4, :],
                                              in_=ps[:, :].rearrange("p (j c) -> p j c", j=4)),
                     reads=[ps_t], writes=[nkst_t] if half2 == 0 else [], pw=[] if half2 == 0 else [nkst_t])
            b.dma("sp", O["nk"][:, hp * 128:(hp + 1) * 128].rearrange("(t p) c -> p t c", p=128), nkst[:],
                  reads=[nkst_t], is_output=True)
            cols = [t * 128 for t in range(12)] + [2048, 2176]
            for q4 in range(4):
                tl = cols[q4 * 4:q4 * 4 + 4]
                ps, ps_t = b.psum()
                for j, col in enumerate(tl):
                    for kc in range(8):
                        b.op("pe", lambda e: e.matmul(ps[:, j * 128:(j + 1) * 128], lhsT=hT[:, kc, col:col + 128],
                                                       rhs=wv[:, kc, 0:128], start=(kc == 0), stop=(kc == 7)),
                             reads=[hT_t, wv_t], writes=[ps_t])
                nj = len(tl)
                pv = ps[:, 0:nj * 128].rearrange("p (j c) -> p j c", j=nj)
                first = (q4 == 0)
                if q4 < 2:
                    st, st_t = nvst[:, q4 * 4:q4 * 4 + 4, :], nvst_t
                    b.op("act", lambda e: e.copy(out=st, in_=pv), reads=[ps_t],
                         writes=[nvst_t] if first else [], pw=[] if first else [nvst_t])
                else:
                    vs, st_t = vsts[q4 % 2]
                    st = vs[:, 0:nj, :]
                    b.op("act", lambda e: e.copy(out=st, in_=pv), reads=[ps_t], writes=[st_t])
                b.op("dve", lambda e: e.tensor_copy(out=vext[:, q4 * 4:q4 * 4 + nj, 0, 0:64], in_=st[:, :, 0:64]),
                     reads=[st_t], writes=[vext_t] if first else [], pw=[] if first else [vext_t])
                b.op("pool", lambda e: e.tensor_copy(out=vext[:, q4 * 4:q4 * 4 + nj, 1, 64:128], in_=st[:, :, 64:128]),
                     reads=[st_t], pw=[vext_t])
                yield
            b.dma("sp", O["nv"][:, hp * 128:(hp + 1) * 128].rearrange("(t p) c -> p t c", p=128), nvst[:],
                  reads=[nvst_t], is_output=True)
            cvv = I["cv"][:, hp * 128:(hp + 1) * 128].rearrange("(c p) n -> p c n", p=128)
            b.dma("pool", vext[:, 14:16, 0, 0:64], cvv[:, :, 0:64], pw=[vext_t])
            b.dma("pool", vext[:, 14:16, 1, 64:128], cvv[:, :, 64:128], pw=[vext_t])
            for tb in range(3):
                ps, ps_t = proj(wg, wg_t, tb * 512)
                b.op("act", lambda e: e.activation(out=sga[:, tb * 512:(tb + 1) * 512], in_=ps[:, :], func=AF.Silu),
                     reads=[ps_t], writes=[sga_t] if tb == 0 else [], pw=[] if tb == 0 else [sga_t])
                yield

        def attn_gen(hp):
            qT, qT_t, kT, kT_t, sga, sga_t, vext, vext_t = bufs(hp)
            if AT_STOP == 1:
                return
            def head_gen(e2, hp=hp, vext=vext, vext_t=vext_t):
                h = 2 * hp + e2
                pb = 64 * e2
                bt, bt_t = bts[h % 2]
                b.dma("sp", bt[:], I["btab"][h].rearrange("p (i c) -> p i c", i=24), writes=[bt_t])
                b.op("pool", lambda e: e.tensor_tensor(out=bt[:], in0=bt[:],
                                                        in1=cmask[:].unsqueeze(1).to_broadcast([128, 24, 64]), op=ALU.add),
                     reads=[bt_t, cmask_t], writes=[bt_t])
                def s_stage(bb):
                    ptl = []
                    for kc2 in range(2):
                        ch = 2 * bb + kc2
                        ps, ps_t = b.psum()
                        b.op("pe", lambda e: e.matmul(ps[:, 0:256], lhsT=kT[pb:pb + 64, ch * 128:(ch + 1) * 128],
                                                       rhs=qT[pb:pb + 64, bb * 256:(bb + 1) * 256], start=True, stop=True),
                             reads=[kT_t, qT_t], writes=[ps_t])
                        p, p_t = pt(e2)
                        b.op("act", lambda e: e.activation(out=p[:, 0:256], in_=ps[:, 0:256], func=AF.Exp),
                             reads=[ps_t], writes=[p_t])
                        ptl.append((p, p_t, ch))
                    return ptl

                def pv_stage(bb, ptl):
                    pso, pso_t = b.psum()
                    for i, (p, p_t, ch) in enumerate(ptl):
                        b.op("pe", lambda e: e.matmul(pso[:, 0:256], lhsT=vext[:, ch, e2, :], rhs=p[:, 0:256],
                                                       start=(i == 0), stop=(i == 1)),
                             reads=[vext_t, p_t], writes=[pso_t])
                    finalize(pso, pso_t, e2, hp, bb * 256, 256, sga, sga_t)

                prev = s_stage(0)
                yield
                for bb in range(1, 4):
                    cur = s_stage(bb)
                    pv_stage(bb - 1, prev)
                    prev = cur
                    yield
                pend = (3, prev)
                if AT_STOP == 2:
                    pv_stage(*pend)
                    return
                ptl = []
                for j in range(6):
                    kcol = 1024 + j * 128 if j < 4 else 1536 + (j - 4) * 128
                    i0 = (6 - 2 * j) if j < 4 else (14 + 2 - 2 * (j - 4))
                    ps, ps_t = b.psum()
                    b.op("pe", lambda e: e.matmul(ps[:, :], lhsT=kT[pb:pb + 64, kcol:kcol + 128],
                                                   rhs=qT[pb:pb + 64, 1024:1536], start=True, stop=True),
                         reads=[kT_t, qT_t], writes=[ps_t])
                    T, T_t = Ts[j % 2]
                    b.op("dve", lambda e: e.tensor_tensor(out=T[:].rearrange("p (r c) -> p r c", r=8),
                                                           in0=ps[:, :].rearrange("p (r c) -> p r c", r=8),
                                                           in1=bt[:, i0:i0 + 8, :], op=ALU.add),
                         reads=[ps_t, bt_t], writes=[T_t])
                    p, p_t = pt(e2)
                    b.op("act", lambda e: e.activation(out=p[:], in_=T[:], func=AF.Exp), reads=[T_t], writes=[p_t])
                    b.op("pool", lambda e: e.tensor_tensor(out=p[:].rearrange("p (r c) -> p r c", r=8),
                                                            in0=p[:].rearrange("p (r c) -> p r c", r=8),
                                                            in1=rmask[:, j, :].unsqueeze(2).to_broadcast([128, 8, 64]), op=ALU.mult),
                         reads=[p_t, rmask_t], writes=[p_t])
                    ptl.append((p, p_t, 8 + j))
                    if pend is not None:
                        pv_stage(*pend)
                        pend = None
                    yield
                for c in range(2):
                    ps, ps_t = b.psum()
                    b.op("pe", lambda e: e.matmul(ps[:, :], lhsT=kT[pb:pb + 64, 1792 + c * 128:1792 + (c + 1) * 128],
                                                   rhs=qT[pb:pb + 64, 1024:1536], start=True, stop=True),
                         reads=[kT_t, qT_t], writes=[ps_t])
                    p, p_t = pt(e2)
                    b.op("act", lambda e: e.activation(out=p[:], in_=ps[:, :], func=AF.Exp), reads=[ps_t], writes=[p_t])
                    ptl.append((p, p_t, 14 + c))
                    yield
                pso, pso_t = b.psum()
                for i, (p, p_t, ch) in enumerate(ptl):
                    b.op("pe", lambda e: e.matmul(pso[:, :], lhsT=vext[:, ch, e2, :], rhs=p[:], start=(i == 0), stop=(i == 7)),
                         reads=[vext_t, p_t], writes=[pso_t])
                finalize(pso, pso_t, e2, hp, 1024, 512, sga, sga_t)
            yield from interleave_gen(head_gen(0), head_gen(1))

        W = {0: shared.get("W0") or load_pair(0)}
        if AT_NHP > 1:
            W[1] = load_pair(1)
        interleave(proj_gen(0, W[0]))
        for hp in range(AT_NHP):
            if hp + 2 < AT_NHP:
                W[hp + 2] = load_pair(hp + 2)
            if hp == AT_NHP - 1 and AT_NHP == 8:
                shared["M0"] = [b.load_w(I["w_bh"], 0, 128), b.load_w(I["w_in"], 8 * D, 128),
                                b.load_w(I["w_ba"], 0, 128), b.load_w(I["w_in"], 9 * D, 128)]
                shared["M1"] = [b.load_w(I["w_bh"], 128, 128), b.load_w(I["w_in"], 8 * D + 128, 128),
                                b.load_w(I["w_ba"], 128, 128), b.load_w(I["w_in"], 9 * D + 128, 128)]
            gens = [attn_gen(hp)]
            if hp + 1 < AT_NHP:
                gens.append(proj_gen(hp + 1, W[hp + 1]))
            interleave(*gens)
        b.barrier()
        b.es = old_es


def phase_merge(b, I, hT, hT_t, yg, yg_t, ya, ya_t, mg, mg_t, shared):
    with ExitStack() as es:
        old_es = b.es
        b.es = es
        sgs = [b.sb("msg%d" % i, [128, 512], F32) for i in range(4)]
        t1s = [b.sb("mt%d" % i, [128, 512], F32) for i in range(4)]
        k = 0
        def load_oc(oc):
            return [b.load_w(I["w_bh"], oc * 128, 128), b.load_w(I["w_in"], 8 * D + oc * 128, 128),
                    b.load_w(I["w_ba"], oc * 128, 128), b.load_w(I["w_in"], 9 * D + oc * 128, 128)]
        nxt = shared.get("M0") or load_oc(0)
        for oc in range(8):
            (wbh, wbh_t), (wmh, wmh_t), (wba, wba_t), (wma, wma_t) = nxt
            if oc + 1 < 8:
                nxt = shared["M1"] if (oc == 0 and "M1" in shared) else load_oc(oc + 1)
            for tb in range(3):
                prods = []
                for (wp, wp_t, src, src_t, wm, wm_t) in ((wbh, wbh_t, yg, yg_t, wmh, wmh_t), (wba, wba_t, ya, ya_t, wma, wma_t)):
                    psm, psm_t = b.psum()
                    for kc in range(8):
                        b.op("pe", lambda e: e.matmul(psm[:, :], lhsT=wm[:, kc, 0:128], rhs=hT[:, kc, tb * 512:(tb + 1) * 512],
                                                       start=(kc == 0), stop=(kc == 7)),
                             reads=[wm_t, hT_t], writes=[psm_t])
                    sg, sg_t = sgs[k % 4]
                    b.op("act", lambda e: e.activation(out=sg[:], in_=psm[:, :], func=AF.Sigmoid), reads=[psm_t], writes=[sg_t])
                    psp, psp_t = b.psum()
                    for kc in range(8):
                        b.op("pe", lambda e: e.matmul(psp[:, :], lhsT=wp[:, kc, 0:128], rhs=src[:, kc, tb * 512:(tb + 1) * 512],
                                                       start=(kc == 0), stop=(kc == 7)),
                             reads=[wp_t, src_t], writes=[psp_t])
                    t1, t1_t = t1s[k % 4]
                    k += 1
                    b.op("dve", lambda e: e.tensor_tensor(out=t1[:], in0=psp[:, :], in1=sg[:], op=ALU.mult),
                         reads=[psp_t, sg_t], writes=[t1_t])
                    prods.append((t1, t1_t))
                b.op("pool", lambda e: e.tensor_tensor(out=mg[:, oc, tb * 512:(tb + 1) * 512], in0=prods[0][0][:],
                                                        in1=prods[1][0][:], op=ALU.add),
                     reads=[prods[0][1], prods[1][1]], pw=[mg_t])
        b.barrier()
        b.es = old_es


def phase_final(b, I, O, par, par_t, mg, mg_t, pre):
    nc = b.nc
    with ExitStack() as es:
        old_es = b.es
        b.es = es
        rows, rows_t = pre["rows"]
        grow, grow_t = b.sb("grow", [128, 2, D], F32)
        with ExitStack() as es2:
            b.es = es2
            brow, brow_t = pre["brow"]
            sc, sc_t = b.sb("sc", [128, 16], F32)
            screp, screp_t = b.sb("screp", [128, 2, 8, 128], BF16)
            b.op("act", lambda e: e.activation(out=sc[:], in_=par[:, P_COND:P_COND + 16], func=AF.Silu),
                 reads=[par_t], writes=[sc_t])
            for w in range(2):
                b.op("dve", lambda e: e.tensor_copy(out=screp[:, w, :, :],
                                                     in_=sc[:, w:16:2].unsqueeze(2).to_broadcast([128, 8, 128])),
                     reads=[sc_t], pw=[screp_t])
            for cb in range(2):
                wa, wa_t = pre["wa"][cb]
                for w in range(2):
                    ps, ps_t = b.psum()
                    for kc in range(8):
                        b.op("pe", lambda e: e.matmul(ps[:, :], lhsT=screp[:, w, kc, :], rhs=wa[:, kc, :],
                                                       start=(kc == 0), stop=(kc == 7)),
                             reads=[screp_t, wa_t], writes=[ps_t])
                    b.op("dve", lambda e: e.tensor_tensor(out=grow[:, w, cb * 512:(cb + 1) * 512], in0=ps[:, :],
                                                           in1=brow[:, cb * 512:(cb + 1) * 512], op=ALU.add),
                         reads=[ps_t, brow_t], pw=[grow_t])
            b.barrier()
        b.es = es
        wo = pre["wo"]
        xts = [b.sb("fx%d" % i, [128, D], F32) for i in range(3)]
        rts = [b.sb("fr%d" % i, [128, D], F32) for i in range(4)]
        FMAX = int(nc.vector.BN_STATS_FMAX)
        SD = int(nc.vector.BN_STATS_DIM)
        AD = int(nc.vector.BN_AGGR_DIM)
        nst = (D + FMAX - 1) // FMAX
        statss = [b.sb("stats%d" % i, [128, nst, SD], F32) for i in range(4)]
        mvs = [b.sb("mv%d" % i, [128, AD], F32) for i in range(4)]
        rstds = [b.sb("rstd%d" % i, [128, 1], F32) for i in range(4)]
        for tt in range(12):
            w = 0 if tt < 8 else 1
            xt, xt_t = xts[tt % 3]
            rt, rt_t = rts[tt % 4]
            stats, stats_t = statss[tt % 4]
            mv, mv_t = mvs[tt % 4]
            rstd, rstd_t = rstds[tt % 4]
            if tt == 0:
                for t0 in range(2):
                    b.dma("sp", xts[t0][0][:], I["xall"][t0 * 128:(t0 + 1) * 128, :], writes=[xts[t0][1]])
            if tt + 2 < 12:
                nx, nx_t = xts[(tt + 2) % 3]
                b.dma("sp", nx[:], I["xall"][(tt + 2) * 128:(tt + 3) * 128, :], writes=[nx_t])
            for cb in range(2):
                ps, ps_t = b.psum()
                for kc in range(8):
                    b.op("pe", lambda e: e.matmul(ps[:, :], lhsT=mg[:, kc, tt * 128:(tt + 1) * 128], rhs=wo[cb][0][:, kc, :],
                                                   start=(kc == 0), stop=(kc == 7)),
                         reads=[mg_t, wo[cb][1]], writes=[ps_t])
                b.op("dve", lambda e: e.tensor_tensor(out=rt[:, cb * 512:(cb + 1) * 512], in0=ps[:, :],
                                                       in1=grow[:, w, cb * 512:(cb + 1) * 512], op=ALU.mult),
                     reads=[ps_t, grow_t], writes=[rt_t] if cb == 0 else [], pw=[] if cb == 0 else [rt_t])
            b.op("dve", lambda e: e.scalar_tensor_tensor(out=rt[:], in0=xt[:], scalar=ALPHA, in1=rt[:],
                                                          op0=ALU.mult, op1=ALU.add),
                 reads=[xt_t, rt_t], writes=[rt_t])
            for c in range(nst):
                lo = c * FMAX
                hi = min(D, lo + FMAX)
                b.op("dve", lambda e: e.bn_stats(out=stats[:, c, :], in_=rt[:, lo:hi]),
                     reads=[rt_t], writes=[stats_t] if c == 0 else [], pw=[] if c == 0 else [stats_t])
            b.op("dve", lambda e: e.bn_aggr(out=mv[:], in_=stats[:]), reads=[stats_t], writes=[mv_t])
            b.op("dve", lambda e: e.tensor_scalar(out=rstd[:], in0=mv[:, 1:2], scalar1=LN_EPS, scalar2=None, op0=ALU.add),
                 reads=[mv_t], writes=[rstd_t])
            b.op("act", lambda e: e.sqrt(out=rstd[:], in_=rstd[:]), reads=[rstd_t], writes=[rstd_t])
            b.op("dve", lambda e: e.reciprocal(out=rstd[:], in_=rstd[:]), reads=[rstd_t], writes=[rstd_t])
            b.op("dve", lambda e: e.tensor_scalar(out=rt[:], in0=rt[:], scalar1=mv[:, 0:1], scalar2=rstd[:, 0:1],
                                                   op0=ALU.subtract, op1=ALU.mult),
                 reads=[rt_t, mv_t, rstd_t], writes=[rt_t])
            b.op("pool", lambda e: e.tensor_tensor(out=rt[:], in0=rt[:], in1=rows[:, 0, :], op=ALU.mult),
                 reads=[rt_t, rows_t], writes=[rt_t])
            b.op("pool", lambda e: e.tensor_tensor(out=rt[:], in0=rt[:], in1=rows[:, 1, :], op=ALU.add),
                 reads=[rt_t, rows_t], writes=[rt_t])
            b.dma("pool", O["y"][tt * 128:(tt + 1) * 128, :], rt[:], reads=[rt_t], is_output=True)
        b.barrier()
        b.es = old_es


def _attn_tabs(half, rpb):
    kc = np.arange(64)[:, None]
    col = np.arange(64)[None, :]
    dcidx = np.clip(kc - col + 15, 0, 30)
    bt = np.zeros((16, 128, 24, 64), np.float32)
    for e in range(2):
        for ip in range(14):
            dr = (13 - ip) - 7 + e
            if -7 <= dr <= 7:
                bt[:, e * 64:(e + 1) * 64, ip, :] = rpb[:, dr + 7][:, dcidx]
        D0 = 8 if half == 0 else -4
        for ip in range(10):
            dr = D0 + (9 - ip) - 7 + e
            if -7 <= dr <= 7:
                bt[:, e * 64:(e + 1) * 64, 14 + ip, :] = rpb[:, dr + 7][:, dcidx]
    c0 = np.clip(col - 8, 0, 48)
    inwin = (kc >= c0) & (kc < c0 + 16)
    cm = np.where(inwin, 0.0, -30000.0).astype(np.float32)
    cmask = np.concatenate([cm, cm], 0)
    rmask = np.zeros((128, 6, 8), np.float32)
    for j in range(6):
        for e in range(2):
            if half == 0:
                kr = 2 * j + e if j < 4 else 8 + 2 * (j - 4) + e
            else:
                kr = 8 + 2 * j + e if j < 4 else 4 + 2 * (j - 4) + e
            for rl in range(8):
                r = rl if half == 0 else 8 + rl
                r0 = min(max(r - 4, 0), 8)
                rmask[e * 64:(e + 1) * 64, j, rl] = 1.0 if (r0 <= kr < r0 + 8) else 0.0
    return bt.reshape(16, 128, 24 * 64), cmask, rmask.reshape(128, 48)

def _core_inputs(j, inp):
    bs, half = j // 2, j % 2
    c = _consts(half)
    xs = inp["x_sample"][bs]
    own = xs[half * 512:(half + 1) * 512]
    oth = xs[(1 - half) * 512:(2 - half) * 512]
    halo = xs[512:768] if half == 0 else xs[256:512]
    xall = np.concatenate([inp["x_prompt"][4 * j:4 * j + 4].reshape(1024, D), own, oth, halo], 0)
    par = np.zeros((128, P_NP), np.float32)
    cond = np.stack([inp["c_ctx"], inp["c"][bs]], 0)
    par[:, P_COND:P_COND + 16] = cond.reshape(2, 8, 128).transpose(2, 1, 0).reshape(128, 16)
    par[:, P_BADA:P_BADA + 24] = inp["b_ada"][0].reshape(24, 128).T
    par[:, P_CW:P_CW + 72] = inp["conv_w"][0].reshape(3, 24, 128).transpose(2, 0, 1).reshape(128, 72)
    par[:, P_CB:P_CB + 24] = inp["conv_b"][0].reshape(24, 128).T
    par[:, P_HD:P_HD + 8] = inp["hyena_d"][0].reshape(8, 128).T
    par[:, P_FLAG] = 1.0 if half == 1 else 0.0
    par[:, P_FLAG + 1] = 1.0 if half == 0 else 0.0
    par[0:64, P_FPAR + 0] = inp["filt_b1"][0]
    par[0:64, P_FPAR + 1] = inp["filt_b2"][0]
    par[0:64, P_FPAR + 2] = inp["filt_freq"][0, 0]
    par[0:64, P_FPAR + 3] = inp["filt_freq"][0, 1]
    w3 = inp["filt_w3"][0]
    w3c = np.concatenate([w3, w3[:, :D] if half == 1 else w3[:, D:]], 1)
    m = {
        "xall": xall, "params": par,
        "ck": inp["cache_k"][bs, 0].reshape(256, D), "cv": inp["cache_v"][bs, 0].reshape(256, D),
        "w_ada": inp["w_ada"][0], "w_in": inp["w_in"][0], "w_bh": inp["w_bh"][0], "w_ba": inp["w_ba"][0],
        "w_out": inp["w_out"][0], "fw1": inp["filt_w1"][0], "fw2": inp["filt_w2"][0], "w3c": w3c,
        "rows": np.stack([inp["b_ada"][0, 2 * D:], inp["ln_g"][0], inp["ln_b"][0]], 0),
    }
    m["btab"], m["cmask"], m["rmask"] = _attn_tabs(half, inp["rpb"][0])
    m.update(c)
    out = {}
    for k, v in m.items():
        if k in ("fwd256", "bwd256", "inv256", "fwd512", "bwd512", "inv512", "x1024"):
            out[k] = np.ascontiguousarray(v)
        else:
            out[k] = np.ascontiguousarray(v, dtype=np.float32)
    return out


_PROG = {}


def kernel(**inputs):
    inp = {k: np.asarray(v) for k, v in inputs.items()}
    if "nc" not in _PROG:
        _PROG["nc"], _PROG["b"] = build_program()
    nc = _PROG["nc"]
    in_maps = [_core_inputs(j, inp) for j in range(NCORES)]
    res = run_bass_kernel_spmd(nc, in_maps, core_ids=list(range(NCORES)))
    _PROG["last"] = res
    y_p = np.zeros((32, 256, D), np.float32)
    y_s = np.zeros((4, 1024, D), np.float32)
    nk = np.zeros((32, 1, 256, 16, 64), np.float32)
    nv = np.zeros((32, 1, 256, 16, 64), np.float32)
    for j in range(NCORES):
        r = res.results[j]
        bs, half = j // 2, j % 2
        y_p[4 * j:4 * j + 4] = r["y"][:1024].reshape(4, 256, D)
        y_s[bs, half * 512:(half + 1) * 512] = r["y"][1024:]
        nk[4 * j:4 * j + 4, 0] = r["nk"].reshape(4, 256, 16, 64)
        nv[4 * j:4 * j + 4, 0] = r["nv"].reshape(4, 256, 16, 64)
    return y_p, y_s, nk, nv
```

```python
import math
from contextlib import ExitStack

import numpy as np
import concourse.bass as bass
import concourse.mybir as mybir
from concourse.bass_utils import run_bass_kernel_spmd

F32 = mybir.dt.float32
BF16 = mybir.dt.bfloat16
AF = mybir.ActivationFunctionType
ALU = mybir.AluOpType

D = 1024
NCORES = 8
PR0, OWN0, OTH0, HALO0, NTOK = 0, 1024, 1536, 2048, 2304
NOWN = 1536
ALPHA = 2.0 ** 0.25
LN_EPS = 1e-5
TWO_PI = 2.0 * math.pi

P_COND, P_BADA, P_CW, P_CB, P_HD, P_FLAG, P_FPAR, P_NP = 0, 16, 40, 112, 136, 144, 146, 150

class StopBuild(Exception):
    pass


HY_STOP = 0
AT_STOP = 0
SAME_ENGINE_NOSYNC = ("pe",)
AT_NHP = 8
AT_VAR = 0
DEBUG = {}
STOP_AFTER = None


class Tok:
    __slots__ = ("w", "r", "x")

    def __init__(self):
        self.w = {}
        self.r = {}
        self.x = {}


class Bld:
    def __init__(self, nc, es, n_dma=32):
        self.nc = nc
        self.es = es
        self.engs = {"pe": nc.tensor, "act": nc.scalar, "dve": nc.vector, "pool": nc.gpsimd, "sp": nc.sync}
        self.sems = {}
        self.cnt = {}
        for e in ("pe", "act", "dve", "pool"):
            self.sems[e] = es.enter_context(nc.semaphore("s_" + e))
            self.cnt[e] = 0
        self.dcnt = []
        for i in range(n_dma):
            self.sems[("d", i)] = es.enter_context(nc.semaphore("s_d%d" % i))
            self.dcnt.append(0)
        self.drr = 0
        self.drr_sw = 0
        self.waited = {e: {} for e in self.engs}
        self.out_events = {}
        self.ps = []
        for i in range(8):
            t = es.enter_context(nc.psum_tensor("psb%d" % i, [128, 512], F32))
            self.ps.append((t, Tok()))
        self.psi = 0
        self.dbg = {}

    def psum(self):
        t = self.ps[self.psi]
        self.psi = (self.psi + 1) % 8
        return t

    def _deps(self, reads, writes, pw=()):
        deps = {}

        def add(d):
            for sk, v in d.items():
                if deps.get(sk, 0) < v:
                    deps[sk] = v
        for t in reads:
            add(t.w)
        for t in writes:
            add(t.w)
            add(t.r)
        for t in pw:
            add(t.r)
            add(t.x)
        return deps

    def _wait(self, eng, deps):
        for sk, v in deps.items():
            if sk == eng and eng in SAME_ENGINE_NOSYNC:
                continue
            if self.waited[eng].get(sk, 0) >= v:
                continue
            self.engs[eng].wait_ge(self.sems[sk], v)
            self.waited[eng][sk] = v

    def _record(self, sk, v, reads, writes, pw=()):
        for t in writes:
            t.w = {sk: v}
            t.x = {sk: v}
            t.r = {}
        for t in pw:
            if t.w.get(sk, 0) < v:
                t.w[sk] = v
        for t in reads:
            if t.r.get(sk, 0) < v:
                t.r[sk] = v

    def op(self, eng, fn, reads=(), writes=(), pw=()):
        self._wait(eng, self._deps(reads, writes, pw))
        ins = fn(self.engs[eng])
        self.cnt[eng] += 1
        ins.then_inc(self.sems[eng], 1)
        self._record(eng, self.cnt[eng], reads, writes, pw)

    def dma(self, q, out, in_, reads=(), writes=(), pw=(), is_output=False, **kw):
        self._wait(q, self._deps(reads, writes, pw))
        half = len(self.dcnt) // 2
        if q == "pool":
            i = half + self.drr_sw
            self.drr_sw = (self.drr_sw + 1) % half
        else:
            i = self.drr
            self.drr = (self.drr + 1) % half
        if self.dcnt[i] > 0:
            self._wait(q, {("d", i): self.dcnt[i]})
        self.dcnt[i] += 16
        self.engs[q].dma_start(out=out, in_=in_, **kw).then_inc(self.sems[("d", i)], 16)
        self._record(("d", i), self.dcnt[i], reads, writes, pw)
        if is_output:
            self.out_events[("d", i)] = self.dcnt[i]

    def barrier(self, dma=True):
        deps = {e: c for e, c in self.cnt.items() if c > 0}
        if dma:
            for i, c in enumerate(self.dcnt):
                if c > 0:
                    deps[("d", i)] = c
        for e in self.engs:
            self._wait(e, deps)

    def finish(self):
        self._wait("sp", dict(self.out_events))
        self.barrier()

    def init_wrings(self):
        self.wsm = [self.sb("wsm%d" % i, [128, 8, 128], BF16) for i in range(8)]
        self.wsi = 0
        self.wbi = 0

    def alloc_big(self, n=2):
        self.wbi += 1000
        self.wbg = [self.sb("wbg%d_%d" % (self.wbi, i), [128, 8, 512], BF16) for i in range(n)]

    def load_w(self, dram_w, col0, ncols):
        if ncols <= 128:
            t, tk = self.wsm[self.wsi % len(self.wsm)]
            self.wsi += 1
        else:
            t, tk = self.wbg[self.wbi % len(self.wbg)]
            self.wbi += 1
        self.dma("pool", t[:, :, 0:ncols], dram_w[:, col0:col0 + ncols].rearrange("(c p) n -> p c n", p=128), writes=[tk])
        return t, tk

    def sb(self, name, shape, dt, side=None):
        if side is None:
            t = self.es.enter_context(self.nc.sbuf_tensor("sb_" + name, shape, dt))
        else:
            t = self.es.enter_context(self.nc.sbuf_tensor("sb_" + name, shape, dt, side=side))
        return t, Tok()

    def dump(self, name, ap, tok, shape, dt=F32):
        if name not in DEBUG:
            return
        d = self.nc.dram_tensor("dbg_" + name, list(shape), dt, kind="ExternalOutput").ap()
        self.dma("sp", d, ap, reads=[tok], is_output=True)
        self.dbg[name] = d


def _fwd_tab(L, n):
    s = np.arange(L, dtype=np.float64)[:, None]
    f = (np.arange(n // 2, dtype=np.float64) + 0.5)[None, :]
    ang = 2.0 * np.pi * s * f / n
    return np.concatenate([np.cos(ang), -np.sin(ang)], 1).astype(np.float32)


def _bwd_tab(L, n):
    s = np.arange(L, dtype=np.float64)[:, None]
    f = (np.arange(n // 2, dtype=np.float64) + 0.5)[None, :]
    ang = 2.0 * np.pi * s * f / n
    t = np.concatenate([np.cos(ang), np.sin(ang)], 1)
    t[0] = 0.0
    return t.astype(np.float32)


def _inv_tab(n, Lout):
    t = np.arange(Lout, dtype=np.float64)[None, :]
    f = (np.arange(n // 2, dtype=np.float64) + 0.5)[:, None]
    ang = 2.0 * np.pi * f * t / n
    return ((2.0 / n) * np.concatenate([np.cos(ang), -np.sin(ang)], 0)).astype(np.float32)


def _x_tab(half):
    d = (np.arange(1024, dtype=np.float64) - 512.0)[:, None]
    f = (np.arange(512, dtype=np.float64) + 0.5)[None, :]
    ang = 2.0 * np.pi * d * f / 1024.0
    sign = -1.0 if half == 1 else 1.0
    t = np.concatenate([np.cos(ang), sign * np.sin(ang)], 1)
    t[0] = 0.0
    return t.astype(np.float32)


def _z_tab(L):
    t = np.linspace(0.0, 1.0, L, dtype=np.float32)[:, None]
    bands = np.linspace(1e-4, 15.0, 16, dtype=np.float32)[None]
    w = (2.0 * np.float32(math.pi) * np.arange(L, dtype=np.float32)[:, None] / np.float32(L)).astype(np.float32)
    z = np.concatenate([t, np.cos(bands * w), -np.sin(bands * w)], axis=-1).astype(np.float32)
    return np.ascontiguousarray(z.T)


def _decay_consts():
    mn = math.log(1e-2) / 1.5
    mx = math.log(1e-2) / 0.3
    deltas = np.linspace(mn, mx, 1024, dtype=np.float32)
    absd = np.abs(deltas).astype(np.float32)[None, :]
    negt = np.zeros((128, 10), np.float32)
    t256 = np.linspace(0.0, 1.0, 256, dtype=np.float32)
    t1k = np.linspace(0.0, 1.0, 1024, dtype=np.float32)
    for tc in range(2):
        negt[:, tc] = -t256[tc * 128:(tc + 1) * 128]
    for tc in range(8):
        negt[:, 2 + tc] = -t1k[tc * 128:(tc + 1) * 128]
    return absd, negt


_CONST_CACHE = {}


def _consts(half):
    if half in _CONST_CACHE:
        return _CONST_CACHE[half]
    c = {
        "fwd256": _fwd_tab(256, 512), "bwd256": _bwd_tab(256, 512), "inv256": _inv_tab(512, 256),
        "fwd512": _fwd_tab(512, 1024), "bwd512": _bwd_tab(512, 1024), "inv512": _inv_tab(1024, 512),
        "x1024": _x_tab(half),
        "zt256": _z_tab(256), "zt1024": _z_tab(1024),
        "absd": _decay_consts()[0], "negt": _decay_consts()[1],
        "ident": np.eye(128, dtype=np.float32),
    }
    import ml_dtypes
    for k in ("fwd256", "bwd256", "inv256", "fwd512", "bwd512", "inv512", "x1024"):
        c[k] = c[k].astype(ml_dtypes.bfloat16)
    _CONST_CACHE[half] = c
    return c


def build_program():
    nc = bass.Bass("TRN2", target_bir_lowering=False)

    BF_TABS = ("fwd256", "bwd256", "inv256", "fwd512", "bwd512", "inv512", "x1024")

    def din(name, shape):
        return nc.dram_tensor(name, list(shape), BF16 if name in BF_TABS else F32, kind="ExternalInput").ap()

    def dout(name, shape):
        return nc.dram_tensor(name, list(shape), F32, kind="ExternalOutput").ap()

    I = {}
    for name, shape in [
        ("xall", (NTOK, D)), ("params", (128, P_NP)), ("ck", (256, D)), ("cv", (256, D)),
        ("w_ada", (D, 3 * D)), ("w_in", (D, 10 * D)), ("w_bh", (D, D)), ("w_ba", (D, D)), ("w_out", (D, D)),
        ("fw1", (33, 64)), ("fw2", (64, 64)), ("w3c", (64, 3 * D)),
        ("rows", (3, D)),
        ("fwd256", (256, 512)), ("bwd256", (256, 512)), ("inv256", (512, 256)),
        ("fwd512", (512, 1024)), ("bwd512", (512, 1024)), ("inv512", (1024, 512)), ("x1024", (1024, 1024)),
        ("zt256", (33, 256)), ("zt1024", (33, 1024)), ("absd", (1, D)), ("negt", (128, 10)),
        ("ident", (128, 128)), ("btab", (16, 128, 24 * 64)), ("cmask", (128, 64)), ("rmask", (128, 48)),
    ]:
        I[name] = din(name, shape)
    O = {"y": dout("y", (NOWN, D)), "nk": dout("nk", (1024, D)), "nv": dout("nv", (1024, D))}

    with ExitStack() as es:
        b = Bld(nc, es)
        par, par_t = b.sb("par", [128, P_NP], F32)
        ident, ident_t = b.sb("ident", [128, 128], F32)
        identb, identb_t = b.sb("identb", [128, 128], BF16)
        b.dma("sp", par[:], I["params"][:, :], writes=[par_t])
        b.dma("sp", ident[:], I["ident"][:, :], writes=[ident_t])
        b.op("dve", lambda e: e.tensor_copy(out=identb[:], in_=ident[:]), reads=[ident_t], writes=[identb_t])

        esH = ExitStack()
        b.es = esH
        h256, h256_t = b.sb("h256", [128, 4, D], BF16, side="right")
        hoo, hoo_t = b.sb("hoo", [128, 8, D], BF16, side="right")
        hox, hox_t = b.sb("hox", [128, 8, D], BF16, side="right")
        dft = {}
        for name, rows, cols in (("fwd256", 256, 512), ("fwd512", 512, 1024), ("inv256", 512, 256), ("inv512", 1024, 512)):
            t, tk = b.sb("t_" + name, [128, rows // 128, cols], BF16, side="right")
            dft[name] = (t, tk)
        b.es = es

        b.init_wrings()
        esA = ExitStack()
        b.es = esA
        b.wbg = [b.sb("wbgA%d" % i, [128, 8, 512], BF16, side="right") for i in range(4)]
        b.es = es
        ada_pre = []
        phase_filters(b, I, par, par_t, h256, h256_t, hoo, hoo_t, hox, hox_t, dft, ada_pre)
        b.dump("h256", h256[:], h256_t, [128, 4, D], BF16)
        b.dump("hoo", hoo[:], hoo_t, [128, 8, D], BF16)
        b.dump("hox", hox[:], hox_t, [128, 8, D], BF16)
        if STOP_AFTER == "filters":
            b.finish()
            esH.close()
            return nc, b

        modsb, modsb_t = b.sb("modsb", [128, 24, 2], F32)
        hT, _ = b.sb("hT", [128, 8, NTOK], BF16)
        hT_t = Tok()
        yg, yg_t = b.sb("yg", [128, 8, NOWN], BF16)
        scb, scb_t = b.sb("scb", [128, 16], BF16)
        esX = ExitStack()
        b.es = esX
        xts = [b.sb("xt%d" % i, [128, D], F32) for i in range(8)]
        b.es = es
        for i in range(8):
            b.dma("sp", xts[i][0][:], I["xall"][i * 128:(i + 1) * 128, :], writes=[xts[i][1]])
        phase_mod(b, I, par, par_t, modsb, modsb_t, ada_pre, scb, scb_t)
        b.dump("mod", modsb[:], modsb_t, [128, 24, 2])
        pre_w = {(cc, sec): b.load_w(I["w_in"], sec * D + cc * 128, 128) for cc in range(2) for sec in range(2)}
        shared = {}
        phase_ht(b, I, ident, ident_t, modsb, modsb_t, hT, hT_t, xts, 8)
        esX.close()
        esA.close()
        b.dump("hT", hT[:], hT_t, [128, 8, NTOK], BF16)
        if STOP_AFTER == "ht":
            b.finish()
            esH.close()
            return nc, b
        try:
            phase_hyena(b, I, par, par_t, identb, identb_t, hT, hT_t, yg, yg_t, h256, h256_t, hoo, hoo_t, hox, hox_t, dft, pre_w, shared)
        except StopBuild:
            b.finish()
            esH.close()
            return nc, b
        b.dump("yg", yg[:], yg_t, [128, 8, NOWN], BF16)
        if STOP_AFTER == "hyena":
            b.finish()
            esH.close()
            return nc, b
        b.barrier()
        esH.close()
        ya, ya_t = b.sb("ya", [128, 8, NOWN], BF16)
        phase_attn(b, I, O, par, par_t, ident, ident_t, hT, hT_t, ya, ya_t, shared)
        b.dump("ya", ya[:], ya_t, [128, 8, NOWN], BF16)
        if STOP_AFTER == "attn":
            b.finish()
            return nc, b
        mg, mg_t = b.sb("mg", [128, 8, NOWN], BF16)
        b.wbi += 1000
        b.wbg = [b.sb("wbgF%d" % i, [128, 8, 512], BF16) for i in range(4)]
        pre = {"wa": [b.load_w(I["w_ada"], 2 * D + cb * 512, 512) for cb in range(2)],
               "wo": [b.load_w(I["w_out"], cb * 512, 512) for cb in range(2)]}
        pre["rows"] = b.sb("rows", [128, 2, D], F32)
        pre["brow"] = b.sb("brow", [128, D], F32)
        for r in range(2):
            b.dma("sp", pre["rows"][0][:, r, :], I["rows"][r + 1:r + 2, :].to_broadcast([128, D]), pw=[pre["rows"][1]])
        b.dma("sp", pre["brow"][0][:], I["rows"][0:1, :].to_broadcast([128, D]), writes=[pre["brow"][1]])
        phase_merge(b, I, hT, hT_t, yg, yg_t, ya, ya_t, mg, mg_t, shared)
        b.dump("mg", mg[:], mg_t, [128, 8, NOWN], BF16)
        phase_final(b, I, O, par, par_t, mg, mg_t, pre)

        b.finish()
    return nc, b


def phase_mod(b, I, par, par_t, modsb, modsb_t, ada_pre, scb, scb_t):
    with ExitStack() as es:
        old_es = b.es
        b.es = es
        b.op("act", lambda e: e.activation(out=scb[:], in_=par[:, P_COND:P_COND + 16], func=AF.Silu),
             reads=[par_t], writes=[scb_t])
        ps, ps_t = b.psum()
        for jb in range(4):
            w, w_t = ada_pre[jb]
            for c4 in range(4):
                ch = jb * 4 + c4
                for kc in range(8):
                    b.op("pe", lambda e: e.matmul(ps[:, 2 * ch:2 * ch + 2], lhsT=w[:, kc, c4 * 128:(c4 + 1) * 128],
                                                   rhs=scb[:, 2 * kc:2 * kc + 2], start=(kc == 0), stop=(kc == 7)),
                         reads=[w_t, scb_t], writes=[ps_t])
        b.op("dve", lambda e: e.tensor_tensor(
            out=modsb[:, 0:16, :], in0=ps[:, 0:32].rearrange("p (c w) -> p c w", w=2),
            in1=par[:, P_BADA:P_BADA + 16].unsqueeze(2).to_broadcast([128, 16, 2]), op=ALU.add),
            reads=[ps_t, par_t], writes=[modsb_t])
        b.op("dve", lambda e: e.tensor_scalar(out=modsb[:, 8:16, :], in0=modsb[:, 8:16, :], scalar1=1.0, scalar2=None,
                                               op0=ALU.add), reads=[modsb_t], writes=[modsb_t])
        b.es = old_es


def phase_ht(b, I, ident, ident_t, modsb, modsb_t, hT, hT_t, xts, npre):
    with ExitStack() as es:
        old_es = b.es
        b.es = es
        xi = 0
        k = 0
        groups = [(0, 4, 0), (512, 4, 0), (1024, 4, 1), (1536, 4, 1), (2048, 2, 1)]
        for (col0, ntile, w) in groups:
            tiles = []
            for t in range(ntile):
                xt, xt_t = xts[xi % 8]
                xi += 1
                r0 = col0 + t * 128
                if xi > npre:
                    b.dma("sp", xt[:], I["xall"][r0:r0 + 128, :], writes=[xt_t])
                tiles.append((xt, xt_t))
            for fc in range(8):
                ps, ps_t = b.psum()
                for t, (xt, xt_t) in enumerate(tiles):
                    b.op("pe", lambda e: e.transpose(ps[:, t * 128:(t + 1) * 128], xt[:, fc * 128:(fc + 1) * 128], ident[:]),
                         reads=[xt_t, ident_t], writes=[ps_t])
                n = ntile * 128
                k += 1
                if k % 2 == 0:
                    b.op("act", lambda e: e.activation(out=hT[:, fc, col0:col0 + n], in_=ps[:, 0:n], func=AF.Identity,
                                                        bias=modsb[:, fc, w:w + 1], scale=modsb[:, 8 + fc, w:w + 1]),
                         reads=[ps_t, modsb_t], pw=[hT_t])
                else:
                    b.op("dve", lambda e: e.tensor_scalar(out=hT[:, fc, col0:col0 + n], in0=ps[:, 0:n],
                                                           scalar1=modsb[:, 8 + fc, w:w + 1], scalar2=modsb[:, fc, w:w + 1],
                                                           op0=ALU.mult, op1=ALU.add),
                         reads=[ps_t, modsb_t], pw=[hT_t])
        b.barrier()
        b.es = old_es


def phase_hyena(b, I, par, par_t, identb, identb_t, hT, hT_t, yg, yg_t, h256, h256_t, hoo, hoo_t, hox, hox_t, dft, pre_w, shared):
    with ExitStack() as es:
        old_es = b.es
        b.es = es

        fwd256, fwd256_t = dft["fwd256"]
        inv256, inv256_t = dft["inv256"]
        fwd512, fwd512_t = dft["fwd512"]
        inv512, inv512_t = dft["inv512"]

        cwf, cwf_t = b.sb("cwf", [128, 4, 24], F32)
        for i, (j, fl) in enumerate(((0, 0), (2, 0), (0, 1), (2, 1))):
            b.op("dve", lambda e: e.tensor_scalar(out=cwf[:, i, :], in0=par[:, P_CW + j * 24:P_CW + (j + 1) * 24],
                                                   scalar1=par[:, P_FLAG + fl:P_FLAG + fl + 1], scalar2=None, op0=ALU.mult),
                 reads=[par_t], pw=[cwf_t])

        def cw(j, ci):
            return par[:, P_CW + j * 24 + ci:P_CW + j * 24 + ci + 1]

        cA, _ = b.sb("cA", [128, 2048], F32)
        cB, _ = b.sb("cB", [128, 2048], F32)
        cA_t = [Tok() for _ in range(4)]
        cB_t = [Tok() for _ in range(4)]
        edgs = [b.sb("edges%d" % i, [128, 4], F32) for i in range(2)]
        sgs = [b.sb("sg%d" % i, [128, 512], F32) for i in range(2)]
        tmps = [b.sb("sp%d" % i, [128, 256], F32) for i in range(10)]
        tmpi = [0]

        def tmp(eng="dve"):
            tmpi[0] += 1
            ring = tmps if eng == "dve" else ptmps
            return ring[tmpi[0] % len(ring)]
        utm, utm_t = b.sb("utm", [128, 16, 256], BF16)
        ufm, ufm_t = b.sb("ufm", [128, 2, 2048], BF16)
        xg, xg_t = b.sb("xg", [128, 2, NOWN], BF16)
        Y4s = [b.sb("Y4_%d" % i, [128, 4, 256], BF16) for i in range(2)]
        Y8, Y8_t = b.sb("Y8", [128, 8, 256], BF16)
        eps = [b.sb("ep%d" % i, [128, 512], F32) for i in range(4)]
        epi = [0]
        ptmps = []
        pend_is = [None]
        pend_T = []

        def conv_block(ps, ps_t, acc, acc_t, col0, ci, nseg):
            seglen = 512 // nseg
            b.op("act", lambda e: e.activation(out=acc[:, col0:col0 + 512], in_=ps[:, :], func=AF.Identity,
                                                bias=par[:, P_CB + ci:P_CB + ci + 1], scale=cw(1, ci)),
                 reads=[ps_t, par_t], writes=[acc_t])
            av = acc[:, col0:col0 + 512].rearrange("p (s l) -> p s l", s=nseg)
            pv = ps[:, :].rearrange("p (s l) -> p s l", s=nseg)
            b.op("dve", lambda e: e.scalar_tensor_tensor(out=av[:, :, 1:seglen], in0=pv[:, :, 0:seglen - 1], scalar=cw(0, ci),
                                                          in1=av[:, :, 1:seglen], op0=ALU.mult, op1=ALU.add),
                 reads=[ps_t, par_t, acc_t], writes=[acc_t])
            b.op("dve", lambda e: e.scalar_tensor_tensor(out=av[:, :, 0:seglen - 1], in0=pv[:, :, 1:seglen], scalar=cw(2, ci),
                                                          in1=av[:, :, 0:seglen - 1], op0=ALU.mult, op1=ALU.add),
                 reads=[ps_t, par_t, acc_t], writes=[acc_t])

        def fix(acc, acc_t, col, wi, ci, src, src_t):
            b.op("dve", lambda e: e.scalar_tensor_tensor(out=acc[:, col:col + 1], in0=src, scalar=cwf[:, wi, ci:ci + 1],
                                                          in1=acc[:, col:col + 1], op0=ALU.mult, op1=ALU.add),
                 reads=[src_t, cwf_t, acc_t], writes=[acc_t])

        def proj(w, w_t, col0, n=512):
            ps, ps_t = b.psum()
            for kc in range(8):
                b.op("pe", lambda e: e.matmul(ps[:, 0:n], lhsT=w[:, kc, 0:128], rhs=hT[:, kc, col0:col0 + n],
                                               start=(kc == 0), stop=(kc == 7)),
                     reads=[w_t, hT_t], writes=[ps_t])
            return ps, ps_t

        for g in range(4):
            ch0 = g * 256
            for cc in range(2):
                chunk = g * 2 + cc
                ws = []
                for sec in range(2):
                    if g == 0:
                        ws.append(pre_w[(cc, sec)])
                    elif cc == 0:
                        ws.append(shared["ws_next"][sec])
                    else:
                        ws.append(b.load_w(I["w_in"], sec * D + chunk * 128, 128))

                def P(sec, tb, cc=cc, chunk=chunk, ws=ws):
                    w, w_t = ws[sec]
                    acc, acc_t = (cA, cA_t) if sec == 0 else (cB, cB_t)
                    ed, ed_t = edgs[sec]
                    ci = sec * 8 + chunk
                    ps, ps_t = proj(w, w_t, tb * 512)
                    conv_block(ps, ps_t, acc, acc_t[tb], tb * 512, ci, 2 if tb < 2 else 1)
                    if tb >= 2:
                        b.op("dve", lambda e: e.tensor_copy(out=ed[:, 2 * (tb - 2):2 * (tb - 2) + 2], in_=ps[:, 0:512:511]),
                             reads=[ps_t], writes=[ed_t] if tb == 2 else [], pw=[] if tb == 2 else [ed_t])

                def U(tb, cc=cc):
                    first = (cc == 0 and tb == 0)
                    sl = slice(tb * 512, (tb + 1) * 512)
                    b.op("dve", lambda e: e.tensor_tensor(out=ufm[:, cc, sl], in0=cA[:, sl], in1=cB[:, sl], op=ALU.mult),
                         reads=[cA_t[tb], cB_t[tb]], writes=[ufm_t] if first else [], pw=[] if first else [ufm_t])

                def T(tb, cc=cc):
                    first = (cc == 0 and tb == 0)
                    ps, ps_t = b.psum()
                    psb = ps[:, :].bitcast(BF16)
                    for j in range(4):
                        tt = tb * 4 + j
                        b.op("pe", lambda e: e.transpose(psb[:, j * 128:(j + 1) * 128], ufm[:, cc, tt * 128:(tt + 1) * 128],
                                                          identb[:]),
                             reads=[ufm_t, identb_t], writes=[ps_t])
                    b.op("act", lambda e: e.copy(out=utm[:, tb * 4:tb * 4 + 4, cc * 128:(cc + 1) * 128],
                                                  in_=psb[:, 0:512].rearrange("p (j c) -> p j c", j=4)),
                         reads=[ps_t], writes=[utm_t] if first else [], pw=[] if first else [utm_t])

                P(0, 0); P(1, 0); U(0)
                P(0, 1); P(1, 1); U(1)
                for tfn in pend_T:
                    tfn()
                pend_T.clear()
                P(0, 2); P(1, 2); T(0)
                P(0, 3); P(1, 3); T(1)
                for sec in range(2):
                    acc, acc_t = (cA, cA_t) if sec == 0 else (cB, cB_t)
                    ed, ed_t = edgs[sec]
                    ci = sec * 8 + chunk
                    fix(acc, acc_t[2], 1024, 0, ci, ed[:, 3:4], ed_t)
                    fix(acc, acc_t[2], 1535, 3, ci, ed[:, 2:3], ed_t)
                    fix(acc, acc_t[3], 1536, 2, ci, ed[:, 1:2], ed_t)
                    fix(acc, acc_t[3], 2047, 1, ci, ed[:, 0:1], ed_t)
                U(2); U(3)
                pend_T.extend([lambda T=T: T(2), lambda T=T: T(3)])
            if HY_STOP == 4:
                b.barrier(); b.es = old_es; return
            def X(tb, g=g, ccs=(0, 1)):
                for cc in ccs:
                    chunk = g * 2 + cc
                    w, w_t = xw[(cc, 2)]
                    ci = 16 + chunk
                    acc, acc_t = (cA, cA_t) if cc == 0 else (cB, cB_t)
                    ps, ps_t = proj(w, w_t, tb * 512)
                    conv_block(ps, ps_t, acc, acc_t[tb], tb * 512, ci, 2 if tb < 2 else 1)
                    if tb == 2:
                        pse, pse_t = b.psum()
                        for kc in range(8):
                            b.op("pe", lambda e: e.matmul(pse[:, 0:2], lhsT=w[:, kc, 0:128], rhs=hT[:, kc, OTH0:OTH0 + 512:511],
                                                           start=(kc == 0), stop=(kc == 7)),
                                 reads=[w_t, hT_t], writes=[pse_t])
                        fix(acc, acc_t[2], 1024, 0, ci, pse[:, 1:2], pse_t)
                        fix(acc, acc_t[2], 1535, 3, ci, pse[:, 0:1], pse_t)
                    w, w_t = xw[(cc, 3)]
                    ps, ps_t = proj(w, w_t, tb * 512)
                    sg, sg_t = sgs[cc]
                    b.op("act", lambda e: e.activation(out=sg[:], in_=ps[:, :], func=AF.Silu), reads=[ps_t], writes=[sg_t])
                    b.op("pool", lambda e: e.tensor_tensor(out=xg[:, cc, tb * 512:(tb + 1) * 512], in0=acc[:, tb * 512:(tb + 1) * 512],
                                                            in1=sg[:], op=ALU.mult),
                         reads=[acc_t[tb], sg_t], writes=[xg_t] if (cc == 0 and tb == 0) else [],
                         pw=[] if (cc == 0 and tb == 0) else [xg_t])

            xw = {}
            for cc in range(2):
                for sec in (2, 3):
                    xw[(cc, sec)] = b.load_w(I["w_in"], sec * D + (g * 2 + cc) * 128, 128)
            if g < 3:
                shared["ws_next"] = [b.load_w(I["w_in"], sec * D + ((g + 1) * 2) * 128, 128) for sec in range(2)]
            if g == 3 and HY_STOP == 0:
                shared["W0"] = [b.load_w(I["w_in"], (4 + i) * D, 128) for i in range(4)]
            def spectral(Ups, Hre, Him, H_t, Yre, Yim, Y_t, first, accum, eng="dve", ceng="pool"):
                terms_re = []
                terms_im = []
                for (ps, ps_t, hre, him, h_t) in accum:
                    ure, uim = ps[:, 0:256], ps[:, 256:512]
                    terms_re += [(ure, hre, ps_t, h_t, 1.0), (uim, him, ps_t, h_t, -1.0)]
                    terms_im += [(ure, him, ps_t, h_t, 1.0), (uim, hre, ps_t, h_t, 1.0)]
                for terms, Yo in ((terms_re, Yre), (terms_im, Yim)):
                    acc = None
                    for i, (u, h, u_t, h_t, sgn) in enumerate(terms):
                        t, t_t = tmp(eng)
                        b.op(eng, lambda e: e.tensor_tensor(out=t[:], in0=u, in1=h, op=ALU.mult),
                             reads=[u_t, h_t], writes=[t_t])
                        if acc is None:
                            acc = (t, t_t)
                            continue
                        last = (i == len(terms) - 1)
                        op = ALU.add if sgn > 0 else ALU.subtract
                        if last:
                            b.op(ceng, lambda e: e.tensor_tensor(out=Yo, in0=acc[0][:], in1=t[:], op=op),
                                 reads=[acc[1], t_t], pw=[Y_t])
                        else:
                            n, n_t = tmp(eng)
                            b.op(ceng, lambda e: e.tensor_tensor(out=n[:], in0=acc[0][:], in1=t[:], op=op),
                                 reads=[acc[1], t_t], writes=[n_t])
                            acc = (n, n_t)

            def fwd_dft(tab, tab_t, nk, tc0, ci_re, ci_im):
                ps, ps_t = b.psum()
                for part, ci in ((0, ci_re), (1, ci_im)):
                    for kc in range(nk):
                        b.op("pe", lambda e: e.matmul(ps[:, part * 256:(part + 1) * 256], lhsT=tab[:, kc, ci * 128:(ci + 1) * 128],
                                                       rhs=utm[:, tc0 + kc, :], start=(kc == 0), stop=(kc == nk - 1)),
                             reads=[tab_t, utm_t], writes=[ps_t])
                return ps, ps_t

            def epilogue(ps, ps_t, off, cc, tok0, n, g=g):
                chunk = g * 2 + cc
                epi[0] += 1
                ep, ep_t = eps[epi[0] % 4]
                b.op("dve", lambda e: e.scalar_tensor_tensor(out=ep[:, 0:n], in0=ufm[:, cc, tok0:tok0 + n],
                                                              scalar=par[:, P_HD + chunk:P_HD + chunk + 1],
                                                              in1=ps[:, off:off + n], op0=ALU.mult, op1=ALU.add),
                     reads=[ufm_t, par_t, ps_t], writes=[ep_t])
                b.op("pool", lambda e: e.tensor_tensor(out=yg[:, chunk, tok0:tok0 + n], in0=ep[:, 0:n],
                                                        in1=xg[:, cc, tok0:tok0 + n], op=ALU.mult),
                     reads=[ep_t, xg_t], pw=[yg_t])

            def F(bb):
                Y4, Y4_t = Y4s[bb % 2]
                for ip in range(2):
                    ps, ps_t = fwd_dft(fwd256, fwd256_t, 2, 2 * bb, ip, ip + 2)
                    spectral(None, None, None, None, Y4[:, ip, :], Y4[:, ip + 2, :], Y4_t, True,
                             [(ps, ps_t, h256[:, ip, ch0:ch0 + 256], h256[:, ip + 2, ch0:ch0 + 256], h256_t)])

            def Iv(bb):
                Y4, Y4_t = Y4s[bb % 2]
                ps, ps_t = b.psum()
                for cc in range(2):
                    for ci in range(4):
                        b.op("pe", lambda e: e.matmul(ps[:, cc * 256:(cc + 1) * 256], lhsT=Y4[:, ci, cc * 128:(cc + 1) * 128],
                                                       rhs=inv256[:, ci, :], start=(ci == 0), stop=(ci == 3)),
                             reads=[Y4_t, inv256_t], writes=[ps_t])
                for cc in range(2):
                    epilogue(ps, ps_t, cc * 256, cc, bb * 256, 256)

            def FS(ip):
                ps1, ps1_t = fwd_dft(fwd512, fwd512_t, 4, 8, ip, ip + 4)
                ps2, ps2_t = fwd_dft(fwd512, fwd512_t, 4, 12, ip, ip + 4)
                spectral(None, None, None, None, Y8[:, ip, :], Y8[:, ip + 4, :], Y8_t, True,
                         [(ps1, ps1_t, hoo[:, ip, ch0:ch0 + 256], hoo[:, ip + 4, ch0:ch0 + 256], hoo_t),
                          (ps2, ps2_t, hox[:, ip, ch0:ch0 + 256], hox[:, ip + 4, ch0:ch0 + 256], hox_t)])

            def IS(epilogue=epilogue):
                for cc in range(2):
                    ps, ps_t = b.psum()
                    for ci in range(8):
                        b.op("pe", lambda e: e.matmul(ps[:, :], lhsT=Y8[:, ci, cc * 128:(cc + 1) * 128], rhs=inv512[:, ci, :],
                                                       start=(ci == 0), stop=(ci == 7)),
                             reads=[Y8_t, inv512_t], writes=[ps_t])
                    epilogue(ps, ps_t, 0, cc, 1024, 512)

            X(0)
            for tfn in pend_T:
                tfn()
            pend_T.clear()
            F(0); F(1); Iv(0); F(2); Iv(1); X(1); F(3); Iv(2); FS(0); Iv(3); FS(1); X(2, ccs=(0,)); FS(2); X(2, ccs=(1,)); FS(3); IS()
            if HY_STOP == 8:
                b.barrier(); b.es = old_es; return
        b.barrier()
        b.es = old_es


def interleave_gen(*gens):
    gens = list(gens)
    while gens:
        for g in list(gens):
            try:
                next(g)
            except StopIteration:
                gens.remove(g)
        yield


def interleave(*gens):
    gens = list(gens)
    while gens:
        for g in list(gens):
            try:
                next(g)
            except StopIteration:
                gens.remove(g)


def rr(lst, state=[0]):
    state[0] += 1
    return lst[state[0] % len(lst)]


def phase_filters(b, I, par, par_t, h256, h256_t, hoo, hoo_t, hox, hox_t, dft, ada_pre):
    nc = b.nc
    with ExitStack() as es:
        old_es = b.es
        b.es = es
        fw1, fw1_t = b.sb("fw1", [33, 64], F32)
        fw2, fw2_t = b.sb("fw2", [64, 64], F32)
        w3, w3_t = b.sb("w3", [64, 3 * D], BF16)
        fb, fb_t = b.sb("fb", [64, 4], F32)
        b.dma("sp", fw1[:], I["fw1"][:, :], writes=[fw1_t])
        b.dma("sp", fw2[:], I["fw2"][:, :], writes=[fw2_t])
        b.dma("pool", w3[:], I["w3c"][:, :], writes=[w3_t])
        zts = {}
        for L in (256, 1024):
            zts[L] = b.sb("zt%d" % L, [33, L], F32)
            b.dma("sp", zts[L][0][:], I["zt%d" % L][:, :], writes=[zts[L][1]])
        absd, absd_t = b.sb("absd", [128, D], F32)
        b.dma("sp", absd[:], I["absd"][0:1, :].to_broadcast([128, D]), writes=[absd_t])
        negt, negt_t = b.sb("negt", [128, 10], F32)
        b.dma("sp", negt[:], I["negt"][:, :], writes=[negt_t])
        for name in ("fwd256",):
            b.dma("sp", dft[name][0][:], I[name].rearrange("(c p) n -> p c n", p=128), writes=[dft[name][1]])
        fpar = par[0:64, P_FPAR:P_FPAR + 4]
        for l in range(2):
            b.op("dve", lambda e, l=l: e.tensor_scalar(out=fb[:, l:l + 1], in0=fpar[:, l:l + 1],
                                                        scalar1=fpar[:, 2 + l:3 + l], scalar2=None,
                                                        op0=ALU.mult),
                 reads=[par_t], writes=[fb_t])

        def load_tab(name, rows, cols):
            t, tk = b.sb(name, [128, rows // 128, cols], BF16)
            b.dma("sp", t[:], I[name].rearrange("(c p) n -> p c n", p=128), writes=[tk])
            return t, tk
        fwd256, fwd256_t = dft["fwd256"]
        fwd512, fwd512_t = dft["fwd512"]
        bwd256, bwd256_t = load_tab("bwd256", 256, 512)
        b.dma("sp", dft["fwd512"][0][:], I["fwd512"].rearrange("(c p) n -> p c n", p=128), writes=[dft["fwd512"][1]])
        bwd512, bwd512_t = load_tab("bwd512", 512, 1024)
        x1024, x1024_t = load_tab("x1024", 1024, 1024)
        for jb in range(4):
            ada_pre.append(b.load_w(I["w_ada"], jb * 512, 512))
        for name in ("inv256", "inv512"):
            b.dma("sp", dft[name][0][:], I[name].rearrange("(c p) n -> p c n", p=128), writes=[dft[name][1]])

        for L, ztn, decn in ((256, "zt256", "dec256"), (1024, "zt1024", "dec1024")):
            with ExitStack() as es2:
                b.es = es2
                zt, zt_t = zts[L]
                hid = []
                for l in range(2):
                    hid.append(b.sb("hid%d_%d" % (l, L), [64, L], F32))
                hidb, hidb_t = b.sb("hidb%d" % L, [64, L], BF16)
                tmp, tmp_t = b.sb("ftmp%d" % L, [64, 512], F32)
                tmpi, tmpi_t = b.sb("ftmpi%d" % L, [64, 512], mybir.dt.int32)
                tmpf, tmpf_t = b.sb("ftmpf%d" % L, [64, 512], F32)
                nblk = max(1, L // 512)
                bw = min(L, 512)
                for l in range(2):
                    src, src_t = (zt, zt_t) if l == 0 else hid[0]
                    wl, wl_t = (fw1, fw1_t) if l == 0 else (fw2, fw2_t)
                    K = 33 if l == 0 else 64
                    dst, dst_t = hid[l]
                    for blk in range(nblk):
                        ps, ps_t = b.psum()
                        sl = slice(blk * bw, (blk + 1) * bw)
                        b.op("pe", lambda e: e.matmul(ps[0:64, 0:bw], lhsT=wl[0:K, :], rhs=src[0:K, sl],
                                                       start=True, stop=True),
                             reads=[wl_t, src_t], writes=[ps_t])
                        b.op("dve", lambda e: e.tensor_scalar(out=tmp[:, 0:bw], in0=ps[0:64, 0:bw],
                                                               scalar1=fpar[:, 2 + l:3 + l], scalar2=fb[:, l:l + 1],
                                                               op0=ALU.mult, op1=ALU.add),
                             reads=[ps_t, par_t, fb_t], writes=[tmp_t])
                        b.op("dve", lambda e: e.tensor_scalar(out=tmpi[:, 0:bw], in0=tmp[:, 0:bw],
                                                               scalar1=1.0 / TWO_PI, scalar2=None, op0=ALU.mult),
                             reads=[tmp_t], writes=[tmpi_t])
                        b.op("dve", lambda e: e.tensor_copy(out=tmpf[:, 0:bw], in_=tmpi[:, 0:bw]),
                             reads=[tmpi_t], writes=[tmpf_t])
                        b.op("dve", lambda e: e.scalar_tensor_tensor(out=tmp[:, 0:bw], in0=tmpf[:, 0:bw], scalar=-TWO_PI,
                                                                      in1=tmp[:, 0:bw], op0=ALU.mult, op1=ALU.add),
                             reads=[tmpf_t, tmp_t], writes=[tmp_t])
                        b.op("act", lambda e: e.activation(out=dst[:, sl], in_=tmp[:, 0:bw], func=AF.Sin),
                             reads=[tmp_t], writes=[dst_t])
                b.op("dve", lambda e: e.tensor_copy(out=hidb[:], in_=hid[1][0][:]), reads=[hid[1][1]], writes=[hidb_t])
                b.dump("hid%d" % L, hid[1][0][:], hid[1][1], [64, L])

                ntc = L // 128
                ntc_ab = ntc if L == 256 else 4
                taps, taps_t = b.sb("taps%d" % L, [128, ntc_ab, 2 * D], BF16)
                if L == 1024:
                    tapsx, tapsx_t = b.sb("tapsx%d" % L, [128, ntc, D], BF16)
                decs = [b.sb("dec%d_%d" % (L, i), [128, D], F32) for i in range(2)]
                for tc in range(ntc):
                    dec, dec_t = decs[tc % 2]
                    tcol = tc if L == 256 else 2 + tc
                    b.op("act", lambda e: e.activation(out=dec[:], in_=absd[:], func=AF.Exp, scale=negt[:, tcol:tcol + 1]),
                         reads=[absd_t, negt_t], writes=[dec_t])
                    if L == 256:
                        colblks = [0, 1, 2, 3]
                    else:
                        colblks = ([0, 1, 2, 3] if tc < 4 else []) + [4, 5]
                    for cb in colblks:
                        ps, ps_t = b.psum()
                        b.op("pe", lambda e: e.matmul(ps[:, :], lhsT=hidb[:, tc * 128:(tc + 1) * 128],
                                                       rhs=w3[:, cb * 512:(cb + 1) * 512], start=True, stop=True),
                             reads=[hidb_t, w3_t], writes=[ps_t])
                        dc = (cb % 2) * 512
                        if cb < 4:
                            dst, dst_t = taps[:, tc, cb * 512:(cb + 1) * 512], taps_t
                        else:
                            dst, dst_t = tapsx[:, tc, (cb - 4) * 512:(cb - 3) * 512], tapsx_t
                        b.op("dve", lambda e: e.scalar_tensor_tensor(out=dst, in0=dec[:, dc:dc + 512],
                                                                      scalar=0.05, in1=ps[:, :], op0=ALU.add, op1=ALU.mult),
                             reads=[ps_t, dec_t], writes=[dst_t])

                evac = ["act", "dve"]
                if L == 256:
                    jobs = [(h256, h256_t, 4, [(fwd256, fwd256_t, 0, 2), (bwd256, bwd256_t, 1, 2)])]
                else:
                    jobs = [(hoo, hoo_t, 8, [(fwd512, fwd512_t, 0, 4), (bwd512, bwd512_t, 1, 4)]),
                            (hox, hox_t, 8, [(x1024, x1024_t, 2, 8)])]
                k = 0
                for (H, H_t, ncoef, srcs) in jobs:
                    for ci in range(ncoef):
                        for cb in range(2):
                            ps, ps_t = b.psum()
                            mm = []
                            for (tab, tab_t, sec, ntcs) in srcs:
                                for tc in range(ntcs):
                                    mm.append((tab, tab_t, sec, tc))
                            for i, (tab, tab_t, sec, tc) in enumerate(mm):
                                b.op("pe", lambda e: e.matmul(
                                    ps[:, :], lhsT=tab[:, tc, ci * 128:(ci + 1) * 128],
                                    rhs=(taps[:, tc, sec * D + cb * 512: sec * D + (cb + 1) * 512] if sec < 2
                                         else tapsx[:, tc, cb * 512:(cb + 1) * 512]),
                                    start=(i == 0), stop=(i == len(mm) - 1)),
                                    reads=[tab_t, taps_t if sec < 2 else tapsx_t], writes=[ps_t])
                            k += 1
                            if k % 2 == 0:
                                b.op("act", lambda e: e.copy(out=H[:, ci, cb * 512:(cb + 1) * 512], in_=ps[:, :]),
                                     reads=[ps_t], writes=[H_t])
                            else:
                                b.op("dve", lambda e: e.tensor_copy(out=H[:, ci, cb * 512:(cb + 1) * 512], in_=ps[:, :]),
                                     reads=[ps_t], writes=[H_t])
                b.barrier(dma=False)
            b.es = es
        b.barrier(dma=False)
        b.es = old_es


def phase_attn(b, I, O, par, par_t, ident, ident_t, hT, hT_t, ya, ya_t, shared):
    with ExitStack() as es:
        old_es = b.es
        b.es = es
        cmask, cmask_t = b.sb("cmask", [128, 64], F32)
        rmask, rmask_t = b.sb("rmask", [128, 6, 8], BF16)
        b.dma("sp", cmask[:], I["cmask"][:, :], writes=[cmask_t])
        b.dma("pool", rmask[:], I["rmask"].rearrange("p (j r) -> p j r", j=6), writes=[rmask_t])
        vexts = [b.sb("vext%d" % i, [128, 16, 2, 128], BF16) for i in range(2)]
        for (v, v_t) in vexts:
            b.op("pool", lambda e: e.memset(v[:], 1.0), writes=[v_t])
        qTs = [b.sb("qT%d" % i, [128, NOWN], BF16) for i in range(2)]
        kTs = [b.sb("kT%d" % i, [128, 2048], BF16) for i in range(2)]
        kf, kf_t = b.sb("kf", [128, 1024], F32)
        ckst, ckst_t = b.sb("ckst", [128, 2, 128], F32)
        nkst, nkst_t = b.sb("nkst", [128, 8, 128], F32)
        nvst, nvst_t = b.sb("nvst", [128, 8, 128], F32)
        sgas = [b.sb("sga%d" % i, [128, NOWN], BF16) for i in range(2)]
        vsts = [b.sb("vst%d" % i, [128, 4, 128], F32) for i in range(2)]
        bts = [b.sb("bt%d" % i, [128, 24, 64], F32) for i in range(2)]
        ptss = [[b.sb("pt%d_%d" % (s2, i), [128, 512], BF16) for i in range(9)] for s2 in range(2)]
        Ts = [b.sb("T%d" % i, [128, 512], F32) for i in range(2)]
        rcs = [b.sb("rc%d" % i, [128, 512], F32) for i in range(4)]
        pti = [0, 0]

        def pt(s2):
            pti[s2] += 1
            return ptss[s2][pti[s2] % 9]

        def proj(w, w_t, col0, n=512):
            ps, ps_t = b.psum()
            for kc in range(8):
                b.op("pe", lambda e: e.matmul(ps[:, 0:n], lhsT=w[:, kc, 0:128], rhs=hT[:, kc, col0:col0 + n],
                                               start=(kc == 0), stop=(kc == 7)),
                     reads=[w_t, hT_t], writes=[ps_t])
            return ps, ps_t

        fin = [0]

        def finalize(pso, pso_t, e, hp, tok0, n, sga, sga_t):
            ob = 64 * e
            db = 64 * (1 - e)
            rc, rc_t = rcs[fin[0] % len(rcs)]
            fin[0] += 1
            b.op("dve", lambda en: en.tensor_copy(out=rc[ob:ob + 64, 0:n], in_=pso[db:db + 64, 0:n]),
                 reads=[pso_t], writes=[rc_t])
            b.op("act", lambda en: en.activation(out=rc[ob:ob + 64, 0:n], in_=rc[ob:ob + 64, 0:n], func=AF.Ln),
                 reads=[rc_t], writes=[rc_t])
            b.op("act", lambda en: en.activation(out=rc[ob:ob + 64, 0:n], in_=rc[ob:ob + 64, 0:n], func=AF.Exp, scale=-1.0),
                 reads=[rc_t], writes=[rc_t])
            b.op("dve", lambda en: en.tensor_tensor(out=rc[ob:ob + 64, 0:n], in0=pso[ob:ob + 64, 0:n],
                                                     in1=rc[ob:ob + 64, 0:n], op=ALU.mult),
                 reads=[pso_t, rc_t], writes=[rc_t])
            b.op("pool", lambda en: en.tensor_tensor(out=ya[ob:ob + 64, hp, tok0:tok0 + n], in0=rc[ob:ob + 64, 0:n],
                                                      in1=sga[ob:ob + 64, tok0:tok0 + n], op=ALU.mult),
                 reads=[rc_t, sga_t], pw=[ya_t])

        def load_pair(hp):
            return [b.load_w(I["w_in"], (4 + i) * D + hp * 128, 128) for i in range(4)]

        def bufs(hp):
            return qTs[hp % 2] + kTs[hp % 2] + sgas[hp % 2] + vexts[hp % 2]

        def proj_gen(hp, W):
            qT, qT_t, kT, kT_t, sga, sga_t, vext, vext_t = bufs(hp)
            (wq, wq_t), (wk, wk_t), (wv, wv_t), (wg, wg_t) = W
            for tb in range(3):
                ps, ps_t = proj(wq, wq_t, tb * 512)
                b.op("act", lambda e: e.activation(out=qT[:, tb * 512:(tb + 1) * 512], in_=ps[:, :], func=AF.Copy, scale=0.125),
                     reads=[ps_t], writes=[qT_t] if tb == 0 else [], pw=[] if tb == 0 else [qT_t])
                yield
            for i, (src0, n, dst0) in enumerate(((0, 512, 0), (512, 512, 512), (1024, 512, 1024), (2048, 256, 1536))):
                ps, ps_t = proj(wk, wk_t, src0, n)
                if i < 2:
                    b.op("act", lambda e: e.copy(out=kf[:, dst0:dst0 + n], in_=ps[:, 0:n]),
                         reads=[ps_t], writes=[kf_t] if i == 0 else [], pw=[] if i == 0 else [kf_t])
                    b.op("pool", lambda e: e.tensor_copy(out=kT[:, dst0:dst0 + n], in_=kf[:, dst0:dst0 + n]),
                         reads=[kf_t], writes=[kT_t] if i == 0 else [], pw=[] if i == 0 else [kT_t])
                else:
                    b.op("dve", lambda e: e.tensor_copy(out=kT[:, dst0:dst0 + n], in_=ps[:, 0:n]),
                         reads=[ps_t], pw=[kT_t])
                yield
            b.dma("sp", ckst[:], I["ck"][:, hp * 128:(hp + 1) * 128].rearrange("(c p) n -> p c n", p=128), writes=[ckst_t])
            ps, ps_t = b.psum()
            for c in range(2):
                b.op("pe", lambda e: e.transpose(ps[:, c * 128:(c + 1) * 128], ckst[:, c, :], ident[:]),
                     reads=[ckst_t, ident_t], writes=[ps_t])
            b.op("dve", lambda e: e.tensor_copy(out=kT[:, 1792:2048], in_=ps[:, 0:256]), reads=[ps_t], pw=[kT_t])
            for half2 in range(2):
                ps, ps_t = b.psum()
                for j in range(4):
                    tt = half2 * 4 + j
                    b.op("pe", lambda e: e.transpose(ps[:, j * 128:(j + 1) * 128], kf[:, tt * 128:(tt + 1) * 128], ident[:]),
                         reads=[kf_t, ident_t], writes=[ps_t])
                b.op("act", lambda e: e.copy(out=nkst[:, half2 * 4:half2 * 4 + 4, :],
                                              in_=ps[:, :].rearrange("p (j c) -> p j c", j=4)),
                     reads=[ps_t], writes=[nkst_t] if half2 == 0 else [], pw=[] if half2 == 0 else [nkst_t])
            b.dma("sp", O["nk"][:, hp * 128:(hp + 1) * 128].rearrange("(t p) c -> p t c", p=128), nkst[:],
                  reads=[nkst_t], is_output=True)
            cols = [t * 128 for t in range(12)] + [2048, 2176]
            for q4 in range(4):
                tl = cols[q4 * 4:q4 * 4 + 4]
                ps, ps_t = b.psum()
                for j, col in enumerate(tl):
                    for kc in range(8):
                        b.op("pe", lambda e: e.matmul(ps[:, j * 128:(j + 1) * 128], lhsT=hT[:, kc, col:col + 128],
                                                       rhs=wv[:, kc, 0:128], start=(kc == 0), stop=(kc == 7)),
                             reads=[hT_t, wv_t], writes=[ps_t])
                nj = len(tl)
                pv = ps[:, 0:nj * 128].rearrange("p (j c) -> p j c", j=nj)
                first = (q4 == 0)
                if q4 < 2:
                    st, st_t = nvst[:, q4 * 4:q4 * 4 + 4, :], nvst_t
                    b.op("act", lambda e: e.copy(out=st, in_=pv), reads=[ps_t],
                         writes=[nvst_t] if first else [], pw=[] if first else [nvst_t])
                else:
                    vs, st_t = vsts[q4 % 2]
                    st = vs[:, 0:nj, :]
                    b.op("act", lambda e: e.copy(out=st, in_=pv), reads=[ps_t], writes=[st_t])
                b.op("pool", lambda e: e.tensor_copy(out=vext[:, q4 * 4:q4 * 4 + nj, 0, 0:64], in_=st[:, :, 0:64]),
                     reads=[st_t], writes=[vext_t] if first else [], pw=[] if first else [vext_t])
                b.op("pool", lambda e: e.tensor_copy(out=vext[:, q4 * 4:q4 * 4 + nj, 1, 64:128], in_=st[:, :, 64:128]),
                     reads=[st_t], pw=[vext_t])
                yield
            b.dma("sp", O["nv"][:, hp * 128:(hp + 1) * 128].rearrange("(t p) c -> p t c", p=128), nvst[:],
                  reads=[nvst_t], is_output=True)
            cvv = I["cv"][:, hp * 128:(hp + 1) * 128].rearrange("(c p) n -> p c n", p=128)
            b.dma("pool", vext[:, 14:16, 0, 0:64], cvv[:, :, 0:64], pw=[vext_t])
            b.dma("pool", vext[:, 14:16, 1, 64:128], cvv[:, :, 64:128], pw=[vext_t])
            for tb in range(3):
                ps, ps_t = proj(wg, wg_t, tb * 512)
                b.op("act", lambda e: e.activation(out=sga[:, tb * 512:(tb + 1) * 512], in_=ps[:, :], func=AF.Silu),
                     reads=[ps_t], writes=[sga_t] if tb == 0 else [], pw=[] if tb == 0 else [sga_t])
                yield

        def attn_gen(hp):
            qT, qT_t, kT, kT_t, sga, sga_t, vext, vext_t = bufs(hp)
            if AT_STOP == 1:
                return
            def head_gen(e2, hp=hp, vext=vext, vext_t=vext_t):
                h = 2 * hp + e2
                pb = 64 * e2
                bt, bt_t = bts[h % 2]
                b.dma("sp", bt[:], I["btab"][h].rearrange("p (i c) -> p i c", i=24), writes=[bt_t])
                b.op("pool", lambda e: e.tensor_tensor(out=bt[:], in0=bt[:],
                                                        in1=cmask[:].unsqueeze(1).to_broadcast([128, 24, 64]), op=ALU.add),
                     reads=[bt_t, cmask_t], writes=[bt_t])
                def s_stage(bb):
                    ptl = []
                    for kc2 in range(2):
                        ch = 2 * bb + kc2
                        ps, ps_t = b.psum()
                        b.op("pe", lambda e: e.matmul(ps[:, 0:256], lhsT=kT[pb:pb + 64, ch * 128:(ch + 1) * 128],
                                                       rhs=qT[pb:pb + 64, bb * 256:(bb + 1) * 256], start=True, stop=True),
                             reads=[kT_t, qT_t], writes=[ps_t])
                        p, p_t = pt(e2)
                        b.op("act", lambda e: e.activation(out=p[:, 0:256], in_=ps[:, 0:256], func=AF.Exp),
                             reads=[ps_t], writes=[p_t])
                        ptl.append((p, p_t, ch))
                    return ptl

                def pv_stage(bb, ptl):
                    pso, pso_t = b.psum()
                    for i, (p, p_t, ch) in enumerate(ptl):
                        b.op("pe", lambda e: e.matmul(pso[:, 0:256], lhsT=vext[:, ch, e2, :], rhs=p[:, 0:256],
                                                       start=(i == 0), stop=(i == 1)),
                             reads=[vext_t, p_t], writes=[pso_t])
                    finalize(pso, pso_t, e2, hp, bb * 256, 256, sga, sga_t)

                prev = s_stage(0)
                yield
                for bb in range(1, 4):
                    cur = s_stage(bb)
                    pv_stage(bb - 1, prev)
                    prev = cur
                    yield
                pend = (3, prev)
                if AT_STOP == 2:
                    pv_stage(*pend)
                    return
                ptl = []
                for j in range(6):
                    kcol = 1024 + j * 128 if j < 4 else 1536 + (j - 4) * 128
                    i0 = (6 - 2 * j) if j < 4 else (14 + 2 - 2 * (j - 4))
                    ps, ps_t = b.psum()
                    b.op("pe", lambda e: e.matmul(ps[:, :], lhsT=kT[pb:pb + 64, kcol:kcol + 128],
                                                   rhs=qT[pb:pb + 64, 1024:1536], start=True, stop=True),
                         reads=[kT_t, qT_t], writes=[ps_t])
                    T, T_t = Ts[j % 2]
                    b.op("dve", lambda e: e.tensor_tensor(out=T[:].rearrange("p (r c) -> p r c", r=8),
                                                           in0=ps[:, :].rearrange("p (r c) -> p r c", r=8),
                                                           in1=bt[:, i0:i0 + 8, :], op=ALU.add),
                         reads=[ps_t, bt_t], writes=[T_t])
                    p, p_t = pt(e2)
                    b.op("act", lambda e: e.activation(out=p[:], in_=T[:], func=AF.Exp), reads=[T_t], writes=[p_t])
                    b.op("pool", lambda e: e.tensor_tensor(out=p[:].rearrange("p (r c) -> p r c", r=8),
                                                            in0=p[:].rearrange("p (r c) -> p r c", r=8),
                                                            in1=rmask[:, j, :].unsqueeze(2).to_broadcast([128, 8, 64]), op=ALU.mult),
                         reads=[p_t, rmask_t], writes=[p_t])
                    ptl.append((p, p_t, 8 + j))
                    if pend is not None:
                        pv_stage(*pend)
                        pend = None
                    yield
                for c in range(2):
                    ps, ps_t = b.psum()
                    b.op("pe", lambda e: e.matmul(ps[:, :], lhsT=kT[pb:pb + 64, 1792 + c * 128:1792 + (c + 1) * 128],
                                                   rhs=qT[pb:pb + 64, 1024:1536], start=True, stop=True),
                         reads=[kT_t, qT_t], writes=[ps_t])
                    p, p_t = pt(e2)
                    b.op("act", lambda e: e.activation(out=p[:], in_=ps[:, :], func=AF.Exp), reads=[ps_t], writes=[p_t])
                    ptl.append((p, p_t, 14 + c))
                    yield
                pso, pso_t = b.psum()
                for i, (p, p_t, ch) in enumerate(ptl):
                    b.op("pe", lambda e: e.matmul(pso[:, :], lhsT=vext[:, ch, e2, :], rhs=p[:], start=(i == 0), stop=(i == 7)),
                         reads=[vext_t, p_t], writes=[pso_t])
                finalize(pso, pso_t, e2, hp, 1024, 512, sga, sga_t)
            yield from interleave_gen(head_gen(0), head_gen(1))

        W = {0: shared.get("W0") or load_pair(0)}
        if AT_NHP > 1:
            W[1] = load_pair(1)
        interleave(proj_gen(0, W[0]))
        for hp in range(AT_NHP):
            if hp + 2 < AT_NHP:
                W[hp + 2] = load_pair(hp + 2)
            if hp == AT_NHP - 1 and AT_NHP == 8:
                shared["M0"] = [b.load_w(I["w_bh"], 0, 128), b.load_w(I["w_in"], 8 * D, 128),
                                b.load_w(I["w_ba"], 0, 128), b.load_w(I["w_in"], 9 * D, 128)]
                shared["M1"] = [b.load_w(I["w_bh"], 128, 128), b.load_w(I["w_in"], 8 * D + 128, 128),
                                b.load_w(I["w_ba"], 128, 128), b.load_w(I["w_in"], 9 * D + 128, 128)]
            gens = [attn_gen(hp)]
            if hp + 1 < AT_NHP:
                gens.append(proj_gen(hp + 1, W[hp + 1]))
            interleave(*gens)
        b.barrier()
        b.es = old_es


def phase_merge(b, I, hT, hT_t, yg, yg_t, ya, ya_t, mg, mg_t, shared):
    with ExitStack() as es:
        old_es = b.es
        b.es = es
        sgs = [b.sb("msg%d" % i, [128, 512], F32) for i in range(4)]
        t1s = [b.sb("mt%d" % i, [128, 512], F32) for i in range(4)]
        k = 0
        def load_oc(oc):
            return [b.load_w(I["w_bh"], oc * 128, 128), b.load_w(I["w_in"], 8 * D + oc * 128, 128),
                    b.load_w(I["w_ba"], oc * 128, 128), b.load_w(I["w_in"], 9 * D + oc * 128, 128)]
        nxt = shared.get("M0") or load_oc(0)
        for oc in range(8):
            (wbh, wbh_t), (wmh, wmh_t), (wba, wba_t), (wma, wma_t) = nxt
            if oc + 1 < 8:
                nxt = shared["M1"] if (oc == 0 and "M1" in shared) else load_oc(oc + 1)
            for tb in range(3):
                prods = []
                for (wp, wp_t, src, src_t, wm, wm_t) in ((wbh, wbh_t, yg, yg_t, wmh, wmh_t), (wba, wba_t, ya, ya_t, wma, wma_t)):
                    psm, psm_t = b.psum()
                    for kc in range(8):
                        b.op("pe", lambda e: e.matmul(psm[:, :], lhsT=wm[:, kc, 0:128], rhs=hT[:, kc, tb * 512:(tb + 1) * 512],
                                                       start=(kc == 0), stop=(kc == 7)),
                             reads=[wm_t, hT_t], writes=[psm_t])
                    sg, sg_t = sgs[k % 4]
                    b.op("act", lambda e: e.activation(out=sg[:], in_=psm[:, :], func=AF.Sigmoid), reads=[psm_t], writes=[sg_t])
                    psp, psp_t = b.psum()
                    for kc in range(8):
                        b.op("pe", lambda e: e.matmul(psp[:, :], lhsT=wp[:, kc, 0:128], rhs=src[:, kc, tb * 512:(tb + 1) * 512],
                                                       start=(kc == 0), stop=(kc == 7)),
                             reads=[wp_t, src_t], writes=[psp_t])
                    t1, t1_t = t1s[k % 4]
                    k += 1
                    b.op("dve", lambda e: e.tensor_tensor(out=t1[:], in0=psp[:, :], in1=sg[:], op=ALU.mult),
                         reads=[psp_t, sg_t], writes=[t1_t])
                    prods.append((t1, t1_t))
                b.op("pool", lambda e: e.tensor_tensor(out=mg[:, oc, tb * 512:(tb + 1) * 512], in0=prods[0][0][:],
                                                        in1=prods[1][0][:], op=ALU.add),
                     reads=[prods[0][1], prods[1][1]], pw=[mg_t])
        b.barrier()
        b.es = old_es


def phase_final(b, I, O, par, par_t, mg, mg_t, pre):
    nc = b.nc
    with ExitStack() as es:
        old_es = b.es
        b.es = es
        rows, rows_t = pre["rows"]
        grow, grow_t = b.sb("grow", [128, 2, D], F32)
        with ExitStack() as es2:
            b.es = es2
            brow, brow_t = pre["brow"]
            sc, sc_t = b.sb("sc", [128, 16], F32)
            screp, screp_t = b.sb("screp", [128, 2, 8, 128], BF16)
            b.op("act", lambda e: e.activation(out=sc[:], in_=par[:, P_COND:P_COND + 16], func=AF.Silu),
                 reads=[par_t], writes=[sc_t])
            for w in range(2):
                b.op("dve", lambda e: e.tensor_copy(out=screp[:, w, :, :],
                                                     in_=sc[:, w:16:2].unsqueeze(2).to_broadcast([128, 8, 128])),
                     reads=[sc_t], pw=[screp_t])
            for cb in range(2):
                wa, wa_t = pre["wa"][cb]
                for w in range(2):
                    ps, ps_t = b.psum()
                    for kc in range(8):
                        b.op("pe", lambda e: e.matmul(ps[:, :], lhsT=screp[:, w, kc, :], rhs=wa[:, kc, :],
                                                       start=(kc == 0), stop=(kc == 7)),
                             reads=[screp_t, wa_t], writes=[ps_t])
                    b.op("dve", lambda e: e.tensor_tensor(out=grow[:, w, cb * 512:(cb + 1) * 512], in0=ps[:, :],
                                                           in1=brow[:, cb * 512:(cb + 1) * 512], op=ALU.add),
                         reads=[ps_t, brow_t], pw=[grow_t])
            b.barrier()
        b.es = es
        wo = pre["wo"]
        xts = [b.sb("fx%d" % i, [128, D], F32) for i in range(3)]
        rts = [b.sb("fr%d" % i, [128, D], F32) for i in range(4)]
        FMAX = int(nc.vector.BN_STATS_FMAX)
        SD = int(nc.vector.BN_STATS_DIM)
        AD = int(nc.vector.BN_AGGR_DIM)
        nst = (D + FMAX - 1) // FMAX
        statss = [b.sb("stats%d" % i, [128, nst, SD], F32) for i in range(4)]
        mvs = [b.sb("mv%d" % i, [128, AD], F32) for i in range(4)]
        rstds = [b.sb("rstd%d" % i, [128, 1], F32) for i in range(4)]
        for tt in range(12):
            w = 0 if tt < 8 else 1
            xt, xt_t = xts[tt % 3]
            rt, rt_t = rts[tt % 4]
            stats, stats_t = statss[tt % 4]
            mv, mv_t = mvs[tt % 4]
            rstd, rstd_t = rstds[tt % 4]
            if tt == 0:
                for t0 in range(2):
                    b.dma("sp", xts[t0][0][:], I["xall"][t0 * 128:(t0 + 1) * 128, :], writes=[xts[t0][1]])
            if tt + 2 < 12:
                nx, nx_t = xts[(tt + 2) % 3]
                b.dma("sp", nx[:], I["xall"][(tt + 2) * 128:(tt + 3) * 128, :], writes=[nx_t])
            for cb in range(2):
                ps, ps_t = b.psum()
                for kc in range(8):
                    b.op("pe", lambda e: e.matmul(ps[:, :], lhsT=mg[:, kc, tt * 128:(tt + 1) * 128], rhs=wo[cb][0][:, kc, :],
                                                   start=(kc == 0), stop=(kc == 7)),
                         reads=[mg_t, wo[cb][1]], writes=[ps_t])
                b.op("dve", lambda e: e.tensor_tensor(out=rt[:, cb * 512:(cb + 1) * 512], in0=ps[:, :],
                                                       in1=grow[:, w, cb * 512:(cb + 1) * 512], op=ALU.mult),
                     reads=[ps_t, grow_t], writes=[rt_t] if cb == 0 else [], pw=[] if cb == 0 else [rt_t])
            b.op("dve", lambda e: e.scalar_tensor_tensor(out=rt[:], in0=xt[:], scalar=ALPHA, in1=rt[:],
                                                          op0=ALU.mult, op1=ALU.add),
                 reads=[xt_t, rt_t], writes=[rt_t])
            for c in range(nst):
                lo = c * FMAX
                hi = min(D, lo + FMAX)
                b.op("dve", lambda e: e.bn_stats(out=stats[:, c, :], in_=rt[:, lo:hi]),
                     reads=[rt_t], writes=[stats_t] if c == 0 else [], pw=[] if c == 0 else [stats_t])
            b.op("dve", lambda e: e.bn_aggr(out=mv[:], in_=stats[:]), reads=[stats_t], writes=[mv_t])
            b.op("dve", lambda e: e.tensor_scalar(out=rstd[:], in0=mv[:, 1:2], scalar1=LN_EPS, scalar2=None, op0=ALU.add),
                 reads=[mv_t], writes=[rstd_t])
            b.op("act", lambda e: e.sqrt(out=rstd[:], in_=rstd[:]), reads=[rstd_t], writes=[rstd_t])
            b.op("dve", lambda e: e.reciprocal(out=rstd[:], in_=rstd[:]), reads=[rstd_t], writes=[rstd_t])
            b.op("dve", lambda e: e.tensor_scalar(out=rt[:], in0=rt[:], scalar1=mv[:, 0:1], scalar2=rstd[:, 0:1],
                                                   op0=ALU.subtract, op1=ALU.mult),
                 reads=[rt_t, mv_t, rstd_t], writes=[rt_t])
            b.op("pool", lambda e: e.tensor_tensor(out=rt[:], in0=rt[:], in1=rows[:, 0, :], op=ALU.mult),
                 reads=[rt_t, rows_t], writes=[rt_t])
            b.op("pool", lambda e: e.tensor_tensor(out=rt[:], in0=rt[:], in1=rows[:, 1, :], op=ALU.add),
                 reads=[rt_t, rows_t], writes=[rt_t])
            b.dma("pool", O["y"][tt * 128:(tt + 1) * 128, :], rt[:], reads=[rt_t], is_output=True)
        b.barrier()
        b.es = old_es


def _attn_tabs(half, rpb):
    kc = np.arange(64)[:, None]
    col = np.arange(64)[None, :]
    dcidx = np.clip(kc - col + 15, 0, 30)
    bt = np.zeros((16, 128, 24, 64), np.float32)
    for e in range(2):
        for ip in range(14):
            dr = (13 - ip) - 7 + e
            if -7 <= dr <= 7:
                bt[:, e * 64:(e + 1) * 64, ip, :] = rpb[:, dr + 7][:, dcidx]
        D0 = 8 if half == 0 else -4
        for ip in range(10):
            dr = D0 + (9 - ip) - 7 + e
            if -7 <= dr <= 7:
                bt[:, e * 64:(e + 1) * 64, 14 + ip, :] = rpb[:, dr + 7][:, dcidx]
    c0 = np.clip(col - 8, 0, 48)
    inwin = (kc >= c0) & (kc < c0 + 16)
    cm = np.where(inwin, 0.0, -30000.0).astype(np.float32)
    cmask = np.concatenate([cm, cm], 0)
    rmask = np.zeros((128, 6, 8), np.float32)
    for j in range(6):
        for e in range(2):
            if half == 0:
                kr = 2 * j + e if j < 4 else 8 + 2 * (j - 4) + e
            else:
                kr = 8 + 2 * j + e if j < 4 else 4 + 2 * (j - 4) + e
            for rl in range(8):
                r = rl if half == 0 else 8 + rl
                r0 = min(max(r - 4, 0), 8)
                rmask[e * 64:(e + 1) * 64, j, rl] = 1.0 if (r0 <= kr < r0 + 8) else 0.0
    return bt.reshape(16, 128, 24 * 64), cmask, rmask.reshape(128, 48)

def _core_inputs(j, inp):
    bs, half = j // 2, j % 2
    c = _consts(half)
    xs = inp["x_sample"][bs]
    own = xs[half * 512:(half + 1) * 512]
    oth = xs[(1 - half) * 512:(2 - half) * 512]
    halo = xs[512:768] if half == 0 else xs[256:512]
    xall = np.concatenate([inp["x_prompt"][4 * j:4 * j + 4].reshape(1024, D), own, oth, halo], 0)
    par = np.zeros((128, P_NP), np.float32)
    cond = np.stack([inp["c_ctx"], inp["c"][bs]], 0)
    par[:, P_COND:P_COND + 16] = cond.reshape(2, 8, 128).transpose(2, 1, 0).reshape(128, 16)
    par[:, P_BADA:P_BADA + 24] = inp["b_ada"][0].reshape(24, 128).T
    par[:, P_CW:P_CW + 72] = inp["conv_w"][0].reshape(3, 24, 128).transpose(2, 0, 1).reshape(128, 72)
    par[:, P_CB:P_CB + 24] = inp["conv_b"][0].reshape(24, 128).T
    par[:, P_HD:P_HD + 8] = inp["hyena_d"][0].reshape(8, 128).T
    par[:, P_FLAG] = 1.0 if half == 1 else 0.0
    par[:, P_FLAG + 1] = 1.0 if half == 0 else 0.0
    par[0:64, P_FPAR + 0] = inp["filt_b1"][0]
    par[0:64, P_FPAR + 1] = inp["filt_b2"][0]
    par[0:64, P_FPAR + 2] = inp["filt_freq"][0, 0]
    par[0:64, P_FPAR + 3] = inp["filt_freq"][0, 1]
    w3 = inp["filt_w3"][0]
    w3c = np.concatenate([w3, w3[:, :D] if half == 1 else w3[:, D:]], 1)
    m = {
        "xall": xall, "params": par,
        "ck": inp["cache_k"][bs, 0].reshape(256, D), "cv": inp["cache_v"][bs, 0].reshape(256, D),
        "w_ada": inp["w_ada"][0], "w_in": inp["w_in"][0], "w_bh": inp["w_bh"][0], "w_ba": inp["w_ba"][0],
        "w_out": inp["w_out"][0], "fw1": inp["filt_w1"][0], "fw2": inp["filt_w2"][0], "w3c": w3c,
        "rows": np.stack([inp["b_ada"][0, 2 * D:], inp["ln_g"][0], inp["ln_b"][0]], 0),
    }
    m["btab"], m["cmask"], m["rmask"] = _attn_tabs(half, inp["rpb"][0])
    m.update(c)
    out = {}
    for k, v in m.items():
        if k in ("fwd256", "bwd256", "inv256", "fwd512", "bwd512", "inv512", "x1024"):
            out[k] = np.ascontiguousarray(v)
        else:
            out[k] = np.ascontiguousarray(v, dtype=np.float32)
    return out


_PROG = {}


def kernel(**inputs):
    inp = {k: np.asarray(v) for k, v in inputs.items()}
    if "nc" not in _PROG:
        _PROG["nc"], _PROG["b"] = build_program()
    nc = _PROG["nc"]
    in_maps = [_core_inputs(j, inp) for j in range(NCORES)]
    res = run_bass_kernel_spmd(nc, in_maps, core_ids=list(range(NCORES)))
    _PROG["last"] = res
    y_p = np.zeros((32, 256, D), np.float32)
    y_s = np.zeros((4, 1024, D), np.float32)
    nk = np.zeros((32, 1, 256, 16, 64), np.float32)
    nv = np.zeros((32, 1, 256, 16, 64), np.float32)
    for j in range(NCORES):
        r = res.results[j]
        bs, half = j // 2, j % 2
        y_p[4 * j:4 * j + 4] = r["y"][:1024].reshape(4, 256, D)
        y_s[bs, half * 512:(half + 1) * 512] = r["y"][1024:]
        nk[4 * j:4 * j + 4, 0] = r["nk"].reshape(4, 256, 16, 64)
        nv[4 * j:4 * j + 4, 0] = r["nv"].reshape(4, 256, 16, 64)
    return y_p, y_s, nk, nv
```

```python
import math
from contextlib import ExitStack

import numpy as np
import concourse.bass as bass
import concourse.mybir as mybir
from concourse.bass_utils import run_bass_kernel_spmd

F32 = mybir.dt.float32
BF16 = mybir.dt.bfloat16
AF = mybir.ActivationFunctionType
ALU = mybir.AluOpType

D = 1024
NCORES = 8
PR0, OWN0, OTH0, HALO0, NTOK = 0, 1024, 1536, 2048, 2304
NOWN = 1536
ALPHA = 2.0 ** 0.25
LN_EPS = 1e-5
TWO_PI = 2.0 * math.pi

P_COND, P_BADA, P_CW, P_CB, P_HD, P_FLAG, P_FPAR, P_NP = 0, 16, 40, 112, 136, 144, 146, 150

class StopBuild(Exception):
    pass


HY_STOP = 0
AT_STOP = 0
SAME_ENGINE_NOSYNC = ("pe",)
AT_NHP = 8
AT_VAR = 0
DEBUG = {}
STOP_AFTER = None


class Tok:
    __slots__ = ("w", "r", "x")

    def __init__(self):
        self.w = {}
        self.r = {}
        self.x = {}


class Bld:
    def __init__(self, nc, es, n_dma=32):
        self.nc = nc
        self.es = es
        self.engs = {"pe": nc.tensor, "act": nc.scalar, "dve": nc.vector, "pool": nc.gpsimd, "sp": nc.sync}
        self.sems = {}
        self.cnt = {}
        for e in ("pe", "act", "dve", "pool"):
            self.sems[e] = es.enter_context(nc.semaphore("s_" + e))
            self.cnt[e] = 0
        self.dcnt = []
        for i in range(n_dma):
            self.sems[("d", i)] = es.enter_context(nc.semaphore("s_d%d" % i))
            self.dcnt.append(0)
        self.drr = 0
        self.drr_sw = 0
        self.waited = {e: {} for e in self.engs}
        self.out_events = {}
        self.ps = []
        for i in range(8):
            t = es.enter_context(nc.psum_tensor("psb%d" % i, [128, 512], F32))
            self.ps.append((t, Tok()))
        self.psi = 0
        self.dbg = {}

    def psum(self):
        t = self.ps[self.psi]
        self.psi = (self.psi + 1) % 8
        return t

    def _deps(self, reads, writes, pw=()):
        deps = {}

        def add(d):
            for sk, v in d.items():
                if deps.get(sk, 0) < v:
                    deps[sk] = v
        for t in reads:
            add(t.w)
        for t in writes:
            add(t.w)
            add(t.r)
        for t in pw:
            add(t.r)
            add(t.x)
        return deps

    def _wait(self, eng, deps):
        for sk, v in deps.items():
            if sk == eng and eng in SAME_ENGINE_NOSYNC:
                continue
            if self.waited[eng].get(sk, 0) >= v:
                continue
            self.engs[eng].wait_ge(self.sems[sk], v)
            self.waited[eng][sk] = v

    def _record(self, sk, v, reads, writes, pw=()):
        for t in writes:
            t.w = {sk: v}
            t.x = {sk: v}
            t.r = {}
        for t in pw:
            if t.w.get(sk, 0) < v:
                t.w[sk] = v
        for t in reads:
            if t.r.get(sk, 0) < v:
                t.r[sk] = v

    def op(self, eng, fn, reads=(), writes=(), pw=()):
        self._wait(eng, self._deps(reads, writes, pw))
        ins = fn(self.engs[eng])
        self.cnt[eng] += 1
        ins.then_inc(self.sems[eng], 1)
        self._record(eng, self.cnt[eng], reads, writes, pw)

    def dma(self, q, out, in_, reads=(), writes=(), pw=(), is_output=False, **kw):
        self._wait(q, self._deps(reads, writes, pw))
        half = len(self.dcnt) // 2
        if q == "pool":
            i = half + self.drr_sw
            self.drr_sw = (self.drr_sw + 1) % half
        else:
            i = self.drr
            self.drr = (self.drr + 1) % half
        if self.dcnt[i] > 0:
            self._wait(q, {("d", i): self.dcnt[i]})
        self.dcnt[i] += 16
        self.engs[q].dma_start(out=out, in_=in_, **kw).then_inc(self.sems[("d", i)], 16)
        self._record(("d", i), self.dcnt[i], reads, writes, pw)
        if is_output:
            self.out_events[("d", i)] = self.dcnt[i]

    def barrier(self, dma=True):
        deps = {e: c for e, c in self.cnt.items() if c > 0}
        if dma:
            for i, c in enumerate(self.dcnt):
                if c > 0:
                    deps[("d", i)] = c
        for e in self.engs:
            self._wait(e, deps)

    def finish(self):
        self._wait("sp", dict(self.out_events))
        self.barrier()

    def init_wrings(self):
        self.wsm = [self.sb("wsm%d" % i, [128, 8, 128], BF16) for i in range(8)]
        self.wsi = 0
        self.wbi = 0

    def alloc_big(self, n=2):
        self.wbi += 1000
        self.wbg = [self.sb("wbg%d_%d" % (self.wbi, i), [128, 8, 512], BF16) for i in range(n)]

    def load_w(self, dram_w, col0, ncols):
        if ncols <= 128:
            t, tk = self.wsm[self.wsi % len(self.wsm)]
            self.wsi += 1
        else:
            t, tk = self.wbg[self.wbi % len(self.wbg)]
            self.wbi += 1
        self.dma("pool", t[:, :, 0:ncols], dram_w[:, col0:col0 + ncols].rearrange("(c p) n -> p c n", p=128), writes=[tk])
        return t, tk

    def sb(self, name, shape, dt, side=None):
        if side is None:
            t = self.es.enter_context(self.nc.sbuf_tensor("sb_" + name, shape, dt))
        else:
            t = self.es.enter_context(self.nc.sbuf_tensor("sb_" + name, shape, dt, side=side))
        return t, Tok()

    def dump(self, name, ap, tok, shape, dt=F32):
        if name not in DEBUG:
            return
        d = self.nc.dram_tensor("dbg_" + name, list(shape), dt, kind="ExternalOutput").ap()
        self.dma("sp", d, ap, reads=[tok], is_output=True)
        self.dbg[name] = d


def _fwd_tab(L, n):
    s = np.arange(L, dtype=np.float64)[:, None]
    f = (np.arange(n // 2, dtype=np.float64) + 0.5)[None, :]
    ang = 2.0 * np.pi * s * f / n
    return np.concatenate([np.cos(ang), -np.sin(ang)], 1).astype(np.float32)


def _bwd_tab(L, n):
    s = np.arange(L, dtype=np.float64)[:, None]
    f = (np.arange(n // 2, dtype=np.float64) + 0.5)[None, :]
    ang = 2.0 * np.pi * s * f / n
    t = np.concatenate([np.cos(ang), np.sin(ang)], 1)
    t[0] = 0.0
    return t.astype(np.float32)


def _inv_tab(n, Lout):
    t = np.arange(Lout, dtype=np.float64)[None, :]
    f = (np.arange(n // 2, dtype=np.float64) + 0.5)[:, None]
    ang = 2.0 * np.pi * f * t / n
    return ((2.0 / n) * np.concatenate([np.cos(ang), -np.sin(ang)], 0)).astype(np.float32)


def _x_tab(half):
    d = (np.arange(1024, dtype=np.float64) - 512.0)[:, None]
    f = (np.arange(512, dtype=np.float64) + 0.5)[None, :]
    ang = 2.0 * np.pi * d * f / 1024.0
    sign = -1.0 if half == 1 else 1.0
    t = np.concatenate([np.cos(ang), sign * np.sin(ang)], 1)
    t[0] = 0.0
    return t.astype(np.float32)


def _z_tab(L):
    t = np.linspace(0.0, 1.0, L, dtype=np.float32)[:, None]
    bands = np.linspace(1e-4, 15.0, 16, dtype=np.float32)[None]
    w = (2.0 * np.float32(math.pi) * np.arange(L, dtype=np.float32)[:, None] / np.float32(L)).astype(np.float32)
    z = np.concatenate([t, np.cos(bands * w), -np.sin(bands * w)], axis=-1).astype(np.float32)
    return np.ascontiguousarray(z.T)


def _decay_consts():
    mn = math.log(1e-2) / 1.5
    mx = math.log(1e-2) / 0.3
    deltas = np.linspace(mn, mx, 1024, dtype=np.float32)
    absd = np.abs(deltas).astype(np.float32)[None, :]
    negt = np.zeros((128, 10), np.float32)
    t256 = np.linspace(0.0, 1.0, 256, dtype=np.float32)
    t1k = np.linspace(0.0, 1.0, 1024, dtype=np.float32)
    for tc in range(2):
        negt[:, tc] = -t256[tc * 128:(tc + 1) * 128]
    for tc in range(8):
        negt[:, 2 + tc] = -t1k[tc * 128:(tc + 1) * 128]
    return absd, negt


_CONST_CACHE = {}


def _consts(half):
    if half in _CONST_CACHE:
        return _CONST_CACHE[half]
    c = {
        "fwd256": _fwd_tab(256, 512), "bwd256": _bwd_tab(256, 512), "inv256": _inv_tab(512, 256),
        "fwd512": _fwd_tab(512, 1024), "bwd512": _bwd_tab(512, 1024), "inv512": _inv_tab(1024, 512),
        "x1024": _x_tab(half),
        "zt256": _z_tab(256), "zt1024": _z_tab(1024),
        "absd": _decay_consts()[0], "negt": _decay_consts()[1],
        "ident": np.eye(128, dtype=np.float32),
    }
    import ml_dtypes
    for k in ("fwd256", "bwd256", "inv256", "fwd512", "bwd512", "inv512", "x1024"):
        c[k] = c[k].astype(ml_dtypes.bfloat16)
    _CONST_CACHE[half] = c
    return c


def build_program():
    nc = bass.Bass("TRN2", target_bir_lowering=False)

    BF_TABS = ("fwd256", "bwd256", "inv256", "fwd512", "bwd512", "inv512", "x1024")

    def din(name, shape):
        return nc.dram_tensor(name, list(shape), BF16 if name in BF_TABS else F32, kind="ExternalInput").ap()

    def dout(name, shape):
        return nc.dram_tensor(name, list(shape), F32, kind="ExternalOutput").ap()

    I = {}
    for name, shape in [
        ("xall", (NTOK, D)), ("params", (128, P_NP)), ("ck", (256, D)), ("cv", (256, D)),
        ("w_ada", (D, 3 * D)), ("w_in", (D, 10 * D)), ("w_bh", (D, D)), ("w_ba", (D, D)), ("w_out", (D, D)),
        ("fw1", (33, 64)), ("fw2", (64, 64)), ("w3c", (64, 3 * D)),
        ("rows", (3, D)),
        ("fwd256", (256, 512)), ("bwd256", (256, 512)), ("inv256", (512, 256)),
        ("fwd512", (512, 1024)), ("bwd512", (512, 1024)), ("inv512", (1024, 512)), ("x1024", (1024, 1024)),
        ("zt256", (33, 256)), ("zt1024", (33, 1024)), ("absd", (1, D)), ("negt", (128, 10)),
        ("ident", (128, 128)), ("btab", (16, 128, 24 * 64)), ("cmask", (128, 64)), ("rmask", (128, 48)),
    ]:
        I[name] = din(name, shape)
    O = {"y": dout("y", (NOWN, D)), "nk": dout("nk", (1024, D)), "nv": dout("nv", (1024, D))}

    with ExitStack() as es:
        b = Bld(nc, es)
        par, par_t = b.sb("par", [128, P_NP], F32)
        ident, ident_t = b.sb("ident", [128, 128], F32)
        identb, identb_t = b.sb("identb", [128, 128], BF16)
        b.dma("sp", par[:], I["params"][:, :], writes=[par_t])
        b.dma("sp", ident[:], I["ident"][:, :], writes=[ident_t])
        b.op("dve", lambda e: e.tensor_copy(out=identb[:], in_=ident[:]), reads=[ident_t], writes=[identb_t])

        esH = ExitStack()
        b.es = esH
        h256, h256_t = b.sb("h256", [128, 4, D], BF16, side="right")
        hoo, hoo_t = b.sb("hoo", [128, 8, D], BF16, side="right")
        hox, hox_t = b.sb("hox", [128, 8, D], BF16, side="right")
        dft = {}
        for name, rows, cols in (("fwd256", 256, 512), ("fwd512", 512, 1024), ("inv256", 512, 256), ("inv512", 1024, 512)):
            t, tk = b.sb("t_" + name, [128, rows // 128, cols], BF16, side="right")
            dft[name] = (t, tk)
        b.es = es

        b.init_wrings()
        esA = ExitStack()
        b.es = esA
        b.wbg = [b.sb("wbgA%d" % i, [128, 8, 512], BF16, side="right") for i in range(4)]
        b.es = es
        ada_pre = []
        phase_filters(b, I, par, par_t, h256, h256_t, hoo, hoo_t, hox, hox_t, dft, ada_pre)
        b.dump("h256", h256[:], h256_t, [128, 4, D], BF16)
        b.dump("hoo", hoo[:], hoo_t, [128, 8, D], BF16)
        b.dump("hox", hox[:], hox_t, [128, 8, D], BF16)
        if STOP_AFTER == "filters":
            b.finish()
            esH.close()
            return nc, b

        modsb, modsb_t = b.sb("modsb", [128, 24, 2], F32)
        hT, _ = b.sb("hT", [128, 8, NTOK], BF16)
        hT_t = Tok()
        yg, yg_t = b.sb("yg", [128, 8, NOWN], BF16)
        scb, scb_t = b.sb("scb", [128, 16], BF16)
        esX = ExitStack()
        b.es = esX
        xts = [b.sb("xt%d" % i, [128, D], F32) for i in range(8)]
        b.es = es
        for i in range(8):
            b.dma("sp", xts[i][0][:], I["xall"][i * 128:(i + 1) * 128, :], writes=[xts[i][1]])
        phase_mod(b, I, par, par_t, modsb, modsb_t, ada_pre, scb, scb_t)
        b.dump("mod", modsb[:], modsb_t, [128, 24, 2])
        pre_w = {(cc, sec): b.load_w(I["w_in"], sec * D + cc * 128, 128) for cc in range(2) for sec in range(2)}
        shared = {}
        phase_ht(b, I, ident, ident_t, modsb, modsb_t, hT, hT_t, xts, 8)
        esX.close()
        esA.close()
        b.dump("hT", hT[:], hT_t, [128, 8, NTOK], BF16)
        if STOP_AFTER == "ht":
            b.finish()
            esH.close()
            return nc, b
        try:
            phase_hyena(b, I, par, par_t, identb, identb_t, hT, hT_t, yg, yg_t, h256, h256_t, hoo, hoo_t, hox, hox_t, dft, pre_w, shared)
        except StopBuild:
            b.finish()
            esH.close()
            return nc, b
        b.dump("yg", yg[:], yg_t, [128, 8, NOWN], BF16)
        if STOP_AFTER == "hyena":
            b.finish()
            esH.close()
            return nc, b
        b.barrier()
        esH.close()
        ya, ya_t = b.sb("ya", [128, 8, NOWN], BF16)
        phase_attn(b, I, O, par, par_t, ident, ident_t, hT, hT_t, ya, ya_t, shared)
        b.dump("ya", ya[:], ya_t, [128, 8, NOWN], BF16)
        if STOP_AFTER == "attn":
            b.finish()
            return nc, b
        mg, mg_t = b.sb("mg", [128, 8, NOWN], BF16)
        b.wbi += 1000
        b.wbg = [b.sb("wbgF%d" % i, [128, 8, 512], BF16) for i in range(4)]
        pre = {"wa": [b.load_w(I["w_ada"], 2 * D + cb * 512, 512) for cb in range(2)],
               "wo": [b.load_w(I["w_out"], cb * 512, 512) for cb in range(2)]}
        pre["rows"] = b.sb("rows", [128, 2, D], F32)
        pre["brow"] = b.sb("brow", [128, D], F32)
        for r in range(2):
            b.dma("sp", pre["rows"][0][:, r, :], I["rows"][r + 1:r + 2, :].to_broadcast([128, D]), pw=[pre["rows"][1]])
        b.dma("sp", pre["brow"][0][:], I["rows"][0:1, :].to_broadcast([128, D]), writes=[pre["brow"][1]])
        phase_merge(b, I, hT, hT_t, yg, yg_t, ya, ya_t, mg, mg_t, shared)
        b.dump("mg", mg[:], mg_t, [128, 8, NOWN], BF16)
        phase_final(b, I, O, par, par_t, mg, mg_t, pre)

        b.finish()
    return nc, b


def phase_mod(b, I, par, par_t, modsb, modsb_t, ada_pre, scb, scb_t):
    with ExitStack() as es:
        old_es = b.es
        b.es = es
        b.op("act", lambda e: e.activation(out=scb[:], in_=par[:, P_COND:P_COND + 16], func=AF.Silu),
             reads=[par_t], writes=[scb_t])
        ps, ps_t = b.psum()
        for jb in range(4):
            w, w_t = ada_pre[jb]
            for c4 in range(4):
                ch = jb * 4 + c4
                for kc in range(8):
                    b.op("pe", lambda e: e.matmul(ps[:, 2 * ch:2 * ch + 2], lhsT=w[:, kc, c4 * 128:(c4 + 1) * 128],
                                                   rhs=scb[:, 2 * kc:2 * kc + 2], start=(kc == 0), stop=(kc == 7)),
                         reads=[w_t, scb_t], writes=[ps_t])
        b.op("dve", lambda e: e.tensor_tensor(
            out=modsb[:, 0:16, :], in0=ps[:, 0:32].rearrange("p (c w) -> p c w", w=2),
            in1=par[:, P_BADA:P_BADA + 16].unsqueeze(2).to_broadcast([128, 16, 2]), op=ALU.add),
            reads=[ps_t, par_t], writes=[modsb_t])
        b.op("dve", lambda e: e.tensor_scalar(out=modsb[:, 8:16, :], in0=modsb[:, 8:16, :], scalar1=1.0, scalar2=None,
                                               op0=ALU.add), reads=[modsb_t], writes=[modsb_t])
        b.es = old_es


def phase_ht(b, I, ident, ident_t, modsb, modsb_t, hT, hT_t, xts, npre):
    with ExitStack() as es:
        old_es = b.es
        b.es = es
        xi = 0
        k = 0
        groups = [(0, 4, 0), (512, 4, 0), (1024, 4, 1), (1536, 4, 1), (2048, 2, 1)]
        for (col0, ntile, w) in groups:
            tiles = []
            for t in range(ntile):
                xt, xt_t = xts[xi % 8]
                xi += 1
                r0 = col0 + t * 128
                if xi > npre:
                    b.dma("sp", xt[:], I["xall"][r0:r0 + 128, :], writes=[xt_t])
                tiles.append((xt, xt_t))
            for fc in range(8):
                ps, ps_t = b.psum()
                for t, (xt, xt_t) in enumerate(tiles):
                    b.op("pe", lambda e: e.transpose(ps[:, t * 128:(t + 1) * 128], xt[:, fc * 128:(fc + 1) * 128], ident[:]),
                         reads=[xt_t, ident_t], writes=[ps_t])
                n = ntile * 128
                k += 1
                if k % 2 == 0:
                    b.op("act", lambda e: e.activation(out=hT[:, fc, col0:col0 + n], in_=ps[:, 0:n], func=AF.Identity,
                                                        bias=modsb[:, fc, w:w + 1], scale=modsb[:, 8 + fc, w:w + 1]),
                         reads=[ps_t, modsb_t], pw=[hT_t])
                else:
                    b.op("dve", lambda e: e.tensor_scalar(out=hT[:, fc, col0:col0 + n], in0=ps[:, 0:n],
                                                           scalar1=modsb[:, 8 + fc, w:w + 1], scalar2=modsb[:, fc, w:w + 1],
                                                           op0=ALU.mult, op1=ALU.add),
                         reads=[ps_t, modsb_t], pw=[hT_t])
        b.barrier()
        b.es = old_es


def phase_hyena(b, I, par, par_t, identb, identb_t, hT, hT_t, yg, yg_t, h256, h256_t, hoo, hoo_t, hox, hox_t, dft, pre_w, shared):
    with ExitStack() as es:
        old_es = b.es
        b.es = es

        fwd256, fwd256_t = dft["fwd256"]
        inv256, inv256_t = dft["inv256"]
        fwd512, fwd512_t = dft["fwd512"]
        inv512, inv512_t = dft["inv512"]

        cwf, cwf_t = b.sb("cwf", [128, 4, 24], F32)
        for i, (j, fl) in enumerate(((0, 0), (2, 0), (0, 1), (2, 1))):
            b.op("dve", lambda e: e.tensor_scalar(out=cwf[:, i, :], in0=par[:, P_CW + j * 24:P_CW + (j + 1) * 24],
                                                   scalar1=par[:, P_FLAG + fl:P_FLAG + fl + 1], scalar2=None, op0=ALU.mult),
                 reads=[par_t], pw=[cwf_t])

        def cw(j, ci):
            return par[:, P_CW + j * 24 + ci:P_CW + j * 24 + ci + 1]

        cA, _ = b.sb("cA", [128, 2048], F32)
        cB, _ = b.sb("cB", [128, 2048], F32)
        cA_t = [Tok() for _ in range(4)]
        cB_t = [Tok() for _ in range(4)]
        edgs = [b.sb("edges%d" % i, [128, 4], F32) for i in range(2)]
        sgs = [b.sb("sg%d" % i, [128, 512], F32) for i in range(2)]
        tmps = [b.sb("sp%d" % i, [128, 256], F32) for i in range(10)]
        tmpi = [0]

        def tmp(eng="dve"):
            tmpi[0] += 1
            ring = tmps if eng == "dve" else ptmps
            return ring[tmpi[0] % len(ring)]
        utm, utm_t = b.sb("utm", [128, 16, 256], BF16)
        ufm, ufm_t = b.sb("ufm", [128, 2, 2048], BF16)
        xg, xg_t = b.sb("xg", [128, 2, NOWN], BF16)
        Y4s = [b.sb("Y4_%d" % i, [128, 4, 256], BF16) for i in range(2)]
        Y8, Y8_t = b.sb("Y8", [128, 8, 256], BF16)
        eps = [b.sb("ep%d" % i, [128, 512], F32) for i in range(4)]
        epi = [0]
        ptmps = []
        pend_is = [None]
        pend_T = []

        def conv_block(ps, ps_t, acc, acc_t, col0, ci, nseg):
            seglen = 512 // nseg
            b.op("act", lambda e: e.activation(out=acc[:, col0:col0 + 512], in_=ps[:, :], func=AF.Identity,
                                                bias=par[:, P_CB + ci:P_CB + ci + 1], scale=cw(1, ci)),
                 reads=[ps_t, par_t], writes=[acc_t])
            av = acc[:, col0:col0 + 512].rearrange("p (s l) -> p s l", s=nseg)
            pv = ps[:, :].rearrange("p (s l) -> p s l", s=nseg)
            b.op("dve", lambda e: e.scalar_tensor_tensor(out=av[:, :, 1:seglen], in0=pv[:, :, 0:seglen - 1], scalar=cw(0, ci),
                                                          in1=av[:, :, 1:seglen], op0=ALU.mult, op1=ALU.add),
                 reads=[ps_t, par_t, acc_t], writes=[acc_t])
            b.op("dve", lambda e: e.scalar_tensor_tensor(out=av[:, :, 0:seglen - 1], in0=pv[:, :, 1:seglen], scalar=cw(2, ci),
                                                          in1=av[:, :, 0:seglen - 1], op0=ALU.mult, op1=ALU.add),
                 reads=[ps_t, par_t, acc_t], writes=[acc_t])

        def fix(acc, acc_t, col, wi, ci, src, src_t):
            b.op("dve", lambda e: e.scalar_tensor_tensor(out=acc[:, col:col + 1], in0=src, scalar=cwf[:, wi, ci:ci + 1],
                                                          in1=acc[:, col:col + 1], op0=ALU.mult, op1=ALU.add),
                 reads=[src_t, cwf_t, acc_t], writes=[acc_t])

        def proj(w, w_t, col0, n=512):
            ps, ps_t = b.psum()
            for kc in range(8):
                b.op("pe", lambda e: e.matmul(ps[:, 0:n], lhsT=w[:, kc, 0:128], rhs=hT[:, kc, col0:col0 + n],
                                               start=(kc == 0), stop=(kc == 7)),
                     reads=[w_t, hT_t], writes=[ps_t])
            return ps, ps_t

        for g in range(4):
            ch0 = g * 256
            for cc in range(2):
                chunk = g * 2 + cc
                ws = []
                for sec in range(2):
                    if g == 0:
                        ws.append(pre_w[(cc, sec)])
                    elif cc == 0:
                        ws.append(shared["ws_next"][sec])
                    else:
                        ws.append(b.load_w(I["w_in"], sec * D + chunk * 128, 128))

                def P(sec, tb, cc=cc, chunk=chunk, ws=ws):
                    w, w_t = ws[sec]
                    acc, acc_t = (cA, cA_t) if sec == 0 else (cB, cB_t)
                    ed, ed_t = edgs[sec]
                    ci = sec * 8 + chunk
                    ps, ps_t = proj(w, w_t, tb * 512)
                    conv_block(ps, ps_t, acc, acc_t[tb], tb * 512, ci, 2 if tb < 2 else 1)
                    if tb >= 2:
                        b.op("dve", lambda e: e.tensor_copy(out=ed[:, 2 * (tb - 2):2 * (tb - 2) + 2], in_=ps[:, 0:512:511]),
                             reads=[ps_t], writes=[ed_t] if tb == 2 else [], pw=[] if tb == 2 else [ed_t])

                def U(tb, cc=cc):
                    first = (cc == 0 and tb == 0)
                    sl = slice(tb * 512, (tb + 1) * 512)
                    b.op("dve", lambda e: e.tensor_tensor(out=ufm[:, cc, sl], in0=cA[:, sl], in1=cB[:, sl], op=ALU.mult),
                         reads=[cA_t[tb], cB_t[tb]], writes=[ufm_t] if first else [], pw=[] if first else [ufm_t])

                def T(tb, cc=cc):
                    first = (cc == 0 and tb == 0)
                    ps, ps_t = b.psum()
                    psb = ps[:, :].bitcast(BF16)
                    for j in range(4):
                        tt = tb * 4 + j
                        b.op("pe", lambda e: e.transpose(psb[:, j * 128:(j + 1) * 128], ufm[:, cc, tt * 128:(tt + 1) * 128],
                                                          identb[:]),
                             reads=[ufm_t, identb_t], writes=[ps_t])
                    b.op("act", lambda e: e.copy(out=utm[:, tb * 4:tb * 4 + 4, cc * 128:(cc + 1) * 128],
                                                  in_=psb[:, 0:512].rearrange("p (j c) -> p j c", j=4)),
                         reads=[ps_t], writes=[utm_t] if first else [], pw=[] if first else [utm_t])

                P(0, 0); P(1, 0); U(0)
                P(0, 1); P(1, 1); U(1)
                for tfn in pend_T:
                    tfn()
                pend_T.clear()
                P(0, 2); P(1, 2); T(0)
                P(0, 3); P(1, 3); T(1)
                for sec in range(2):
                    acc, acc_t = (cA, cA_t) if sec == 0 else (cB, cB_t)
                    ed, ed_t = edgs[sec]
                    ci = sec * 8 + chunk
                    fix(acc, acc_t[2], 1024, 0, ci, ed[:, 3:4], ed_t)
                    fix(acc, acc_t[2], 1535, 3, ci, ed[:, 2:3], ed_t)
                    fix(acc, acc_t[3], 1536, 2, ci, ed[:, 1:2], ed_t)
                    fix(acc, acc_t[3], 2047, 1, ci, ed[:, 0:1], ed_t)
                U(2); U(3)
                pend_T.extend([lambda T=T: T(2), lambda T=T: T(3)])
            if HY_STOP == 4:
                b.barrier(); b.es = old_es; return
            def X(tb, g=g, ccs=(0, 1)):
                for cc in ccs:
                    chunk = g * 2 + cc
                    w, w_t = xw[(cc, 2)]
                    ci = 16 + chunk
                    acc, acc_t = (cA, cA_t) if cc == 0 else (cB, cB_t)
                    ps, ps_t = proj(w, w_t, tb * 512)
                    conv_block(ps, ps_t, acc, acc_t[tb], tb * 512, ci, 2 if tb < 2 else 1)
                    if tb == 2:
                        pse, pse_t = b.psum()
                        for kc in range(8):
                            b.op("pe", lambda e: e.matmul(pse[:, 0:2], lhsT=w[:, kc, 0:128], rhs=hT[:, kc, OTH0:OTH0 + 512:511],
                                                           start=(kc == 0), stop=(kc == 7)),
                                 reads=[w_t, hT_t], writes=[pse_t])
                        fix(acc, acc_t[2], 1024, 0, ci, pse[:, 1:2], pse_t)
                        fix(acc, acc_t[2], 1535, 3, ci, pse[:, 0:1], pse_t)
                    w, w_t = xw[(cc, 3)]
                    ps, ps_t = proj(w, w_t, tb * 512)
                    sg, sg_t = sgs[cc]
                    b.op("act", lambda e: e.activation(out=sg[:], in_=ps[:, :], func=AF.Silu), reads=[ps_t], writes=[sg_t])
                    b.op("pool", lambda e: e.tensor_tensor(out=xg[:, cc, tb * 512:(tb + 1) * 512], in0=acc[:, tb * 512:(tb + 1) * 512],
                                                            in1=sg[:], op=ALU.mult),
                         reads=[acc_t[tb], sg_t], writes=[xg_t] if (cc == 0 and tb == 0) else [],
                         pw=[] if (cc == 0 and tb == 0) else [xg_t])

            xw = {}
            for cc in range(2):
                for sec in (2, 3):
                    xw[(cc, sec)] = b.load_w(I["w_in"], sec * D + (g * 2 + cc) * 128, 128)
            if g < 3:
                shared["ws_next"] = [b.load_w(I["w_in"], sec * D + ((g + 1) * 2) * 128, 128) for sec in range(2)]
            if g == 3 and HY_STOP == 0:
                shared["W0"] = [b.load_w(I["w_in"], (4 + i) * D, 128) for i in range(4)]
            def spectral(Ups, Hre, Him, H_t, Yre, Yim, Y_t, first, accum, eng="dve", ceng="pool"):
                terms_re = []
                terms_im = []
                for (ps, ps_t, hre, him, h_t) in accum:
                    ure, uim = ps[:, 0:256], ps[:, 256:512]
                    terms_re += [(ure, hre, ps_t, h_t, 1.0), (uim, him, ps_t, h_t, -1.0)]
                    terms_im += [(ure, him, ps_t, h_t, 1.0), (uim, hre, ps_t, h_t, 1.0)]
                for terms, Yo in ((terms_re, Yre), (terms_im, Yim)):
                    acc = None
                    for i, (u, h, u_t, h_t, sgn) in enumerate(terms):
                        t, t_t = tmp(eng)
                        b.op(eng, lambda e: e.tensor_tensor(out=t[:], in0=u, in1=h, op=ALU.mult),
                             reads=[u_t, h_t], writes=[t_t])
                        if acc is None:
                            acc = (t, t_t)
                            continue
                        last = (i == len(terms) - 1)
                        op = ALU.add if sgn > 0 else ALU.subtract
                        if last:
                            b.op(ceng, lambda e: e.tensor_tensor(out=Yo, in0=acc[0][:], in1=t[:], op=op),
                                 reads=[acc[1], t_t], pw=[Y_t])
                        else:
                            n, n_t = tmp(eng)
                            b.op(ceng, lambda e: e.tensor_tensor(out=n[:], in0=acc[0][:], in1=t[:], op=op),
                                 reads=[acc[1], t_t], writes=[n_t])
                            acc = (n, n_t)

            def fwd_dft(tab, tab_t, nk, tc0, ci_re, ci_im):
                ps, ps_t = b.psum()
                for part, ci in ((0, ci_re), (1, ci_im)):
                    for kc in range(nk):
                        b.op("pe", lambda e: e.matmul(ps[:, part * 256:(part + 1) * 256], lhsT=tab[:, kc, ci * 128:(ci + 1) * 128],
                                                       rhs=utm[:, tc0 + kc, :], start=(kc == 0), stop=(kc == nk - 1)),
                             reads=[tab_t, utm_t], writes=[ps_t])
                return ps, ps_t

            def epilogue(ps, ps_t, off, cc, tok0, n, g=g):
                chunk = g * 2 + cc
                epi[0] += 1
                ep, ep_t = eps[epi[0] % 4]
                b.op("dve", lambda e: e.scalar_tensor_tensor(out=ep[:, 0:n], in0=ufm[:, cc, tok0:tok0 + n],
                                                              scalar=par[:, P_HD + chunk:P_HD + chunk + 1],
                                                              in1=ps[:, off:off + n], op0=ALU.mult, op1=ALU.add),
                     reads=[ufm_t, par_t, ps_t], writes=[ep_t])
                b.op("pool", lambda e: e.tensor_tensor(out=yg[:, chunk, tok0:tok0 + n], in0=ep[:, 0:n],
                                                        in1=xg[:, cc, tok0:tok0 + n], op=ALU.mult),
                     reads=[ep_t, xg_t], pw=[yg_t])

            def F(bb):
                Y4, Y4_t = Y4s[bb % 2]
                for ip in range(2):
                    ps, ps_t = fwd_dft(fwd256, fwd256_t, 2, 2 * bb, ip, ip + 2)
                    spectral(None, None, None, None, Y4[:, ip, :], Y4[:, ip + 2, :], Y4_t, True,
                             [(ps, ps_t, h256[:, ip, ch0:ch0 + 256], h256[:, ip + 2, ch0:ch0 + 256], h256_t)])

            def Iv(bb):
                Y4, Y4_t = Y4s[bb % 2]
                ps, ps_t = b.psum()
                for cc in range(2):
                    for ci in range(4):
                        b.op("pe", lambda e: e.matmul(ps[:, cc * 256:(cc + 1) * 256], lhsT=Y4[:, ci, cc * 128:(cc + 1) * 128],
                                                       rhs=inv256[:, ci, :], start=(ci == 0), stop=(ci == 3)),
                             reads=[Y4_t, inv256_t], writes=[ps_t])
                for cc in range(2):
                    epilogue(ps, ps_t, cc * 256, cc, bb * 256, 256)

            def FS(ip):
                ps1, ps1_t = fwd_dft(fwd512, fwd512_t, 4, 8, ip, ip + 4)
                ps2, ps2_t = fwd_dft(fwd512, fwd512_t, 4, 12, ip, ip + 4)
                spectral(None, None, None, None, Y8[:, ip, :], Y8[:, ip + 4, :], Y8_t, True,
                         [(ps1, ps1_t, hoo[:, ip, ch0:ch0 + 256], hoo[:, ip + 4, ch0:ch0 + 256], hoo_t),
                          (ps2, ps2_t, hox[:, ip, ch0:ch0 + 256], hox[:, ip + 4, ch0:ch0 + 256], hox_t)])

            def IS(epilogue=epilogue):
                for cc in range(2):
                    ps, ps_t = b.psum()
                    for ci in range(8):
                        b.op("pe", lambda e: e.matmul(ps[:, :], lhsT=Y8[:, ci, cc * 128:(cc + 1) * 128], rhs=inv512[:, ci, :],
                                                       start=(ci == 0), stop=(ci == 7)),
                             reads=[Y8_t, inv512_t], writes=[ps_t])
                    epilogue(ps, ps_t, 0, cc, 1024, 512)

            X(0)
            for tfn in pend_T:
                tfn()
            pend_T.clear()
            F(0); F(1); Iv(0); F(2); Iv(1); X(1); F(3); Iv(2); FS(0); Iv(3); FS(1); X(2, ccs=(0,)); FS(2); X(2, ccs=(1,)); FS(3); IS()
            if HY_STOP == 8:
                b.barrier(); b.es = old_es; return
        b.barrier()
        b.es = old_es


def interleave_gen(*gens):
    gens = list(gens)
    while gens:
        for g in list(gens):
            try:
                next(g)
            except StopIteration:
                gens.remove(g)
        yield


def interleave(*gens):
    gens = list(gens)
    while gens:
        for g in list(gens):
            try:
                next(g)
            except StopIteration:
                gens.remove(g)


def rr(lst, state=[0]):
    state[0] += 1
    return lst[state[0] % len(lst)]


def phase_filters(b, I, par, par_t, h256, h256_t, hoo, hoo_t, hox, hox_t, dft, ada_pre):
    nc = b.nc
    with ExitStack() as es:
        old_es = b.es
        b.es = es
        fw1, fw1_t = b.sb("fw1", [33, 64], F32)
        fw2, fw2_t = b.sb("fw2", [64, 64], F32)
        w3, w3_t = b.sb("w3", [64, 3 * D], BF16)
        fb, fb_t = b.sb("fb", [64, 4], F32)
        b.dma("sp", fw1[:], I["fw1"][:, :], writes=[fw1_t])
        b.dma("sp", fw2[:], I["fw2"][:, :], writes=[fw2_t])
        b.dma("pool", w3[:], I["w3c"][:, :], writes=[w3_t])
        zts = {}
        for L in (256, 1024):
            zts[L] = b.sb("zt%d" % L, [33, L], F32)
            b.dma("sp", zts[L][0][:], I["zt%d" % L][:, :], writes=[zts[L][1]])
        absd, absd_t = b.sb("absd", [128, D], F32)
        b.dma("sp", absd[:], I["absd"][0:1, :].to_broadcast([128, D]), writes=[absd_t])
        negt, negt_t = b.sb("negt", [128, 10], F32)
        b.dma("sp", negt[:], I["negt"][:, :], writes=[negt_t])
        for name in ("fwd256",):
            b.dma("sp", dft[name][0][:], I[name].rearrange("(c p) n -> p c n", p=128), writes=[dft[name][1]])
        fpar = par[0:64, P_FPAR:P_FPAR + 4]
        for l in range(2):
            b.op("dve", lambda e, l=l: e.tensor_scalar(out=fb[:, l:l + 1], in0=fpar[:, l:l + 1],
                                                        scalar1=fpar[:, 2 + l:3 + l], scalar2=None,
                                                        op0=ALU.mult),
                 reads=[par_t], writes=[fb_t])

        def load_tab(name, rows, cols):
            t, tk = b.sb(name, [128, rows // 128, cols], BF16)
            b.dma("sp", t[:], I[name].rearrange("(c p) n -> p c n", p=128), writes=[tk])
            return t, tk
        fwd256, fwd256_t = dft["fwd256"]
        fwd512, fwd512_t = dft["fwd512"]
        bwd256, bwd256_t = load_tab("bwd256", 256, 512)
        b.dma("sp", dft["fwd512"][0][:], I["fwd512"].rearrange("(c p) n -> p c n", p=128), writes=[dft["fwd512"][1]])
        bwd512, bwd512_t = load_tab("bwd512", 512, 1024)
        x1024, x1024_t = load_tab("x1024", 1024, 1024)
        for jb in range(4):
            ada_pre.append(b.load_w(I["w_ada"], jb * 512, 512))
        for name in ("inv256", "inv512"):
            b.dma("sp", dft[name][0][:], I[name].rearrange("(c p) n -> p c n", p=128), writes=[dft[name][1]])

        for L, ztn, decn in ((256, "zt256", "dec256"), (1024, "zt1024", "dec1024")):
            with ExitStack() as es2:
                b.es = es2
                zt, zt_t = zts[L]
                hid = []
                for l in range(2):
                    hid.append(b.sb("hid%d_%d" % (l, L), [64, L], F32))
                hidb, hidb_t = b.sb("hidb%d" % L, [64, L], BF16)
                tmp, tmp_t = b.sb("ftmp%d" % L, [64, 512], F32)
                tmpi, tmpi_t = b.sb("ftmpi%d" % L, [64, 512], mybir.dt.int32)
                tmpf, tmpf_t = b.sb("ftmpf%d" % L, [64, 512], F32)
                nblk = max(1, L // 512)
                bw = min(L, 512)
                for l in range(2):
                    src, src_t = (zt, zt_t) if l == 0 else hid[0]
                    wl, wl_t = (fw1, fw1_t) if l == 0 else (fw2, fw2_t)
                    K = 33 if l == 0 else 64
                    dst, dst_t = hid[l]
                    for blk in range(nblk):
                        ps, ps_t = b.psum()
                        sl = slice(blk * bw, (blk + 1) * bw)
                        b.op("pe", lambda e: e.matmul(ps[0:64, 0:bw], lhsT=wl[0:K, :], rhs=src[0:K, sl],
                                                       start=True, stop=True),
                             reads=[wl_t, src_t], writes=[ps_t])
                        b.op("dve", lambda e: e.tensor_scalar(out=tmp[:, 0:bw], in0=ps[0:64, 0:bw],
                                                               scalar1=fpar[:, 2 + l:3 + l], scalar2=fb[:, l:l + 1],
                                                               op0=ALU.mult, op1=ALU.add),
                             reads=[ps_t, par_t, fb_t], writes=[tmp_t])
                        b.op("dve", lambda e: e.tensor_scalar(out=tmpi[:, 0:bw], in0=tmp[:, 0:bw],
                                                               scalar1=1.0 / TWO_PI, scalar2=None, op0=ALU.mult),
                             reads=[tmp_t], writes=[tmpi_t])
                        b.op("dve", lambda e: e.tensor_copy(out=tmpf[:, 0:bw], in_=tmpi[:, 0:bw]),
                             reads=[tmpi_t], writes=[tmpf_t])
                        b.op("dve", lambda e: e.scalar_tensor_tensor(out=tmp[:, 0:bw], in0=tmpf[:, 0:bw], scalar=-TWO_PI,
                                                                      in1=tmp[:, 0:bw], op0=ALU.mult, op1=ALU.add),
                             reads=[tmpf_t, tmp_t], writes=[tmp_t])
                        b.op("act", lambda e: e.activation(out=dst[:, sl], in_=tmp[:, 0:bw], func=AF.Sin),
                             reads=[tmp_t], writes=[dst_t])
                b.op("dve", lambda e: e.tensor_copy(out=hidb[:], in_=hid[1][0][:]), reads=[hid[1][1]], writes=[hidb_t])
                b.dump("hid%d" % L, hid[1][0][:], hid[1][1], [64, L])

                ntc = L // 128
                ntc_ab = ntc if L == 256 else 4
                taps, taps_t = b.sb("taps%d" % L, [128, ntc_ab, 2 * D], BF16)
                if L == 1024:
                    tapsx, tapsx_t = b.sb("tapsx%d" % L, [128, ntc, D], BF16)
                decs = [b.sb("dec%d_%d" % (L, i), [128, D], F32) for i in range(2)]
                for tc in range(ntc):
                    dec, dec_t = decs[tc % 2]
                    tcol = tc if L == 256 else 2 + tc
                    b.op("act", lambda e: e.activation(out=dec[:], in_=absd[:], func=AF.Exp, scale=negt[:, tcol:tcol + 1]),
                         reads=[absd_t, negt_t], writes=[dec_t])
                    if L == 256:
                        colblks = [0, 1, 2, 3]
                    else:
                        colblks = ([0, 1, 2, 3] if tc < 4 else []) + [4, 5]
                    for cb in colblks:
                        ps, ps_t = b.psum()
                        b.op("pe", lambda e: e.matmul(ps[:, :], lhsT=hidb[:, tc * 128:(tc + 1) * 128],
                                                       rhs=w3[:, cb * 512:(cb + 1) * 512], start=True, stop=True),
                             reads=[hidb_t, w3_t], writes=[ps_t])
                        dc = (cb % 2) * 512
                        if cb < 4:
                            dst, dst_t = taps[:, tc, cb * 512:(cb + 1) * 512], taps_t
                        else:
                            dst, dst_t = tapsx[:, tc, (cb - 4) * 512:(cb - 3) * 512], tapsx_t
                        b.op("dve", lambda e: e.scalar_tensor_tensor(out=dst, in0=dec[:, dc:dc + 512],
                                                                      scalar=0.05, in1=ps[:, :], op0=ALU.add, op1=ALU.mult),
                             reads=[ps_t, dec_t], writes=[dst_t])

                evac = ["act", "dve"]
                if L == 256:
                    jobs = [(h256, h256_t, 4, [(fwd256, fwd256_t, 0, 2), (bwd256, bwd256_t, 1, 2)])]
                else:
                    jobs = [(hoo, hoo_t, 8, [(fwd512, fwd512_t, 0, 4), (bwd512, bwd512_t, 1, 4)]),
                            (hox, hox_t, 8, [(x1024, x1024_t, 2, 8)])]
                k = 0
                for (H, H_t, ncoef, srcs) in jobs:
                    for ci in range(ncoef):
                        for cb in range(2):
                            ps, ps_t = b.psum()
                            mm = []
                            for (tab, tab_t, sec, ntcs) in srcs:
                                for tc in range(ntcs):
                                    mm.append((tab, tab_t, sec, tc))
                            for i, (tab, tab_t, sec, tc) in enumerate(mm):
                                b.op("pe", lambda e: e.matmul(
                                    ps[:, :], lhsT=tab[:, tc, ci * 128:(ci + 1) * 128],
                                    rhs=(taps[:, tc, sec * D + cb * 512: sec * D + (cb + 1) * 512] if sec < 2
                                         else tapsx[:, tc, cb * 512:(cb + 1) * 512]),
                                    start=(i == 0), stop=(i == len(mm) - 1)),
                                    reads=[tab_t, taps_t if sec < 2 else tapsx_t], writes=[ps_t])
                            k += 1
                            if k % 2 == 0:
                                b.op("act", lambda e: e.copy(out=H[:, ci, cb * 512:(cb + 1) * 512], in_=ps[:, :]),
                                     reads=[ps_t], writes=[H_t])
                            else:
                                b.op("dve", lambda e: e.tensor_copy(out=H[:, ci, cb * 512:(cb + 1) * 512], in_=ps[:, :]),
                                     reads=[ps_t], writes=[H_t])
                b.barrier(dma=False)
            b.es = es
        b.barrier(dma=False)
        b.es = old_es


def phase_attn(b, I, O, par, par_t, ident, ident_t, hT, hT_t, ya, ya_t, shared):
    with ExitStack() as es:
        old_es = b.es
        b.es = es
        cmask, cmask_t = b.sb("cmask", [128, 64], F32)
        rmask, rmask_t = b.sb("rmask", [128, 6, 8], BF16)
        b.dma("sp", cmask[:], I["cmask"][:, :], writes=[cmask_t])
        b.dma("pool", rmask[:], I["rmask"].rearrange("p (j r) -> p j r", j=6), writes=[rmask_t])
        vexts = [b.sb("vext%d" % i, [128, 16, 2, 128], BF16) for i in range(2)]
        for (v, v_t) in vexts:
            b.op("pool", lambda e: e.memset(v[:], 1.0), writes=[v_t])
        qTs = [b.sb("qT%d" % i, [128, NOWN], BF16) for i in range(2)]
        kTs = [b.sb("kT%d" % i, [128, 2048], BF16) for i in range(2)]
        kf, kf_t = b.sb("kf", [128, 1024], F32)
        ckst, ckst_t = b.sb("ckst", [128, 2, 128], F32)
        nkst, nkst_t = b.sb("nkst", [128, 8, 128], F32)
        nvst, nvst_t = b.sb("nvst", [128, 8, 128], F32)
        sgas = [b.sb("sga%d" % i, [128, NOWN], BF16) for i in range(2)]
        vsts = [b.sb("vst%d" % i, [128, 4, 128], F32) for i in range(2)]
        bts = [b.sb("bt%d" % i, [128, 24, 64], F32) for i in range(2)]
        ptss = [[b.sb("pt%d_%d" % (s2, i), [128, 512], BF16) for i in range(9)] for s2 in range(2)]
        Ts = [b.sb("T%d" % i, [128, 512], F32) for i in range(2)]
        rcs = [b.sb("rc%d" % i, [128, 512], F32) for i in range(4)]
        pti = [0, 0]

        def pt(s2):
            pti[s2] += 1
            return ptss[s2][pti[s2] % 9]

        def proj(w, w_t, col0, n=512):
            ps, ps_t = b.psum()
            for kc in range(8):
                b.op("pe", lambda e: e.matmul(ps[:, 0:n], lhsT=w[:, kc, 0:128], rhs=hT[:, kc, col0:col0 + n],
                                               start=(kc == 0), stop=(kc == 7)),
                     reads=[w_t, hT_t], writes=[ps_t])
            return ps, ps_t

        fin = [0]

        def finalize(pso, pso_t, e, hp, tok0, n, sga, sga_t):
            ob = 64 * e
            db = 64 * (1 - e)
            rc, rc_t = rcs[fin[0] % len(rcs)]
            fin[0] += 1
            b.op("dve", lambda en: en.tensor_copy(out=rc[ob:ob + 64, 0:n], in_=pso[db:db + 64, 0:n]),
                 reads=[pso_t], writes=[rc_t])
            b.op("act", lambda en: en.activation(out=rc[ob:ob + 64, 0:n], in_=rc[ob:ob + 64, 0:n], func=AF.Ln),
                 reads=[rc_t], writes=[rc_t])
            b.op("act", lambda en: en.activation(out=rc[ob:ob + 64, 0:n], in_=rc[ob:ob + 64, 0:n], func=AF.Exp, scale=-1.0),
                 reads=[rc_t], writes=[rc_t])
            b.op("dve", lambda en: en.tensor_tensor(out=rc[ob:ob + 64, 0:n], in0=pso[ob:ob + 64, 0:n],
                                                     in1=rc[ob:ob + 64, 0:n], op=ALU.mult),
                 reads=[pso_t, rc_t], writes=[rc_t])
            b.op("pool", lambda en: en.tensor_tensor(out=ya[ob:ob + 64, hp, tok0:tok0 + n], in0=rc[ob:ob + 64, 0:n],
                                                      in1=sga[ob:ob + 64, tok0:tok0 + n], op=ALU.mult),
                 reads=[rc_t, sga_t], pw=[ya_t])

        def load_pair(hp):
            return [b.load_w(I["w_in"], (4 + i) * D + hp * 128, 128) for i in range(4)]

        def bufs(hp):
            return qTs[hp % 2] + kTs[hp % 2] + sgas[hp % 2] + vexts[hp % 2]

        def proj_gen(hp, W):
            qT, qT_t, kT, kT_t, sga, sga_t, vext, vext_t = bufs(hp)
            (wq, wq_t), (wk, wk_t), (wv, wv_t), (wg, wg_t) = W
            for tb in range(3):
                ps, ps_t = proj(wq, wq_t, tb * 512)
                b.op("act", lambda e: e.activation(out=qT[:, tb * 512:(tb + 1) * 512], in_=ps[:, :], func=AF.Copy, scale=0.125),
                     reads=[ps_t], writes=[qT_t] if tb == 0 else [], pw=[] if tb == 0 else [qT_t])
                yield
            for i, (src0, n, dst0) in enumerate(((0, 512, 0), (512, 512, 512), (1024, 512, 1024), (2048, 256, 1536))):
                ps, ps_t = proj(wk, wk_t, src0, n)
                if i < 2:
                    b.op("act", lambda e: e.copy(out=kf[:, dst0:dst0 + n], in_=ps[:, 0:n]),
                         reads=[ps_t], writes=[kf_t] if i == 0 else [], pw=[] if i == 0 else [kf_t])
                    b.op("pool", lambda e: e.tensor_copy(out=kT[:, dst0:dst0 + n], in_=kf[:, dst0:dst0 + n]),
                         reads=[kf_t], writes=[kT_t] if i == 0 else [], pw=[] if i == 0 else [kT_t])
                else:
                    b.op("act", lambda e: e.copy(out=kT[:, dst0:dst0 + n], in_=ps[:, 0:n]),
                         reads=[ps_t], pw=[kT_t])
                yield
            b.dma("sp", ckst[:], I["ck"][:, hp * 128:(hp + 1) * 128].rearrange("(c p) n -> p c n", p=128), writes=[ckst_t])
            ps, ps_t = b.psum()
            for c in range(2):
                b.op("pe", lambda e: e.transpose(ps[:, c * 128:(c + 1) * 128], ckst[:, c, :], ident[:]),
                     reads=[ckst_t, ident_t], writes=[ps_t])
            b.op("dve", lambda e: e.tensor_copy(out=kT[:, 1792:2048], in_=ps[:, 0:256]), reads=[ps_t], pw=[kT_t])
            for half2 in range(2):
                ps, ps_t = b.psum()
                for j in range(4):
                    tt = half2 * 4 + j
                    b.op("pe", lambda e: e.transpose(ps[:, j * 128:(j + 1) * 128], kf[:, tt * 128:(tt + 1) * 128], ident[:]),
                         reads=[kf_t, ident_t], writes=[ps_t])
                b.op("act", lambda e: e.copy(out=nkst[:, half2 * 4:half2 * 4 + 4, :],
                                              in_=ps[:, :].rearrange("p (j c) -> p j c", j=4)),
                     reads=[ps_t], writes=[nkst_t] if half2 == 0 else [], pw=[] if half2 == 0 else [nkst_t])
            b.dma("sp", O["nk"][:, hp * 128:(hp + 1) * 128].rearrange("(t p) c -> p t c", p=128), nkst[:],
                  reads=[nkst_t], is_output=True)
            cols = [t * 128 for t in range(12)] + [2048, 2176]
            for q4 in range(4):
                tl = cols[q4 * 4:q4 * 4 + 4]
                ps, ps_t = b.psum()
                for j, col in enumerate(tl):
                    for kc in range(8):
                        b.op("pe", lambda e: e.matmul(ps[:, j * 128:(j + 1) * 128], lhsT=hT[:, kc, col:col + 128],
                                                       rhs=wv[:, kc, 0:128], start=(kc == 0), stop=(kc == 7)),
                             reads=[hT_t, wv_t], writes=[ps_t])
                nj = len(tl)
                pv = ps[:, 0:nj * 128].rearrange("p (j c) -> p j c", j=nj)
                first = (q4 == 0)
                if q4 < 2:
                    st, st_t = nvst[:, q4 * 4:q4 * 4 + 4, :], nvst_t
                    b.op("act", lambda e: e.copy(out=st, in_=pv), reads=[ps_t],
                         writes=[nvst_t] if first else [], pw=[] if first else [nvst_t])
                else:
                    vs, st_t = vsts[q4 % 2]
                    st = vs[:, 0:nj, :]
                    b.op("act", lambda e: e.copy(out=st, in_=pv), reads=[ps_t], writes=[st_t])
                b.op("pool", lambda e: e.tensor_copy(out=vext[:, q4 * 4:q4 * 4 + nj, 0, 0:64], in_=st[:, :, 0:64]),
                     reads=[st_t], writes=[vext_t] if first else [], pw=[] if first else [vext_t])
                b.op("pool", lambda e: e.tensor_copy(out=vext[:, q4 * 4:q4 * 4 + nj, 1, 64:128], in_=st[:, :, 64:128]),
                     reads=[st_t], pw=[vext_t])
                yield
            b.dma("sp", O["nv"][:, hp * 128:(hp + 1) * 128].rearrange("(t p) c -> p t c", p=128), nvst[:],
                  reads=[nvst_t], is_output=True)
            cvv = I["cv"][:, hp * 128:(hp + 1) * 128].rearrange("(c p) n -> p c n", p=128)
            b.dma("pool", vext[:, 14:16, 0, 0:64], cvv[:, :, 0:64], pw=[vext_t])
            b.dma("pool", vext[:, 14:16, 1, 64:128], cvv[:, :, 64:128], pw=[vext_t])
            for tb in range(3):
                ps, ps_t = proj(wg, wg_t, tb * 512)
                b.op("act", lambda e: e.activation(out=sga[:, tb * 512:(tb + 1) * 512], in_=ps[:, :], func=AF.Silu),
                     reads=[ps_t], writes=[sga_t] if tb == 0 else [], pw=[] if tb == 0 else [sga_t])
                yield

        def attn_gen(hp):
            qT, qT_t, kT, kT_t, sga, sga_t, vext, vext_t = bufs(hp)
            if AT_STOP == 1:
                return
            def head_gen(e2, hp=hp, vext=vext, vext_t=vext_t):
                h = 2 * hp + e2
                pb = 64 * e2
                bt, bt_t = bts[h % 2]
                b.dma("sp", bt[:], I["btab"][h].rearrange("p (i c) -> p i c", i=24), writes=[bt_t])
                b.op("pool", lambda e: e.tensor_tensor(out=bt[:], in0=bt[:],
                                                        in1=cmask[:].unsqueeze(1).to_broadcast([128, 24, 64]), op=ALU.add),
                     reads=[bt_t, cmask_t], writes=[bt_t])
                def s_stage(bb):
                    ptl = []
                    for kc2 in range(2):
                        ch = 2 * bb + kc2
                        ps, ps_t = b.psum()
                        b.op("pe", lambda e: e.matmul(ps[:, 0:256], lhsT=kT[pb:pb + 64, ch * 128:(ch + 1) * 128],
                                                       rhs=qT[pb:pb + 64, bb * 256:(bb + 1) * 256], start=True, stop=True),
                             reads=[kT_t, qT_t], writes=[ps_t])
                        p, p_t = pt(e2)
                        b.op("act", lambda e: e.activation(out=p[:, 0:256], in_=ps[:, 0:256], func=AF.Exp),
                             reads=[ps_t], writes=[p_t])
                        ptl.append((p, p_t, ch))
                    return ptl

                def pv_stage(bb, ptl):
                    pso, pso_t = b.psum()
                    for i, (p, p_t, ch) in enumerate(ptl):
                        b.op("pe", lambda e: e.matmul(pso[:, 0:256], lhsT=vext[:, ch, e2, :], rhs=p[:, 0:256],
                                                       start=(i == 0), stop=(i == 1)),
                             reads=[vext_t, p_t], writes=[pso_t])
                    finalize(pso, pso_t, e2, hp, bb * 256, 256, sga, sga_t)

                prev = s_stage(0)
                yield
                for bb in range(1, 4):
                    cur = s_stage(bb)
                    pv_stage(bb - 1, prev)
                    prev = cur
                    yield
                pend = (3, prev)
                if AT_STOP == 2:
                    pv_stage(*pend)
                    return
                ptl = []
                for j in range(6):
                    kcol = 1024 + j * 128 if j < 4 else 1536 + (j - 4) * 128
                    i0 = (6 - 2 * j) if j < 4 else (14 + 2 - 2 * (j - 4))
                    ps, ps_t = b.psum()
                    b.op("pe", lambda e: e.matmul(ps[:, :], lhsT=kT[pb:pb + 64, kcol:kcol + 128],
                                                   rhs=qT[pb:pb + 64, 1024:1536], start=True, stop=True),
                         reads=[kT_t, qT_t], writes=[ps_t])
                    T, T_t = Ts[j % 2]
                    b.op("dve", lambda e: e.tensor_tensor(out=T[:].rearrange("p (r c) -> p r c", r=8),
                                                           in0=ps[:, :].rearrange("p (r c) -> p r c", r=8),
                                                           in1=bt[:, i0:i0 + 8, :], op=ALU.add),
                         reads=[ps_t, bt_t], writes=[T_t])
                    p, p_t = pt(e2)
                    b.op("act", lambda e: e.activation(out=p[:], in_=T[:], func=AF.Exp), reads=[T_t], writes=[p_t])
                    b.op("pool", lambda e: e.tensor_tensor(out=p[:].rearrange("p (r c) -> p r c", r=8),
                                                            in0=p[:].rearrange("p (r c) -> p r c", r=8),
                                                            in1=rmask[:, j, :].unsqueeze(2).to_broadcast([128, 8, 64]), op=ALU.mult),
                         reads=[p_t, rmask_t], writes=[p_t])
                    ptl.append((p, p_t, 8 + j))
                    if pend is not None:
                        pv_stage(*pend)
                        pend = None
                    yield
                for c in range(2):
                    ps, ps_t = b.psum()
                    b.op("pe", lambda e: e.matmul(ps[:, :], lhsT=kT[pb:pb + 64, 1792 + c * 128:1792 + (c + 1) * 128],
                                                   rhs=qT[pb:pb + 64, 1024:1536], start=True, stop=True),
                         reads=[kT_t, qT_t], writes=[ps_t])
                    p, p_t = pt(e2)
                    b.op("act", lambda e: e.activation(out=p[:], in_=ps[:, :], func=AF.Exp), reads=[ps_t], writes=[p_t])
                    ptl.append((p, p_t, 14 + c))
                    yield
                pso, pso_t = b.psum()
                for i, (p, p_t, ch) in enumerate(ptl):
                    b.op("pe", lambda e: e.matmul(pso[:, :], lhsT=vext[:, ch, e2, :], rhs=p[:], start=(i == 0), stop=(i == 7)),
                         reads=[vext_t, p_t], writes=[pso_t])
                finalize(pso, pso_t, e2, hp, 1024, 512, sga, sga_t)
            yield from interleave_gen(head_gen(0), head_gen(1))

        W = {0: shared.get("W0") or load_pair(0)}
        if AT_NHP > 1:
            W[1] = load_pair(1)
        interleave(proj_gen(0, W[0]))
        for hp in range(AT_NHP):
            if hp + 2 < AT_NHP:
                W[hp + 2] = load_pair(hp + 2)
            if hp == AT_NHP - 1 and AT_NHP == 8:
                shared["M0"] = [b.load_w(I["w_bh"], 0, 128), b.load_w(I["w_in"], 8 * D, 128),
                                b.load_w(I["w_ba"], 0, 128), b.load_w(I["w_in"], 9 * D, 128)]
                shared["M1"] = [b.load_w(I["w_bh"], 128, 128), b.load_w(I["w_in"], 8 * D + 128, 128),
                                b.load_w(I["w_ba"], 128, 128), b.load_w(I["w_in"], 9 * D + 128, 128)]
            gens = [attn_gen(hp)]
            if hp + 1 < AT_NHP:
                gens.append(proj_gen(hp + 1, W[hp + 1]))
            interleave(*gens)
        b.barrier()
        b.es = old_es


def phase_merge(b, I, hT, hT_t, yg, yg_t, ya, ya_t, mg, mg_t, shared):
    with ExitStack() as es:
        old_es = b.es
        b.es = es
        sgs = [b.sb("msg%d" % i, [128, 512], F32) for i in range(4)]
        t1s = [b.sb("mt%d" % i, [128, 512], F32) for i in range(4)]
        k = 0
        def load_oc(oc):
            return [b.load_w(I["w_bh"], oc * 128, 128), b.load_w(I["w_in"], 8 * D + oc * 128, 128),
                    b.load_w(I["w_ba"], oc * 128, 128), b.load_w(I["w_in"], 9 * D + oc * 128, 128)]
        nxt = shared.get("M0") or load_oc(0)
        for oc in range(8):
            (wbh, wbh_t), (wmh, wmh_t), (wba, wba_t), (wma, wma_t) = nxt
            if oc + 1 < 8:
                nxt = shared["M1"] if (oc == 0 and "M1" in shared) else load_oc(oc + 1)
            for tb in range(3):
                prods = []
                for (wp, wp_t, src, src_t, wm, wm_t) in ((wbh, wbh_t, yg, yg_t, wmh, wmh_t), (wba, wba_t, ya, ya_t, wma, wma_t)):
                    psm, psm_t = b.psum()
                    for kc in range(8):
                        b.op("pe", lambda e: e.matmul(psm[:, :], lhsT=wm[:, kc, 0:128], rhs=hT[:, kc, tb * 512:(tb + 1) * 512],
                                                       start=(kc == 0), stop=(kc == 7)),
                             reads=[wm_t, hT_t], writes=[psm_t])
                    sg, sg_t = sgs[k % 4]
                    b.op("act", lambda e: e.activation(out=sg[:], in_=psm[:, :], func=AF.Sigmoid), reads=[psm_t], writes=[sg_t])
                    psp, psp_t = b.psum()
                    for kc in range(8):
                        b.op("pe", lambda e: e.matmul(psp[:, :], lhsT=wp[:, kc, 0:128], rhs=src[:, kc, tb * 512:(tb + 1) * 512],
                                                       start=(kc == 0), stop=(kc == 7)),
                             reads=[wp_t, src_t], writes=[psp_t])
                    t1, t1_t = t1s[k % 4]
                    k += 1
                    b.op("dve", lambda e: e.tensor_tensor(out=t1[:], in0=psp[:, :], in1=sg[:], op=ALU.mult),
                         reads=[psp_t, sg_t], writes=[t1_t])
                    prods.append((t1, t1_t))
                b.op("pool", lambda e: e.tensor_tensor(out=mg[:, oc, tb * 512:(tb + 1) * 512], in0=prods[0][0][:],
                                                        in1=prods[1][0][:], op=ALU.add),
                     reads=[prods[0][1], prods[1][1]], pw=[mg_t])
        b.barrier()
        b.es = old_es


def phase_final(b, I, O, par, par_t, mg, mg_t, pre):
    nc = b.nc
    with ExitStack() as es:
        old_es = b.es
        b.es = es
        rows, rows_t = pre["rows"]
        grow, grow_t = b.sb("grow", [128, 2, D], F32)
        with ExitStack() as es2:
            b.es = es2
            brow, brow_t = pre["brow"]
            sc, sc_t = b.sb("sc", [128, 16], F32)
            screp, screp_t = b.sb("screp", [128, 2, 8, 128], BF16)
            b.op("act", lambda e: e.activation(out=sc[:], in_=par[:, P_COND:P_COND + 16], func=AF.Silu),
                 reads=[par_t], writes=[sc_t])
            for w in range(2):
                b.op("dve", lambda e: e.tensor_copy(out=screp[:, w, :, :],
                                                     in_=sc[:, w:16:2].unsqueeze(2).to_broadcast([128, 8, 128])),
                     reads=[sc_t], pw=[screp_t])
            for cb in range(2):
                wa, wa_t = pre["wa"][cb]
                for w in range(2):
                    ps, ps_t = b.psum()
                    for kc in range(8):
                        b.op("pe", lambda e: e.matmul(ps[:, :], lhsT=screp[:, w, kc, :], rhs=wa[:, kc, :],
                                                       start=(kc == 0), stop=(kc == 7)),
                             reads=[screp_t, wa_t], writes=[ps_t])
                    b.op("dve", lambda e: e.tensor_tensor(out=grow[:, w, cb * 512:(cb + 1) * 512], in0=ps[:, :],
                                                           in1=brow[:, cb * 512:(cb + 1) * 512], op=ALU.add),
                         reads=[ps_t, brow_t], pw=[grow_t])
            b.barrier()
        b.es = es
        wo = pre["wo"]
        xts = [b.sb("fx%d" % i, [128, D], F32) for i in range(3)]
        rts = [b.sb("fr%d" % i, [128, D], F32) for i in range(4)]
        FMAX = int(nc.vector.BN_STATS_FMAX)
        SD = int(nc.vector.BN_STATS_DIM)
        AD = int(nc.vector.BN_AGGR_DIM)
        nst = (D + FMAX - 1) // FMAX
        statss = [b.sb("stats%d" % i, [128, nst, SD], F32) for i in range(4)]
        mvs = [b.sb("mv%d" % i, [128, AD], F32) for i in range(4)]
        rstds = [b.sb("rstd%d" % i, [128, 1], F32) for i in range(4)]
        for tt in range(12):
            w = 0 if tt < 8 else 1
            xt, xt_t = xts[tt % 3]
            rt, rt_t = rts[tt % 4]
            stats, stats_t = statss[tt % 4]
            mv, mv_t = mvs[tt % 4]
            rstd, rstd_t = rstds[tt % 4]
            if tt == 0:
                for t0 in range(2):
                    b.dma("sp", xts[t0][0][:], I["xall"][t0 * 128:(t0 + 1) * 128, :], writes=[xts[t0][1]])
            if tt + 2 < 12:
                nx, nx_t = xts[(tt + 2) % 3]
                b.dma("sp", nx[:], I["xall"][(tt + 2) * 128:(tt + 3) * 128, :], writes=[nx_t])
            for cb in range(2):
                ps, ps_t = b.psum()
                for kc in range(8):
                    b.op("pe", lambda e: e.matmul(ps[:, :], lhsT=mg[:, kc, tt * 128:(tt + 1) * 128], rhs=wo[cb][0][:, kc, :],
                                                   start=(kc == 0), stop=(kc == 7)),
                         reads=[mg_t, wo[cb][1]], writes=[ps_t])
                b.op("dve", lambda e: e.tensor_tensor(out=rt[:, cb * 512:(cb + 1) * 512], in0=ps[:, :],
                                                       in1=grow[:, w, cb * 512:(cb + 1) * 512], op=ALU.mult),
                     reads=[ps_t, grow_t], writes=[rt_t] if cb == 0 else [], pw=[] if cb == 0 else [rt_t])
            b.op("dve", lambda e: e.scalar_tensor_tensor(out=rt[:], in0=xt[:], scalar=ALPHA, in1=rt[:],
                                                          op0=ALU.mult, op1=ALU.add),
                 reads=[xt_t, rt_t], writes=[rt_t])
            for c in range(nst):
                lo = c * FMAX
                hi = min(D, lo + FMAX)
                b.op("dve", lambda e: e.bn_stats(out=stats[:, c, :], in_=rt[:, lo:hi]),
                     reads=[rt_t], writes=[stats_t] if c == 0 else [], pw=[] if c == 0 else [stats_t])
            b.op("dve", lambda e: e.bn_aggr(out=mv[:], in_=stats[:]), reads=[stats_t], writes=[mv_t])
            b.op("dve", lambda e: e.tensor_scalar(out=rstd[:], in0=mv[:, 1:2], scalar1=LN_EPS, scalar2=None, op0=ALU.add),
                 reads=[mv_t], writes=[rstd_t])
            b.op("act", lambda e: e.sqrt(out=rstd[:], in_=rstd[:]), reads=[rstd_t], writes=[rstd_t])
            b.op("dve", lambda e: e.reciprocal(out=rstd[:], in_=rstd[:]), reads=[rstd_t], writes=[rstd_t])
            b.op("dve", lambda e: e.tensor_scalar(out=rt[:], in0=rt[:], scalar1=mv[:, 0:1], scalar2=rstd[:, 0:1],
                                                   op0=ALU.subtract, op1=ALU.mult),
                 reads=[rt_t, mv_t, rstd_t], writes=[rt_t])
            b.op("pool", lambda e: e.tensor_tensor(out=rt[:], in0=rt[:], in1=rows[:, 0, :], op=ALU.mult),
                 reads=[rt_t, rows_t], writes=[rt_t])
            b.op("pool", lambda e: e.tensor_tensor(out=rt[:], in0=rt[:], in1=rows[:, 1, :], op=ALU.add),
                 reads=[rt_t, rows_t], writes=[rt_t])
            b.dma("pool", O["y"][tt * 128:(tt + 1) * 128, :], rt[:], reads=[rt_t], is_output=True)
        b.barrier()
        b.es = old_es


def _attn_tabs(half, rpb):
    kc = np.arange(64)[:, None]
    col = np.arange(64)[None, :]
    dcidx = np.clip(kc - col + 15, 0, 30)
    bt = np.zeros((16, 128, 24, 64), np.float32)
    for e in range(2):
        for ip in range(14):
            dr = (13 - ip) - 7 + e
            if -7 <= dr <= 7:
                bt[:, e * 64:(e + 1) * 64, ip, :] = rpb[:, dr + 7][:, dcidx]
        D0 = 8 if half == 0 else -4
        for ip in range(10):
            dr = D0 + (9 - ip) - 7 + e
            if -7 <= dr <= 7:
                bt[:, e * 64:(e + 1) * 64, 14 + ip, :] = rpb[:, dr + 7][:, dcidx]
    c0 = np.clip(col - 8, 0, 48)
    inwin = (kc >= c0) & (kc < c0 + 16)
    cm = np.where(inwin, 0.0, -30000.0).astype(np.float32)
    cmask = np.concatenate([cm, cm], 0)
    rmask = np.zeros((128, 6, 8), np.float32)
    for j in range(6):
        for e in range(2):
            if half == 0:
                kr = 2 * j + e if j < 4 else 8 + 2 * (j - 4) + e
            else:
                kr = 8 + 2 * j + e if j < 4 else 4 + 2 * (j - 4) + e
            for rl in range(8):
                r = rl if half == 0 else 8 + rl
                r0 = min(max(r - 4, 0), 8)
                rmask[e * 64:(e + 1) * 64, j, rl] = 1.0 if (r0 <= kr < r0 + 8) else 0.0
    return bt.reshape(16, 128, 24 * 64), cmask, rmask.reshape(128, 48)

def _core_inputs(j, inp):
    bs, half = j // 2, j % 2
    c = _consts(half)
    xs = inp["x_sample"][bs]
    own = xs[half * 512:(half + 1) * 512]
    oth = xs[(1 - half) * 512:(2 - half) * 512]
    halo = xs[512:768] if half == 0 else xs[256:512]
    xall = np.concatenate([inp["x_prompt"][4 * j:4 * j + 4].reshape(1024, D), own, oth, halo], 0)
    par = np.zeros((128, P_NP), np.float32)
    cond = np.stack([inp["c_ctx"], inp["c"][bs]], 0)
    par[:, P_COND:P_COND + 16] = cond.reshape(2, 8, 128).transpose(2, 1, 0).reshape(128, 16)
    par[:, P_BADA:P_BADA + 24] = inp["b_ada"][0].reshape(24, 128).T
    par[:, P_CW:P_CW + 72] = inp["conv_w"][0].reshape(3, 24, 128).transpose(2, 0, 1).reshape(128, 72)
    par[:, P_CB:P_CB + 24] = inp["conv_b"][0].reshape(24, 128).T
    par[:, P_HD:P_HD + 8] = inp["hyena_d"][0].reshape(8, 128).T
    par[:, P_FLAG] = 1.0 if half == 1 else 0.0
    par[:, P_FLAG + 1] = 1.0 if half == 0 else 0.0
    par[0:64, P_FPAR + 0] = inp["filt_b1"][0]
    par[0:64, P_FPAR + 1] = inp["filt_b2"][0]
    par[0:64, P_FPAR + 2] = inp["filt_freq"][0, 0]
    par[0:64, P_FPAR + 3] = inp["filt_freq"][0, 1]
    w3 = inp["filt_w3"][0]
    w3c = np.concatenate([w3, w3[:, :D] if half == 1 else w3[:, D:]], 1)
    m = {
        "xall": xall, "params": par,
        "ck": inp["cache_k"][bs, 0].reshape(256, D), "cv": inp["cache_v"][bs, 0].reshape(256, D),
        "w_ada": inp["w_ada"][0], "w_in": inp["w_in"][0], "w_bh": inp["w_bh"][0], "w_ba": inp["w_ba"][0],
        "w_out": inp["w_out"][0], "fw1": inp["filt_w1"][0], "fw2": inp["filt_w2"][0], "w3c": w3c,
        "rows": np.stack([inp["b_ada"][0, 2 * D:], inp["ln_g"][0], inp["ln_b"][0]], 0),
    }
    m["btab"], m["cmask"], m["rmask"] = _attn_tabs(half, inp["rpb"][0])
    m.update(c)
    out = {}
    for k, v in m.items():
        if k in ("fwd256", "bwd256", "inv256", "fwd512", "bwd512", "inv512", "x1024"):
            out[k] = np.ascontiguousarray(v)
        else:
            out[k] = np.ascontiguousarray(v, dtype=np.float32)
    return out


_PROG = {}


def kernel(**inputs):
    inp = {k: np.asarray(v) for k, v in inputs.items()}
    if "nc" not in _PROG:
        _PROG["nc"], _PROG["b"] = build_program()
    nc = _PROG["nc"]
    in_maps = [_core_inputs(j, inp) for j in range(NCORES)]
    res = run_bass_kernel_spmd(nc, in_maps, core_ids=list(range(NCORES)))
    _PROG["last"] = res
    y_p = np.zeros((32, 256, D), np.float32)
    y_s = np.zeros((4, 1024, D), np.float32)
    nk = np.zeros((32, 1, 256, 16, 64), np.float32)
    nv = np.zeros((32, 1, 256, 16, 64), np.float32)
    for j in range(NCORES):
        r = res.results[j]
        bs, half = j // 2, j % 2
        y_p[4 * j:4 * j + 4] = r["y"][:1024].reshape(4, 256, D)
        y_s[bs, half * 512:(half + 1) * 512] = r["y"][1024:]
        nk[4 * j:4 * j + 4, 0] = r["nk"].reshape(4, 256, 16, 64)
        nv[4 * j:4 * j + 4, 0] = r["nv"].reshape(4, 256, 16, 64)
    return y_p, y_s, nk, nv
```
